# Optimizing a Trainium2 kernel written in Bass

```python
import jax, jax.numpy as jnp
from jax import lax
import numpy as np

D_MODEL = 1024
BATCH = 8
SEQ = 8192
DEPTH = 1

HEAD_DIM = 64
N_Q_HEADS = 16
N_KV_HEADS = 4
GROUP = N_Q_HEADS // N_KV_HEADS
ATTN_WIDTH = N_Q_HEADS * HEAD_DIM
KV_WIDTH = N_KV_HEADS * HEAD_DIM
WINDOW = 128
BLOCK = 128
CONV_CH = D_MODEL
CONV_WIDTH = 31
D_FF = 4 * D_MODEL
N_BUCKETS = 32
MAX_DISTANCE = 128
EPS = 1e-6
NEG = -1e30
Q_END = ATTN_WIDTH
K_END = Q_END + KV_WIDTH
V_END = K_END + KV_WIDTH
GLU_END = V_END + 2 * CONV_CH
IN_WIDTH = GLU_END + 2 * D_MODEL

kernel_name = "hybrid_swa_sink_conformer_gated_block"


def rms_norm(x, g):
    xf = x.astype(jnp.float32)
    y = xf * lax.rsqrt(jnp.mean(xf * xf, axis=-1, keepdims=True) + EPS)
    return (y * g.astype(jnp.float32)).astype(x.dtype)


def layer_norm(x, g, b):
    xf = x.astype(jnp.float32)
    mu = jnp.mean(xf, axis=-1, keepdims=True)
    xc = xf - mu
    var = jnp.mean(xc * xc, axis=-1, keepdims=True)
    y = xc * lax.rsqrt(var + EPS) * g.astype(jnp.float32) + b.astype(jnp.float32)
    return y.astype(x.dtype)


def t5_causal_bucket(dist):
    n = jnp.maximum(dist, 0)
    max_exact = N_BUCKETS // 2
    nf = jnp.maximum(n, 1).astype(jnp.float32)
    large = max_exact + (jnp.log(nf / max_exact) / np.float32(np.log(MAX_DISTANCE / max_exact))
                         * (N_BUCKETS - max_exact)).astype(jnp.int32)
    large = jnp.minimum(large, N_BUCKETS - 1)
    return jnp.where(n < max_exact, n, large)


def band_blocks(t, nb):
    b = t.shape[0]
    tp = jnp.pad(t, ((0, 0), (BLOCK, 0), (0, 0), (0, 0))).reshape(b, nb + 1, BLOCK, t.shape[2], t.shape[3])
    return jnp.concatenate([tp[:, :-1], tp[:, 1:]], axis=2)


def sliding_window_attention(q, k, v, sinks, rel_bias):
    b, s = q.shape[0], q.shape[1]
    nb = s // BLOCK
    qb = q.reshape(b, nb, BLOCK, N_KV_HEADS, GROUP, HEAD_DIM)
    kb = band_blocks(k, nb)
    vb = band_blocks(v, nb)
    scores = jnp.einsum('bnqhgd,bnkhd->bnhgqk', qb, kb,
                        preferred_element_type=jnp.float32)
    qi = jnp.arange(BLOCK, dtype=jnp.int32)[:, None]
    kj = jnp.arange(2 * BLOCK, dtype=jnp.int32)[None, :]
    dist = qi + BLOCK - kj
    bias = rel_bias[t5_causal_bucket(dist)].astype(jnp.float32)
    bias = jnp.transpose(bias, (2, 0, 1)).reshape(N_KV_HEADS, GROUP, BLOCK, 2 * BLOCK)
    scores = scores + bias
    key_pos = jnp.arange(nb, dtype=jnp.int32)[:, None] * BLOCK - BLOCK + kj
    valid = ((dist >= 0) & (dist < WINDOW))[None] & (key_pos >= 0)[:, None, :]
    scores = jnp.where(valid[None, :, None, None], scores, NEG)
    sink = sinks.astype(jnp.float32).reshape(N_KV_HEADS, GROUP)[None, None, :, :, None, None]
    sink = jnp.broadcast_to(sink, scores.shape[:-1] + (1,))
    probs = jax.nn.softmax(jnp.concatenate([scores, sink], axis=-1), axis=-1)[..., :-1]
    o = jnp.einsum('bnhgqk,bnkhd->bnqhgd', probs.astype(v.dtype), vb)
    return o.reshape(b, s, ATTN_WIDTH)


def conformer_conv(glu_in, w_dw, b_dw, ln_g, ln_b, w_conv_out):
    a, gate = jnp.split(glu_in, 2, axis=-1)
    h = a * jax.nn.sigmoid(gate)
    h = lax.conv_general_dilated(
        h, w_dw[:, None, :].astype(h.dtype), window_strides=(1,),
        padding=[(CONV_WIDTH - 1, 0)],
        dimension_numbers=('NWC', 'WIO', 'NWC'),
        feature_group_count=CONV_CH) + b_dw
    h = jax.nn.silu(layer_norm(h, ln_g, ln_b))
    return h @ w_conv_out


def setup_inputs(seed: int = 0) -> dict:
    key = jax.random.key(seed)
    ks = jax.random.split(key, 20)
    f32 = jnp.float32

    def nrm(k, shape, scale):
        return jax.random.normal(k, shape, f32) * scale

    L = DEPTH
    return {
        "x": nrm(ks[0], (BATCH, SEQ, D_MODEL), 1.0),
        "norm_mix_g": 1.0 + nrm(ks[1], (L, D_MODEL), 0.02),
        "w_in": nrm(ks[2], (L, D_MODEL, IN_WIDTH), D_MODEL ** -0.5),
        "q_norm_g": 1.0 + nrm(ks[3], (L, HEAD_DIM), 0.02),
        "k_norm_g": 1.0 + nrm(ks[4], (L, HEAD_DIM), 0.02),
        "attn_sinks": nrm(ks[5], (L, N_Q_HEADS), 0.5),
        "rel_bias": nrm(ks[6], (N_BUCKETS, N_Q_HEADS), 0.5),
        "w_attn_o": nrm(ks[7], (L, ATTN_WIDTH, D_MODEL), ATTN_WIDTH ** -0.5),
        "w_dw": nrm(ks[8], (L, CONV_WIDTH, CONV_CH), CONV_WIDTH ** -0.5),
        "b_dw": nrm(ks[9], (L, CONV_CH), 0.02),
        "conv_ln_g": 1.0 + nrm(ks[10], (L, CONV_CH), 0.02),
        "conv_ln_b": nrm(ks[11], (L, CONV_CH), 0.02),
        "w_conv_out": nrm(ks[12], (L, CONV_CH, D_MODEL), CONV_CH ** -0.5),
        "w_out": nrm(ks[13], (L, D_MODEL, D_MODEL), D_MODEL ** -0.5),
        "norm_mlp_g": 1.0 + nrm(ks[14], (L, D_MODEL), 0.02),
        "w_ff1": nrm(ks[15], (L, D_MODEL, D_FF), D_MODEL ** -0.5),
        "w_ff2": nrm(ks[16], (L, D_FF, D_MODEL), D_FF ** -0.5),
    }


def reference(x, norm_mix_g, w_in, q_norm_g, k_norm_g, attn_sinks, rel_bias, w_attn_o,
              w_dw, b_dw, conv_ln_g, conv_ln_b, w_conv_out, w_out, norm_mlp_g, w_ff1, w_ff2):
    b, s, _ = x.shape
    for l in range(DEPTH):
        u = rms_norm(x, norm_mix_g[l])
        proj = u @ w_in[l]
        q = proj[..., :Q_END].reshape(b, s, N_Q_HEADS, HEAD_DIM)
        k = proj[..., Q_END:K_END].reshape(b, s, N_KV_HEADS, HEAD_DIM)
        v = proj[..., K_END:V_END].reshape(b, s, N_KV_HEADS, HEAD_DIM)
        glu_in = proj[..., V_END:GLU_END]
        gate_attn, gate_conv = jnp.split(proj[..., GLU_END:], 2, axis=-1)

        q = rms_norm(q, q_norm_g[l]) * (HEAD_DIM ** -0.5)
        k = rms_norm(k, k_norm_g[l])
        attn = sliding_window_attention(q, k, v, attn_sinks[l], rel_bias) @ w_attn_o[l]
        conv = conformer_conv(glu_in, w_dw[l], b_dw[l], conv_ln_g[l], conv_ln_b[l], w_conv_out[l])

        merged = jax.nn.sigmoid(gate_attn) * attn + jax.nn.sigmoid(gate_conv) * conv
        x = x + merged @ w_out[l]

        hmid = jnp.square(jax.nn.relu(rms_norm(x, norm_mlp_g[l]) @ w_ff1[l]))
        x = x + hmid @ w_ff2[l]
    return x
```

```python
import numpy as np
from contextlib import ExitStack
import concourse.bass as bass
import concourse.mybir as mybir
from concourse.bass_utils import run_bass_kernel_spmd

F32 = mybir.dt.float32
BF16 = mybir.dt.bfloat16
AF = mybir.ActivationFunctionType
ALU = mybir.AluOpType
AX = mybir.AxisListType

D = 1024
SEQ = 8192
NB = 4
T = NB * 128
HD = 64
NQ = 16
NKV = 4
CW = 31
DFF = 4096
EPS = 1e-6
V_END = 1536
GLU_END = V_END + 2048
IN_W = GLU_END + 2048
NCOLP = 5 + CW
NSLOT = 4
ENGS = ("pe", "act", "dve", "pool", "sp")


class Buf:
    __slots__ = ("name", "w", "r")

    def __init__(self, name):
        self.name = name
        self.w = None
        self.r = []


class Sched:
    def __init__(self):
        self.ops = {e: [] for e in ENGS}
        self.cnt = {}
        self.seen = {e: {} for e in ENGS}
        self.keys = []

    def _bump(self, key, n):
        if key not in self.cnt:
            self.cnt[key] = 0
            self.keys.append(key)
        self.cnt[key] += n
        return (key, self.cnt[key])

    def add(self, eng, fn, reads=(), writes=(), dkey=None, ndma=1, extra=()):
        deps = {}

        def need(t):
            if t is None:
                return
            k, v = t
            if eng == "pe" and k == "pe":
                return
            if deps.get(k, 0) < v:
                deps[k] = v
        for b in reads:
            need(b.w)
        for b in writes:
            need(b.w)
            for t in b.r:
                need(t)
        for t in extra:
            need(t)
        waits = []
        seen = self.seen[eng]
        for k, v in deps.items():
            if seen.get(k, 0) >= v:
                continue
            seen[k] = v
            waits.append((k, v))
        if fn is None:
            tok = None
        elif dkey is not None:
            tok = self._bump(dkey, 16 * ndma)
        else:
            tok = self._bump(eng, 1)
        self.ops[eng].append((waits, fn, tok, dkey is not None))
        if tok is not None:
            for b in reads:
                b.r.append(tok)
            for b in writes:
                b.w = tok
                b.r = []
        return tok

    def emit(self, nc, stack):
        sems = {}
        for i, k in enumerate(self.keys):
            sems[k] = stack.enter_context(nc.semaphore("s%d" % i))
        block = stack.enter_context(nc.Block())
        handles = {"pe": block.tensor, "act": block.scalar, "dve": block.vector,
                   "pool": block.gpsimd, "sp": block.sync}
        for eng in ENGS:
            ops = self.ops[eng]
            if not ops:
                continue

            def body(e, ops=ops):
                for waits, fn, tok, is_dma in ops:
                    for k, v in waits:
                        e.wait_ge(sems[k], v)
                    if fn is None:
                        continue
                    if is_dma:
                        fn(e, sems[tok[0]])
                    else:
                        fn(e).then_inc(sems[tok[0]], 1)
            handles[eng](body)


def build(ntiles):
    seq = ntiles * T
    nc = bass.Bass("TRN2", target_bir_lowering=False)
    x_d = nc.dram_tensor("x", [seq, D], F32, kind="ExternalInput").ap()
    w_in_d = nc.dram_tensor("w_in", [D, IN_W], F32, kind="ExternalInput").ap()
    w_ao_d = nc.dram_tensor("w_attn_o", [D, D], F32, kind="ExternalInput").ap()
    w_co_d = nc.dram_tensor("w_conv_out", [D, D], F32, kind="ExternalInput").ap()
    w_out_d = nc.dram_tensor("w_out", [D, D], F32, kind="ExternalInput").ap()
    w_ff1_d = nc.dram_tensor("w_ff1", [D, DFF], F32, kind="ExternalInput").ap()
    w_ff2_d = nc.dram_tensor("w_ff2", [DFF, D], F32, kind="ExternalInput").ap()
    rowp_d = nc.dram_tensor("rowp", [NCOLP, D], F32, kind="ExternalInput").ap()
    gqk_d = nc.dram_tensor("gqk", [128, 2], F32, kind="ExternalInput").ap()
    sink_d = nc.dram_tensor("sinks", [128, NQ], F32, kind="ExternalInput").ap()
    bias_d = nc.dram_tensor("biasT", [128, NQ * 2 * 128], F32, kind="ExternalInput").ap()
    ident_d = nc.dram_tensor("ident", [128, 128], F32, kind="ExternalInput").ap()
    out_d = nc.dram_tensor("out", [seq, D], F32, kind="ExternalOutput").ap()
    wscr = nc.dram_tensor("wscr", [33, 128, 8 * 512], BF16).ap()

    S = Sched()
    with ExitStack() as st:
        def sb(name, shape, dt):
            return st.enter_context(nc.sbuf_tensor(name, shape, dt))

        x_tb = [sb("x_t%d" % i, [128, NB, D], F32) for i in range(2)]
        uT = sb("uT", [128, 8, T], BF16)
        xnb = [sb("xn%d" % i, [128, D], BF16) for i in range(2)]
        sq = sb("sq", [128, 512], F32)
        junk = sq[:].bitcast(BF16)
        gbc = [sb("gbc%d" % i, [128, D], F32) for i in range(2)]
        qn = sb("qn", [128, D], BF16)
        kn = sb("kn", [128, 256], BF16)
        regA = sb("regA", [128, NQ * T], BF16)
        QTz = regA[:].rearrange("p (h t) -> p h t", h=NQ)
        sT = regA[:, 0:8 * T].rearrange("p (c t) -> p c t", c=8)
        mT = regA[:, 8 * T:16 * T].rearrange("p (c t) -> p c t", c=8)
        KT = sb("KT", [128, 2, (NB + 1) * 128], BF16)
        Vaug = sb("Vaug", [128, NB + 1, NKV * 65], BF16)
        Eb = [sb("E%d" % i, [128, 512], BF16) for i in range(3)]
        PTb = [sb("PT%d" % i, [128, 512], BF16) for i in range(3)]
        EB = sb("EB", [128, NQ * 256], BF16)
        attn_n = sb("attn_n", [128, D], BF16)
        attnT = sb("attnT", [128, 8, T], BF16)
        tnb = [sb("tn%d" % i, [128, 512], F32) for i in range(2)]
        regB = sb("regB", [128, 16 * 1024], BF16)
        hT = regB[:, 0:32 * T].rearrange("p (c t) -> p c t", c=32)
        regBf = regB[:].bitcast(F32)
        hbuf = regB[:, 0:8 * (T + 32)].rearrange("p (c t) -> p c t", c=8)
        cv = regBf[:, 4 * (T + 32):4 * (T + 32) + 8 * T].rearrange("p (c t) -> p c t", c=8)
        dg = [sb("dg%d" % i, [128, CW, 128], BF16) for i in range(2)]
        cvb = [sb("cvb%d" % i, [128, T], BF16) for i in range(2)]
        sqb = [sb("sqb%d" % i, [128, T], BF16) for i in range(2)]
        mean_sb = sb("mean_sb", [128, T], F32)
        rstd_sb = sb("rstd_sb", [128, T], F32)
        nmr_sb = sb("nmr_sb", [128, T], F32)
        ybuf = sb("ybuf", [128, T], F32)
        zbuf = sb("zbuf", [128, T], F32)
        rl = [sb("rl%d" % i, [128, T], BF16) for i in range(2)]
        wsl = [sb("wsl%d" % i, [128, 8, 512], BF16) for i in range(NSLOT)]
        ident_f = sb("ident_f", [128, 128], F32)
        ident_b = sb("ident_b", [128, 128], BF16)
        onesm = sb("onesm", [128, 128], BF16)
        rowp = regBf[0:NCOLP, 4096:4096 + D]
        colp = sb("colp", [128, 8, NCOLP], F32)
        colh = sb("colh", [128, 8, NCOLP], F32)
        gqk = sb("gqk_sb", [128, 2], F32)
        gqkp = sb("gqkp", [128, 1], F32)
        es = sb("es", [128, NQ], F32)
        bias_sb = regBf[:, 0:1024]
        ss1 = sb("ss1", [128, NB], F32)
        r1 = sb("r1", [128, NB], F32)
        ssq = sb("ssq", [128, 20], F32)
        rqk = sb("rqk", [128, 20], F32)
        den = sb("den", [128, NQ], F32)
        rden = sb("rden", [128, NQ], F32)
        halo_h = sb("halo_h", [128, 8, 32], BF16)
        ps = [st.enter_context(nc.psum_tensor("ps%d" % i, [128, 512], F32)) for i in range(8)]
        psb = [p[:].bitcast(BF16) for p in ps]

        B = {}

        def bf(name):
            if name not in B:
                B[name] = Buf(name)
            return B[name]
        pB = [bf("ps%d" % i) for i in range(8)]
        rot = [0]

        pool_now = [tuple(range(8))]

        reserved = set()

        def nextbank():
            p = pool_now[0]
            while True:
                i = p[rot[0] % len(p)]
                rot[0] += 1
                if i not in reserved:
                    return i
        tgl = {}

        def alt(name, n=2):
            tgl[name] = tgl.get(name, -1) + 1
            return tgl[name] % n

        S.add("sp", lambda e, s: e.dma_start(out=ident_f[:], in_=ident_d).then_inc(s, 16), writes=[bf("identf")], dkey="c0")
        S.add("sp", lambda e, s: e.dma_start(out=rowp, in_=rowp_d).then_inc(s, 16), writes=[bf("rowp"), bf("hT")], dkey="c1")
        S.add("sp", lambda e, s: e.dma_start(out=gqk[:], in_=gqk_d).then_inc(s, 16), writes=[bf("gqk")], dkey="c2")
        S.add("sp", lambda e, s: e.dma_start(out=es[:], in_=sink_d).then_inc(s, 16), writes=[bf("es")], dkey="c3")
        for gi in range(2):
            S.add("sp", lambda e, s, gi=gi: e.dma_start(out=gbc[gi][:], in_=rowp_d[gi:gi + 1, :].partition_broadcast(128)).then_inc(s, 16),
                  writes=[bf("gbc")], dkey="c5%d" % gi)
        S.add("dve", lambda e: e.tensor_copy(out=ident_b[:], in_=ident_f[:]), reads=[bf("identf")], writes=[bf("identb")])
        S.add("dve", lambda e: e.memset(onesm[:], 1.0 / D), writes=[bf("onesm")])
        S.add("dve", lambda e: e.memset(regA[:], 0.0), writes=[bf("QTz"), bf("sT"), bf("mT")])
        S.add("dve", lambda e: e.memset(KT[:], 0.0), writes=[bf("KT%d" % i) for i in range(NB + 1)])
        S.add("dve", lambda e: e.memset(Vaug[:, 0, :], 0.0), writes=[bf("V0")])
        S.add("dve", lambda e: e.memset(Vaug[:, 1:NB + 1, :], 1.0), writes=[bf("V%d" % i) for i in range(1, NB + 1)])
        S.add("dve", lambda e: e.memset(halo_h[:], 0.0), writes=[bf("halo_h")])
        S.add("dve", lambda e: e.scalar_tensor_tensor(out=gqkp[:], in0=gqk[:, 0:1], scalar=HD ** -0.5, in1=gqk[:, 1:2], op0=ALU.mult, op1=ALU.mult), reads=[bf("gqk")], writes=[bf("gqkp")])
        S.add("act", lambda e: e.activation(out=es[:], in_=es[:], func=AF.Exp), reads=[bf("es")], writes=[bf("es")])
        for hg in range(4):
            S.add("sp", lambda e, s, hg=hg: e.dma_start(out=bias_sb[:], in_=bias_d[:, hg * 1024:(hg + 1) * 1024]).then_inc(s, 16),
                  writes=[bf("bias_sb"), bf("hT")], dkey="c4")
            S.add("act", lambda e, hg=hg: e.activation(out=EB[:, hg * 1024:(hg + 1) * 1024], in_=bias_sb[:], func=AF.Exp),
                  reads=[bf("bias_sb"), bf("hT")], writes=[bf("EB")])
        for kc in range(8):
            bk = nextbank()
            S.add("pe", lambda e, kc=kc, bk=bk: e.transpose(out=ps[bk][:, 0:NCOLP], in_=rowp[:, kc * 128:(kc + 1) * 128],
                                                        identity=ident_f[0:NCOLP, 0:NCOLP]),
                  reads=[bf("rowp"), bf("identf"), bf("hT")], writes=[pB[bk]])
            S.add("dve", lambda e, kc=kc, bk=bk: e.tensor_copy(out=colp[:, kc, :], in_=ps[bk][:, 0:NCOLP]),
                  reads=[pB[bk]], writes=[bf("colp")])
        S.add("dve", lambda e: e.tensor_scalar(out=colh[:], in0=colp[:], scalar1=0.5, scalar2=None, op0=ALU.mult),
              reads=[bf("colp")], writes=[bf("colh")])
        CP = [bf("colp"), bf("colh")]

        def unit_srcs():
            def colpanel(w, c0):
                return w.rearrange("(kc p) n -> p kc n", p=128)[:, :, c0:c0 + 512]
            u = []
            for c0 in (0, 512, 1024):
                u.append(colpanel(w_in_d, c0))
            for j in range(2):
                u.append(colpanel(w_in_d, V_END + j * 512))
                u.append(colpanel(w_in_d, V_END + 1024 + j * 512))
            for j in range(2):
                u.append(colpanel(w_ao_d, j * 512))
                u.append(colpanel(w_in_d, GLU_END + j * 512))
            for j in range(2):
                u.append(colpanel(w_co_d, j * 512))
                u.append(colpanel(w_in_d, GLU_END + 1024 + j * 512))
            for j in range(2):
                u.append(colpanel(w_out_d, j * 512))
            for f in range(8):
                u.append(colpanel(w_ff1_d, f * 512))
            for j2 in range(2):
                for g in range(4):
                    u.append(w_ff2_d[g * 1024:(g + 1) * 1024, j2 * 512:(j2 + 1) * 512].rearrange("(kc p) n -> p kc n", p=128))
            return u
        USRC = unit_srcs()
        NU = len(USRC)
        assert NU == 33
        total_units = ntiles * NU
        issued = [0]
        taken = [0]
        released = [0]

        def issue_fetch():
            n = issued[0]
            if n >= total_units:
                return
            issued[0] += 1
            tile, u = divmod(n, NU)
            s = n % NSLOT
            wb = bf("wsl%d" % s)
            if tile == 0:
                S.add("pool", lambda e, sem, u=u, s=s: e.dma_start(out=wsl[s][:], in_=USRC[u]).then_inc(sem, 16),
                      writes=[wb], dkey="wq%d" % s)
                if ntiles > 1:
                    S.add("sp", lambda e, sem, u=u, s=s: e.dma_start(out=wscr[u], in_=wsl[s][:].rearrange("p k n -> p (k n)")).then_inc(sem, 16),
                          reads=[wb], writes=[bf("scr%d" % u)], dkey="ws%d" % s)
            else:
                S.add("sp", lambda e, sem, u=u, s=s: e.dma_start(out=wsl[s][:].rearrange("p k n -> p (k n)"), in_=wscr[u]).then_inc(sem, 16),
                      reads=[bf("scr%d" % u)], writes=[wb], dkey="wl%d" % s)

        def topup():
            while issued[0] < min(total_units, released[0] + NSLOT):
                issue_fetch()

        def next_unit():
            topup()
            assert taken[0] < issued[0]
            s = taken[0] % NSLOT
            taken[0] += 1
            return wsl[s], bf("wsl%d" % s)

        def release(n):
            released[0] += n
            topup()

        def mm_group(bank, col0, ncol, lhs_fn, rhs_fn, nk, reads, extra_w=()):
            def fn(e):
                i = None
                for k in range(nk):
                    i = e.matmul(ps[bank][:, col0:col0 + ncol], lhsT=lhs_fn(k), rhs=rhs_fn(k), start=(k == 0), stop=(k == nk - 1))
                return i
            return S.add("pe", fn, reads=reads, writes=[pB[bank]] + list(extra_w))

        UT = [bf("uT%d" % j) for j in range(NB)]

        def rms_sq(j, x_t, xbufs):
            S.add("act", lambda e, j=j: e.activation(out=junk, in_=x_t[:, j, :], func=AF.Square, accum_out=ss1[:, j:j + 1]),
                  reads=[xbufs[j]], writes=[bf("sq"), bf("ss1_%d" % j)])

        def rms_head(c0=0, c1=NB):
            rb = [bf("r1_%d" % j) for j in range(c0, c1)]
            S.add("act", lambda e: e.activation(out=r1[:, c0:c1], in_=ss1[:, c0:c1], func=AF.Sqrt, bias=EPS, scale=1.0 / D),
                  reads=[bf("ss1_%d" % j) for j in range(c0, c1)], writes=rb)
            S.add("dve", lambda e: e.reciprocal(out=r1[:, c0:c1], in_=r1[:, c0:c1]), reads=rb, writes=rb)

        def rms_finish(gi, x_t, xbufs, blocks=range(NB), head=True):
            if head:
                rms_head()
            for j in blocks:
                a = alt("xn")
                S.add("dve", lambda e, j=j, a=a: e.scalar_tensor_tensor(out=xnb[a][:], in0=x_t[:, j, :], scalar=r1[:, j:j + 1], in1=gbc[gi][:], op0=ALU.mult, op1=ALU.mult),
                      reads=[xbufs[j], bf("r1_%d" % j), bf("gbc")], writes=[bf("xn%d" % a)])
                bk = nextbank()

                def tr(e, bk=bk, a=a):
                    i = None
                    for kc in range(8):
                        i = e.transpose(out=psb[bk][:, kc * 128:(kc + 1) * 128], in_=xnb[a][:, kc * 128:(kc + 1) * 128], identity=ident_b[:])
                    return i
                S.add("pe", tr, reads=[bf("xn%d" % a), bf("identb")], writes=[pB[bk]])
                S.add("act", lambda e, j=j, bk=bk: e.activation(out=uT[:, :, j * 128:(j + 1) * 128], in_=psb[bk][:].rearrange("p (c t) -> p c t", c=8), func=AF.Copy),
                      reads=[pB[bk]], writes=[UT[j]])

        out_tokens = []
        for tile in range(ntiles):
            t0 = tile * T
            x_t = x_tb[tile % 2]
            xb = [bf("x%d_%d" % (tile % 2, j)) for j in range(NB)]

            def load_x(tl):
                xt2 = x_tb[tl % 2]
                xb2 = [bf("x%d_%d" % (tl % 2, j)) for j in range(NB)]
                tt0 = tl * T
                S.add("sp", lambda e, s: e.dma_start(out=xt2[:], in_=x_d[tt0:tt0 + T, :].rearrange("(j p) d -> p j d", p=128)).then_inc(s, 16),
                      writes=xb2, dkey="xl%d" % (tl % 2))

            def rms1_of(tl, part="all", blocks=range(NB)):
                xt2 = x_tb[tl % 2]
                xb2 = [bf("x%d_%d" % (tl % 2, j)) for j in range(NB)]
                if part in ("all", "head"):
                    for j in range(NB):
                        rms_sq(j, xt2, xb2)
                if part == "all":
                    rms_finish(0, xt2, xb2)
                elif part == "head":
                    rms_head()
                else:
                    rms_finish(0, xt2, xb2, blocks=blocks, head=False)
            if tile == 0:
                load_x(0)
                rms1_of(0)

            def build_dg(c):
                a = c % 2
                S.add("pool", lambda e, a=a, c=c: e.tensor_tensor(out=dg[a][:], in0=ident_b[:].unsqueeze(1).to_broadcast([128, CW, 128]),
                                                                   in1=colh[:, c, 5:5 + CW].unsqueeze(2).to_broadcast([128, CW, 128]), op=ALU.mult),
                      reads=[bf("identb")] + CP, writes=[bf("dg%d" % a)])

            build_dg(0)
            build_dg(1)
            rot[0] = 4
            wq0, wq0b = next_unit()
            wq1, wq1b = next_unit()
            wkv, wkvb = next_unit()
            qkv_banks = {}

            def qkv_mm(j):
                banks = []
                for (w, wb) in ((wq0, wq0b), (wq1, wq1b), (wkv, wkvb)):
                    bk = nextbank()
                    banks.append(bk)
                    mm_group(bk, 0, 512, lambda k, j=j: uT[:, k, j * 128:(j + 1) * 128], lambda k, w=w: w[:, k, :], 8, [UT[j], wb])
                qkv_banks[j] = banks
                reserved.update(banks)

            def qkv_post(j, part="ab"):
                bq0, bq1, bkv = qkv_banks[j]
                if "a" in part:
                    qkv_post_a(j, bq0, bq1, bkv)
                if "b" in part:
                    qkv_post_b(j, bq0, bq1, bkv)

            def qkv_post_a(j, bq0, bq1, bkv):
                for bi, bk in enumerate((bq0, bq1)):
                    S.add("act", lambda e, bk=bk: e.activation(out=sq[:], in_=ps[bk][:], func=AF.Square), reads=[pB[bk]], writes=[bf("sq")])
                    S.add("dve", lambda e, bi=bi: e.tensor_reduce(out=ssq[:, bi * 8:(bi + 1) * 8], in_=sq[:].rearrange("p (h d) -> p h d", d=HD), axis=AX.X, op=ALU.add),
                          reads=[bf("sq")], writes=[bf("ssq")])
                S.add("act", lambda e, bk=bkv: e.activation(out=sq[:, 0:256], in_=ps[bk][:, 0:256], func=AF.Square), reads=[pB[bkv]], writes=[bf("sq")])
                S.add("dve", lambda e: e.tensor_reduce(out=ssq[:, 16:20], in_=sq[:, 0:256].rearrange("p (h d) -> p h d", d=HD), axis=AX.X, op=ALU.add),
                      reads=[bf("sq")], writes=[bf("ssq")])
                S.add("act", lambda e: e.activation(out=rqk[:], in_=ssq[:], func=AF.Sqrt, bias=EPS, scale=1.0 / HD), reads=[bf("ssq")], writes=[bf("rqk")])
                S.add("dve", lambda e: e.reciprocal(out=rqk[:], in_=rqk[:]), reads=[bf("rqk")], writes=[bf("rqk")])
                for hi, bk in enumerate((bq0, bq1)):
                    S.add("dve", lambda e, hi=hi, bk=bk: e.tensor_tensor(
                        out=qn[:, hi * 512:(hi + 1) * 512].rearrange("p (lo mid d) -> p mid lo d", lo=4, mid=2, d=HD),
                        in0=ps[bk][:].rearrange("p (mid lo d) -> p mid lo d", mid=2, lo=4, d=HD),
                        in1=rqk[:, hi * 8:(hi + 1) * 8].rearrange("p (mid lo) -> p mid lo", mid=2).unsqueeze(3).to_broadcast([128, 2, 4, HD]),
                        op=ALU.mult), reads=[pB[bk], bf("rqk")], writes=[bf("qn")])
                S.add("dve", lambda e, bk=bkv: e.tensor_tensor(
                    out=kn[:].rearrange("p (h d) -> p h d", d=HD), in0=ps[bk][:, 0:256].rearrange("p (h d) -> p h d", d=HD),
                    in1=rqk[:, 16:20].unsqueeze(2).to_broadcast([128, 4, HD]), op=ALU.mult), reads=[pB[bkv], bf("rqk")], writes=[bf("kn")])
                S.add("act", lambda e, j=j, bk=bkv: e.activation(out=Vaug[:, j + 1, :].rearrange("p (h d) -> p h d", d=65)[:, :, 0:64],
                                                                 in_=ps[bk][:, 256:512].rearrange("p (h d) -> p h d", d=HD), func=AF.Copy),
                      reads=[pB[bkv]], writes=[bf("V%d" % (j + 1))])

            def qkv_post_b(j, bq0, bq1, bkv):
                bt = nextbank()

                def trq(e, bt=bt):
                    i = None
                    for c in range(8):
                        i = e.transpose(out=psb[bt][:, c * 128:(c + 1) * 128], in_=qn[:, c * 128:(c + 1) * 128], identity=ident_b[:])
                    return i
                S.add("pe", trq, reads=[bf("qn"), bf("identb")], writes=[pB[bt]])
                for mid in range(2):
                    eng = "act" if mid == 0 else "dve"
                    dst = QTz[mid * 64:(mid + 1) * 64, :, j * 128:(j + 1) * 128].rearrange("p (hi m lo) t -> p hi m lo t", hi=2, m=2, lo=4)[:, :, mid, :, :]
                    src = psb[bt][mid * 64:(mid + 1) * 64, :].rearrange("p (hi lo t) -> p hi lo t", hi=2, lo=4)
                    if eng == "act":
                        S.add("act", lambda e, dst=dst, src=src: e.activation(out=dst, in_=src, func=AF.Copy), reads=[pB[bt]], writes=[bf("QTz")])
                    else:
                        S.add("dve", lambda e, dst=dst, src=src: e.tensor_copy(out=dst, in_=src), reads=[pB[bt]], writes=[bf("QTz")])
                bt2 = nextbank()

                def trk(e, bt2=bt2):
                    i = None
                    for c in range(2):
                        i = e.transpose(out=psb[bt2][:, c * 128:(c + 1) * 128], in_=kn[:, c * 128:(c + 1) * 128], identity=ident_b[:])
                    return i
                S.add("pe", trk, reads=[bf("kn"), bf("identb")], writes=[pB[bt2]])
                S.add("act", lambda e, j=j, bt2=bt2: e.activation(out=KT[:, :, (j + 1) * 128:(j + 2) * 128], in_=psb[bt2][:, 0:256].rearrange("p (g t) -> p g t", g=2),
                                                                   func=AF.Identity, scale=gqkp[:, 0:1]),
                      reads=[pB[bt2], bf("gqkp")], writes=[bf("KT%d" % (j + 1))])
                reserved.difference_update(qkv_banks[j])

            def glu_gen():
                S.add("pool", lambda e: e.tensor_copy(out=hbuf[:, :, 0:32], in_=halo_h[:]), reads=[bf("halo_h")], writes=[bf("hbuf")] + [bf("hT")])
                for half in range(2):
                    wa, wab = next_unit()
                    wg, wgb = next_unit()
                    for o4 in range(4):
                        c = half * 4 + o4
                        ba = nextbank()
                        mm_group(ba, 0, T, lambda k, wa=wa, o4=o4: wa[:, k, o4 * 128:(o4 + 1) * 128], lambda k: uT[:, k, :], 8, UT + [wab])
                        bg = nextbank()
                        mm_group(bg, 0, T, lambda k, wg=wg, o4=o4: wg[:, k, o4 * 128:(o4 + 1) * 128], lambda k: uT[:, k, :], 8, UT + [wgb])
                        a = alt("tn")
                        S.add("act", lambda e, bg=bg, a=a: e.activation(out=tnb[a][:], in_=ps[bg][:], func=AF.Tanh, scale=0.5), reads=[pB[bg]], writes=[bf("tn%d" % a)])
                        S.add("dve", lambda e, ba=ba, a=a, c=c: e.scalar_tensor_tensor(out=hbuf[:, c, 32:32 + T], in0=tnb[a][:], scalar=1.0, in1=ps[ba][:], op0=ALU.add, op1=ALU.mult),
                              reads=[bf("tn%d" % a), pB[ba]], writes=[bf("hbuf")])
                        yield
                    release(2)
                S.add("pool", lambda e: e.tensor_copy(out=halo_h[:], in_=hbuf[:, :, T:T + 32]), reads=[bf("hbuf")], writes=[bf("halo_h")])


            def drain(g, n=100):
                for _ in range(n):
                    try:
                        next(g)
                    except StopIteration:
                        return
            qkv_mm(0)
            qkv_mm(1)
            qkv_post(0)
            qkv_mm(2)
            qkv_post(1)
            qkv_mm(3)
            release(3)
            qkv_post(2)
            qkv_post(3, "a")
            gg = glu_gen()
            drain(gg, 3)
            qkv_post(3, "b")
            drain(gg)
            def gated_proj_gen(actT, actbuf, first, held=None, nhold=0):
                nh = [nhold]
                for half in range(2):
                    wp, wpb = next_unit()
                    wg, wgb = next_unit()
                    for o4 in range(4):
                        oc = half * 4 + o4
                        bp = nextbank()
                        mm_group(bp, 0, T, lambda k, wp=wp, o4=o4: wp[:, k, o4 * 128:(o4 + 1) * 128], lambda k: actT[:, k, :], 8, [actbuf, wpb])
                        bg = nextbank()
                        mm_group(bg, 0, T, lambda k, wg=wg, o4=o4: wg[:, k, o4 * 128:(o4 + 1) * 128], lambda k: uT[:, k, :], 8, UT + [wgb])
                        def evac(bp=bp, bg=bg, oc=oc):
                            a = alt("tn")
                            S.add("act", lambda e, bg=bg, a=a: e.activation(out=tnb[a][:], in_=ps[bg][:], func=AF.Tanh, scale=0.5), reads=[pB[bg]], writes=[bf("tn%d" % a)])
                            if first:
                                S.add("dve", lambda e, bp=bp, a=a, oc=oc: e.scalar_tensor_tensor(out=mT[:, oc, :], in0=tnb[a][:], scalar=1.0, in1=ps[bp][:], op0=ALU.add, op1=ALU.mult),
                                      reads=[bf("tn%d" % a), pB[bp]], writes=[bf("mT"), bf("QTz")])
                            else:
                                S.add("dve", lambda e, bp=bp, a=a: e.scalar_tensor_tensor(out=tnb[a][:], in0=tnb[a][:], scalar=1.0, in1=ps[bp][:], op0=ALU.add, op1=ALU.mult),
                                      reads=[bf("tn%d" % a), pB[bp]], writes=[bf("tn%d" % a)])
                                S.add("dve", lambda e, a=a, oc=oc: e.tensor_tensor(out=mT[:, oc, :], in0=tnb[a][:], in1=mT[:, oc, :], op=ALU.add),
                                      reads=[bf("tn%d" % a), bf("mT")], writes=[bf("mT")])
                        if held is not None and nh[0] > 0:
                            nh[0] -= 1
                            held.append(evac)
                        else:
                            evac()
                        yield
                    release(2)

            def drain(g, n=100):
                for _ in range(n):
                    try:
                        next(g)
                    except StopIteration:
                        return

            CB = 4

            def conv_steps():
                for c in range(8):
                    a = c % 2
                    for j0 in range(0, CW, 4):
                        j1 = min(CW, j0 + 4)

                        def cm(e, c=c, a=a, j0=j0, j1=j1):
                            i = None
                            for jt in range(j0, j1):
                                i = e.matmul(ps[CB][:], lhsT=dg[a][:, jt, :], rhs=hbuf[:, c, 2 + jt:2 + jt + T], start=(jt == 0), stop=(jt == CW - 1))
                            return i
                        S.add("pe", cm, reads=[bf("dg%d" % a), bf("hbuf")], writes=[pB[CB]])
                        if j1 == CW:
                            S.add("act", lambda e, c=c: e.activation(out=cv[:, c, :], in_=ps[CB][:], func=AF.Identity, bias=colp[:, c, 2:3]),
                                  reads=[pB[CB]] + CP, writes=[bf("cv%d" % c)])
                            if c + 2 < 8:
                                build_dg(c + 2)
                        yield
            cgen = conv_steps()
            pool_now[0] = (0, 1, 2, 3)

            def conv_step(n=1):
                for _ in range(n):
                    try:
                        next(cgen)
                    except StopIteration:
                        return
            OB = (5, 6, 7)

            def ohead(h):
                return OB[h // 7], (h % 7) * 65
            deferred = []
            for j in range(NB):
                pend = []
                for pr in range(8):
                    if pr == 3 and deferred:
                        deferred.pop(0)()
                    h0 = 2 * pr
                    bk = nextbank()

                    def smm(e, bk=bk, h0=h0, j=j):
                        i = None
                        for hl in range(2):
                            h = h0 + hl
                            gp = (h // 4) // 2
                            for kb in range(2):
                                c = (hl * 2 + kb) * 128
                                i = e.matmul(ps[bk][:, c:c + 128], lhsT=KT[:, gp, (j + kb) * 128:(j + kb + 1) * 128],
                                             rhs=QTz[:, h, j * 128:(j + 1) * 128], start=True, stop=True)
                        return i
                    S.add("pe", smm, reads=[bf("QTz"), bf("KT%d" % j), bf("KT%d" % (j + 1))], writes=[pB[bk]])
                    a = alt("E", 3)
                    S.add("act", lambda e, bk=bk, a=a: e.activation(out=Eb[a][:], in_=ps[bk][:], func=AF.Exp), reads=[pB[bk]], writes=[bf("E%d" % a)])
                    S.add("dve", lambda e, a=a, h0=h0: e.tensor_tensor(out=PTb[a][:], in0=Eb[a][:], in1=EB[:, h0 * 256:(h0 + 2) * 256], op=ALU.mult),
                          reads=[bf("E%d" % a), bf("EB")], writes=[bf("PT%d" % a)])
                    pend.append((h0, a))
                    conv_step(2)
                    if len(pend) == 3 or pr == 7:
                        todo = pend[:1] if pr < 7 else pend
                        pend = pend[1:] if pr < 7 else []
                        for (hh0, aa) in todo:
                            def pv(e, hh0=hh0, aa=aa, j=j):
                                i = None
                                for hl in range(2):
                                    h = hh0 + hl
                                    g = h // 4
                                    ob, off = ohead(h)
                                    for kb in range(2):
                                        c = (hl * 2 + kb) * 128
                                        i = e.matmul(ps[ob][:, off:off + 65], lhsT=PTb[aa][:, c:c + 128], rhs=Vaug[:, j + kb, g * 65:(g + 1) * 65],
                                                     start=(kb == 0), stop=(kb == 1))
                                return i
                            obs = sorted(set(ohead(hh0 + hl)[0] for hl in range(2)))
                            S.add("pe", pv, reads=[bf("PT%d" % aa), bf("V%d" % j), bf("V%d" % (j + 1))], writes=[pB[o] for o in obs])
                for bi, ob in enumerate(OB):
                    hs = bi * 7
                    nh = 7 if bi < 2 else 2
                    ov = ps[ob][:, 0:nh * 65].rearrange("p (h d) -> p h d", d=65)
                    S.add("dve", lambda e, ov=ov, hs=hs, nh=nh: e.tensor_tensor(out=den[:, hs:hs + nh].unsqueeze(2), in0=ov[:, :, 64:65], in1=es[:, hs:hs + nh].unsqueeze(2), op=ALU.add),
                          reads=[pB[ob], bf("es")], writes=[bf("den")])
                    S.add("dve", lambda e, hs=hs, nh=nh: e.reciprocal(out=rden[:, hs:hs + nh], in_=den[:, hs:hs + nh]), reads=[bf("den")], writes=[bf("rden")])
                    S.add("dve", lambda e, ov=ov, hs=hs, nh=nh: e.tensor_tensor(out=attn_n[:, hs * 64:(hs + nh) * 64].rearrange("p (h d) -> p h d", d=HD), in0=ov[:, :, 0:64],
                                                                               in1=rden[:, hs:hs + nh].unsqueeze(2).to_broadcast([128, nh, HD]), op=ALU.mult),
                          reads=[pB[ob], bf("rden")], writes=[bf("attn_n")])
                def finish_blk(j=j):
                    bt = nextbank()

                    def tra(e, bt=bt):
                        i = None
                        for c in range(8):
                            i = e.transpose(out=psb[bt][:, c * 128:(c + 1) * 128], in_=attn_n[:, c * 128:(c + 1) * 128], identity=ident_b[:])
                        return i
                    S.add("pe", tra, reads=[bf("attn_n"), bf("identb")], writes=[pB[bt]])
                    S.add("act", lambda e, j=j, bt=bt: e.activation(out=attnT[:, :, j * 128:(j + 1) * 128], in_=psb[bt][:].rearrange("p (c t) -> p c t", c=8), func=AF.Copy),
                          reads=[pB[bt]], writes=[bf("attnT")])
                deferred.append(finish_blk)
            while deferred:
                deferred.pop(0)()
            conv_step(1000)
            S.add("pool", lambda e: e.tensor_copy(out=KT[:, :, 0:128], in_=KT[:, :, NB * 128:(NB + 1) * 128]), reads=[bf("KT%d" % NB)], writes=[bf("KT0")])
            S.add("pool", lambda e: e.tensor_copy(out=Vaug[:, 0, :], in_=Vaug[:, NB, :]), reads=[bf("V%d" % NB)], writes=[bf("V0")])

            pool_now[0] = (0, 1, 2, 3, 6, 7)
            BM, BQ = 4, 5
            for c in range(8):
                a = alt("cvb")
                S.add("dve", lambda e, c=c, a=a: e.tensor_copy(out=cvb[a][:], in_=cv[:, c, :]), reads=[bf("cv%d" % c)], writes=[bf("cvb%d" % a)])
                S.add("act", lambda e, c=c, a=a: e.activation(out=sqb[a][:], in_=cv[:, c, :], func=AF.Square), reads=[bf("cv%d" % c)], writes=[bf("sqb%d" % a)])
                S.add("pe", lambda e, c=c, a=a: e.matmul(ps[BM][:], lhsT=onesm[:], rhs=cvb[a][:], start=(c == 0), stop=(c == 7)),
                      reads=[bf("cvb%d" % a), bf("onesm")], writes=[pB[BM]])
                S.add("pe", lambda e, c=c, a=a: e.matmul(ps[BQ][:], lhsT=onesm[:], rhs=sqb[a][:], start=(c == 0), stop=(c == 7)),
                      reads=[bf("sqb%d" % a), bf("onesm")], writes=[pB[BQ]])
            held = []
            ga = gated_proj_gen(attnT, bf("attnT"), True, held, 3)
            drain(ga, 3)
            S.add("act", lambda e: e.activation(out=mean_sb[:], in_=ps[BM][:], func=AF.Copy), reads=[pB[BM]], writes=[bf("mean")])
            S.add("dve", lambda e: e.tensor_tensor(out=nmr_sb[:], in0=mean_sb[:], in1=mean_sb[:], op=ALU.mult), reads=[bf("mean")], writes=[bf("nmr")])
            S.add("dve", lambda e: e.tensor_tensor(out=rstd_sb[:], in0=ps[BQ][:], in1=nmr_sb[:], op=ALU.subtract), reads=[pB[BQ], bf("nmr")], writes=[bf("rstd")])
            S.add("dve", lambda e: e.tensor_scalar(out=rstd_sb[:], in0=rstd_sb[:], scalar1=0.0, scalar2=None, op0=ALU.max), reads=[bf("rstd")], writes=[bf("rstd")])
            S.add("act", lambda e: e.activation(out=rstd_sb[:], in_=rstd_sb[:], func=AF.Sqrt, bias=EPS, scale=1.0), reads=[bf("rstd")], writes=[bf("rstd")])
            S.add("dve", lambda e: e.reciprocal(out=rstd_sb[:], in_=rstd_sb[:]), reads=[bf("rstd")], writes=[bf("rstd")])
            S.add("dve", lambda e: e.scalar_tensor_tensor(out=nmr_sb[:], in0=mean_sb[:], scalar=-1.0, in1=rstd_sb[:], op0=ALU.mult, op1=ALU.mult),
                  reads=[bf("mean"), bf("rstd")], writes=[bf("nmr")])
            for ev in held:
                ev()
            ybs = [(ybuf[:], bf("ybuf")), (xnb[0][:].bitcast(F32), bf("xn0"))]
            zbs = [(zbuf[:], bf("zbuf")), (xnb[1][:].bitcast(F32), bf("xn1"))]
            def ln_pre(c):
                yb, ybB = ybs[c % 2]
                S.add("pool", lambda e, c=c, yb=yb: e.tensor_tensor(out=yb, in0=cv[:, c, :], in1=rstd_sb[:], op=ALU.mult), reads=[bf("cv%d" % c), bf("rstd")], writes=[ybB])
                S.add("dve", lambda e, yb=yb: e.tensor_tensor(out=yb, in0=yb, in1=nmr_sb[:], op=ALU.add), reads=[ybB, bf("nmr")], writes=[ybB])
            ln_pre(0)
            for c in range(8):
                yb, ybB = ybs[c % 2]
                zb, zbB = zbs[c % 2]
                if c + 1 < 8:
                    ln_pre(c + 1)
                S.add("act", lambda e, c=c, yb=yb, zb=zb: e.activation(out=zb, in_=yb, func=AF.Identity, scale=colh[:, c, 3:4], bias=colh[:, c, 4:5]),
                      reads=[ybB] + CP, writes=[zbB])
                a = alt("tn")
                S.add("act", lambda e, a=a, zb=zb: e.activation(out=tnb[a][:], in_=zb, func=AF.Tanh), reads=[zbB], writes=[bf("tn%d" % a)])
                S.add("dve", lambda e, a=a, c=c, zb=zb: e.scalar_tensor_tensor(out=sT[:, c, :], in0=tnb[a][:], scalar=1.0, in1=zb, op0=ALU.add, op1=ALU.mult),
                      reads=[bf("tn%d" % a), zbB], writes=[bf("sT"), bf("QTz")])
                drain(ga, 1)
            drain(ga)


            drain(gated_proj_gen(sT, bf("sT"), False))
            pool_now[0] = tuple(range(8))

            wo = [next_unit(), next_unit()]
            for j in range(NB):
                for half in range(2):
                    w, wb = wo[half]
                    bk = nextbank()
                    mm_group(bk, 0, 512, lambda k, j=j: mT[:, k, j * 128:(j + 1) * 128], lambda k, w=w: w[:, k, :], 8, [bf("mT"), wb])
                    S.add("dve", lambda e, j=j, half=half, bk=bk, x_t=x_t: e.scalar_tensor_tensor(out=x_t[:, j, half * 512:(half + 1) * 512], in0=ps[bk][:], scalar=0.5,
                                                                                      in1=x_t[:, j, half * 512:(half + 1) * 512], op0=ALU.mult, op1=ALU.add),
                          reads=[pB[bk], xb[j]], writes=[xb[j]])
                rms_sq(j, x_t, xb)
                if j == 2:
                    rms_head(0, 3)
            release(2)
            if tile + 1 < ntiles:
                load_x(tile + 1)
            if tile + 1 < ntiles:
                S.add("pool", lambda e: e.memset(regA[:], 0.0), reads=[], writes=[bf("QTz"), bf("sT"), bf("mT")])

            rms_finish(1, x_t, xb, blocks=[0, 1, 2], head=False)
            rms_head(3, 4)
            rms_finish(1, x_t, xb, blocks=[3], head=False)

            for f in range(8):
                if f == 4 and tile + 1 < ntiles:
                    rms1_of(tile + 1, "head")
                w, wb = next_unit()
                for o4 in range(4):
                    fc = f * 4 + o4
                    bk = nextbank()
                    mm_group(bk, 0, T, lambda k, w=w, o4=o4: w[:, k, o4 * 128:(o4 + 1) * 128], lambda k: uT[:, k, :], 8, UT + [wb])
                    a = alt("rl")
                    S.add("act", lambda e, bk=bk, a=a: e.activation(out=rl[a][:], in_=ps[bk][:], func=AF.Relu), reads=[pB[bk]], writes=[bf("rl%d" % a)])
                    S.add("pool", lambda e, a=a, fc=fc: e.tensor_tensor(out=hT[:, fc, :], in0=rl[a][:], in1=rl[a][:], op=ALU.mult),
                          reads=[bf("rl%d" % a)], writes=[bf("hT"), bf("hbuf")] + [bf("cv%d" % c) for c in range(8)])
                release(1)

            for j2 in range(2):
                for g in range(4):
                    if j2 == 0 and tile + 1 < ntiles:
                        pool_now[0] = (0, 1, 2, 3)
                        rms1_of(tile + 1, "body", blocks=[g])
                        pool_now[0] = tuple(range(8))
                    w, wb = next_unit()
                    for j in range(NB):
                        bk = (4 if j2 == 0 else 0) + j

                        def f2(e, w=w, g=g, j=j, bk=bk):
                            i = None
                            for k in range(8):
                                i = e.matmul(ps[bk][:], lhsT=hT[:, g * 8 + k, j * 128:(j + 1) * 128], rhs=w[:, k, :], start=(g == 0 and k == 0), stop=(g == 3 and k == 7))
                            return i
                        S.add("pe", f2, reads=[bf("hT"), wb], writes=[pB[bk]])
                    release(1)
                for j in range(NB):
                    bk = (4 if j2 == 0 else 0) + j
                    S.add("dve", lambda e, j=j, j2=j2, bk=bk, x_t=x_t: e.tensor_tensor(out=x_t[:, j, j2 * 512:(j2 + 1) * 512], in0=ps[bk][:], in1=x_t[:, j, j2 * 512:(j2 + 1) * 512], op=ALU.add),
                          reads=[pB[bk], xb[j]], writes=[xb[j]])
            tk = S.add("sp", lambda e, s, t0=t0, x_t=x_t: e.dma_start(out=out_d[t0:t0 + T, :].rearrange("(j p) d -> p j d", p=128), in_=x_t[:]).then_inc(s, 16),
                       reads=xb, dkey="st%d" % (tile % 2))
            out_tokens.append(tk)
        S.add("sp", None, extra=out_tokens[-2:])
        S.emit(nc, st)
    return nc


def t5_bucket(dist):
    n = np.maximum(dist, 0)
    max_exact = 16
    nf = np.maximum(n, 1).astype(np.float32)
    large = max_exact + (np.log(nf / np.float32(max_exact)) / np.float32(np.log(128 / max_exact)) * np.float32(32 - max_exact)).astype(np.int32)
    large = np.minimum(large, 31)
    return np.where(n < max_exact, n, large)


def host_layout(inputs, ntiles=SEQ // T, ncores=8):
    f = lambda a: np.ascontiguousarray(np.asarray(a, dtype=np.float32))
    x = f(inputs["x"])
    rowp = np.concatenate([f(inputs["norm_mix_g"]), f(inputs["norm_mlp_g"]), f(inputs["b_dw"]), f(inputs["conv_ln_g"]),
                           f(inputs["conv_ln_b"]), f(inputs["w_dw"])[0]], axis=0)
    gq = f(inputs["q_norm_g"])[0]
    gk = f(inputs["k_norm_g"])[0]
    gqk = np.stack([np.tile(gq, 2), np.tile(gk, 2)], axis=1)
    gqk = np.ascontiguousarray(gqk)
    sinks = np.ascontiguousarray(np.broadcast_to(f(inputs["attn_sinks"])[0][None, :], (128, NQ)))
    rb = f(inputs["rel_bias"])
    k = np.arange(128)[:, None]
    q = np.arange(128)[None, :]
    biasT = np.empty((128, NQ, 2, 128), np.float32)
    for kb in range(2):
        dist = q + 128 - (kb * 128 + k)
        valid = (dist >= 0) & (dist < 128)
        g = rb[t5_bucket(dist)]
        g = np.where(valid[:, :, None], g, np.float32(-1e30))
        biasT[:, :, kb, :] = np.transpose(g, (0, 2, 1))
    biasT = np.ascontiguousarray(biasT.reshape(128, NQ * 256))
    common = {
        "w_in": f(inputs["w_in"])[0], "w_attn_o": f(inputs["w_attn_o"])[0], "w_conv_out": f(inputs["w_conv_out"])[0],
        "w_out": f(inputs["w_out"])[0], "w_ff1": f(inputs["w_ff1"])[0], "w_ff2": f(inputs["w_ff2"])[0],
        "rowp": np.ascontiguousarray(rowp), "gqk": gqk, "sinks": sinks, "biasT": biasT,
        "ident": np.eye(128, dtype=np.float32),
    }
    maps = []
    for c in range(ncores):
        m = dict(common)
        m["x"] = np.ascontiguousarray(x[c, :ntiles * T])
        maps.append(m)
    return maps


def kernel(**inputs):
    ntiles = SEQ // T
    nc = build(ntiles)
    maps = host_layout(inputs, ntiles, 8)
    res = run_bass_kernel_spmd(nc, maps, core_ids=list(range(8)))
    return np.stack([r["out"] for r in res.results], axis=0).astype(np.float32)
```

```python
import numpy as np
from contextlib import ExitStack
import concourse.bass as bass
import concourse.mybir as mybir
from concourse.bass_utils import run_bass_kernel_spmd

F32 = mybir.dt.float32
BF16 = mybir.dt.bfloat16
AF = mybir.ActivationFunctionType
ALU = mybir.AluOpType
AX = mybir.AxisListType

D = 1024
SEQ = 8192
NB = 4
T = NB * 128
HD = 64
NQ = 16
NKV = 4
CW = 31
DFF = 4096
EPS = 1e-6
V_END = 1536
GLU_END = V_END + 2048
IN_W = GLU_END + 2048
NCOLP = 5 + CW
NSLOT = 4
ENGS = ("pe", "act", "dve", "pool", "sp")


class Buf:
    __slots__ = ("name", "w", "r")

    def __init__(self, name):
        self.name = name
        self.w = None
        self.r = []


class Sched:
    def __init__(self):
        self.ops = {e: [] for e in ENGS}
        self.cnt = {}
        self.seen = {e: {} for e in ENGS}
        self.keys = []

    def _bump(self, key, n):
        if key not in self.cnt:
            self.cnt[key] = 0
            self.keys.append(key)
        self.cnt[key] += n
        return (key, self.cnt[key])

    def add(self, eng, fn, reads=(), writes=(), dkey=None, ndma=1, extra=()):
        deps = {}

        def need(t):
            if t is None:
                return
            k, v = t
            if eng == "pe" and k == "pe":
                return
            if deps.get(k, 0) < v:
                deps[k] = v
        for b in reads:
            need(b.w)
        for b in writes:
            need(b.w)
            for t in b.r:
                need(t)
        for t in extra:
            need(t)
        waits = []
        seen = self.seen[eng]
        for k, v in deps.items():
            if seen.get(k, 0) >= v:
                continue
            seen[k] = v
            waits.append((k, v))
        if fn is None:
            tok = None
        elif dkey is not None:
            tok = self._bump(dkey, 16 * ndma)
        else:
            tok = self._bump(eng, 1)
        self.ops[eng].append((waits, fn, tok, dkey is not None))
        if tok is not None:
            for b in reads:
                b.r.append(tok)
            for b in writes:
                b.w = tok
                b.r = []
        return tok

    def emit(self, nc, stack):
        sems = {}
        for i, k in enumerate(self.keys):
            sems[k] = stack.enter_context(nc.semaphore("s%d" % i))
        block = stack.enter_context(nc.Block())
        handles = {"pe": block.tensor, "act": block.scalar, "dve": block.vector,
                   "pool": block.gpsimd, "sp": block.sync}
        for eng in ENGS:
            ops = self.ops[eng]
            if not ops:
                continue

            def body(e, ops=ops):
                for waits, fn, tok, is_dma in ops:
                    for k, v in waits:
                        e.wait_ge(sems[k], v)
                    if fn is None:
                        continue
                    if is_dma:
                        fn(e, sems[tok[0]])
                    else:
                        fn(e).then_inc(sems[tok[0]], 1)
            handles[eng](body)


def build(ntiles):
    seq = ntiles * T
    nc = bass.Bass("TRN2", target_bir_lowering=False)
    x_d = nc.dram_tensor("x", [seq, D], F32, kind="ExternalInput").ap()
    w_in_d = nc.dram_tensor("w_in", [D, IN_W], F32, kind="ExternalInput").ap()
    w_ao_d = nc.dram_tensor("w_attn_o", [D, D], F32, kind="ExternalInput").ap()
    w_co_d = nc.dram_tensor("w_conv_out", [D, D], F32, kind="ExternalInput").ap()
    w_out_d = nc.dram_tensor("w_out", [D, D], F32, kind="ExternalInput").ap()
    w_ff1_d = nc.dram_tensor("w_ff1", [D, DFF], F32, kind="ExternalInput").ap()
    w_ff2_d = nc.dram_tensor("w_ff2", [DFF, D], F32, kind="ExternalInput").ap()
    rowp_d = nc.dram_tensor("rowp", [NCOLP, D], F32, kind="ExternalInput").ap()
    gqk_d = nc.dram_tensor("gqk", [128, 2], F32, kind="ExternalInput").ap()
    sink_d = nc.dram_tensor("sinks", [128, NQ], F32, kind="ExternalInput").ap()
    bias_d = nc.dram_tensor("biasT", [128, NQ * 2 * 128], F32, kind="ExternalInput").ap()
    ident_d = nc.dram_tensor("ident", [128, 128], F32, kind="ExternalInput").ap()
    out_d = nc.dram_tensor("out", [seq, D], F32, kind="ExternalOutput").ap()
    wscr = nc.dram_tensor("wscr", [33, 128, 8 * 512], BF16).ap()

    S = Sched()
    with ExitStack() as st:
        def sb(name, shape, dt):
            return st.enter_context(nc.sbuf_tensor(name, shape, dt))

        x_tb = [sb("x_t%d" % i, [128, NB, D], F32) for i in range(2)]
        uT = sb("uT", [128, 8, T], BF16)
        xnb = [sb("xn%d" % i, [128, D], BF16) for i in range(2)]
        sq = sb("sq", [128, 512], F32)
        junk = sq[:].bitcast(BF16)
        gbc = [sb("gbc%d" % i, [128, D], F32) for i in range(2)]
        qn = sb("qn", [128, D], BF16)
        kn = sb("kn", [128, 256], BF16)
        regA = sb("regA", [128, NQ * T], BF16)
        QTz = regA[:].rearrange("p (h t) -> p h t", h=NQ)
        sT = regA[:, 0:8 * T].rearrange("p (c t) -> p c t", c=8)
        mT = regA[:, 8 * T:16 * T].rearrange("p (c t) -> p c t", c=8)
        KT = sb("KT", [128, 2, (NB + 1) * 128], BF16)
        Vaug = sb("Vaug", [128, NB + 1, NKV * 65], BF16)
        Eb = [sb("E%d" % i, [128, 512], BF16) for i in range(3)]
        PTb = [sb("PT%d" % i, [128, 512], BF16) for i in range(3)]
        EB = sb("EB", [128, NQ * 256], BF16)
        attn_n = sb("attn_n", [128, D], BF16)
        attnT = sb("attnT", [128, 8, T], BF16)
        tnb = [sb("tn%d" % i, [128, 512], F32) for i in range(2)]
        regB = sb("regB", [128, 16 * 1024], BF16)
        hT = regB[:, 0:32 * T].rearrange("p (c t) -> p c t", c=32)
        regBf = regB[:].bitcast(F32)
        hbuf = regB[:, 0:8 * (T + 32)].rearrange("p (c t) -> p c t", c=8)
        cv = regBf[:, 4 * (T + 32):4 * (T + 32) + 8 * T].rearrange("p (c t) -> p c t", c=8)
        dg = [sb("dg%d" % i, [128, CW, 128], BF16) for i in range(2)]
        cvb = [sb("cvb%d" % i, [128, T], BF16) for i in range(2)]
        sqb = [sb("sqb%d" % i, [128, T], BF16) for i in range(2)]
        mean_sb = sb("mean_sb", [128, T], F32)
        rstd_sb = sb("rstd_sb", [128, T], F32)
        nmr_sb = sb("nmr_sb", [128, T], F32)
        ybuf = sb("ybuf", [128, T], F32)
        zbuf = sb("zbuf", [128, T], F32)
        rl = [sb("rl%d" % i, [128, T], BF16) for i in range(2)]
        wsl = [sb("wsl%d" % i, [128, 8, 512], BF16) for i in range(NSLOT)]
        ident_f = sb("ident_f", [128, 128], F32)
        ident_b = sb("ident_b", [128, 128], BF16)
        onesm = sb("onesm", [128, 128], BF16)
        rowp = regBf[0:NCOLP, 4096:4096 + D]
        colp = sb("colp", [128, 8, NCOLP], F32)
        colh = sb("colh", [128, 8, NCOLP], F32)
        gqk = sb("gqk_sb", [128, 2], F32)
        gqkp = sb("gqkp", [128, 1], F32)
        es = sb("es", [128, NQ], F32)
        bias_sb = regBf[:, 0:1024]
        ss1 = sb("ss1", [128, NB], F32)
        r1 = sb("r1", [128, NB], F32)
        ssq = sb("ssq", [128, 20], F32)
        rqk = sb("rqk", [128, 20], F32)
        den = sb("den", [128, NQ], F32)
        rden = sb("rden", [128, NQ], F32)
        halo_h = sb("halo_h", [128, 8, 32], BF16)
        ps = [st.enter_context(nc.psum_tensor("ps%d" % i, [128, 512], F32)) for i in range(8)]
        psb = [p[:].bitcast(BF16) for p in ps]

        B = {}

        def bf(name):
            if name not in B:
                B[name] = Buf(name)
            return B[name]
        pB = [bf("ps%d" % i) for i in range(8)]
        rot = [0]

        pool_now = [tuple(range(8))]

        reserved = set()

        def nextbank():
            p = pool_now[0]
            while True:
                i = p[rot[0] % len(p)]
                rot[0] += 1
                if i not in reserved:
                    return i
        tgl = {}

        def alt(name, n=2):
            tgl[name] = tgl.get(name, -1) + 1
            return tgl[name] % n

        S.add("sp", lambda e, s: e.dma_start(out=ident_f[:], in_=ident_d).then_inc(s, 16), writes=[bf("identf")], dkey="c0")
        S.add("sp", lambda e, s: e.dma_start(out=rowp, in_=rowp_d).then_inc(s, 16), writes=[bf("rowp"), bf("hT")], dkey="c1")
        S.add("sp", lambda e, s: e.dma_start(out=gqk[:], in_=gqk_d).then_inc(s, 16), writes=[bf("gqk")], dkey="c2")
        S.add("sp", lambda e, s: e.dma_start(out=es[:], in_=sink_d).then_inc(s, 16), writes=[bf("es")], dkey="c3")
        for gi in range(2):
            S.add("sp", lambda e, s, gi=gi: e.dma_start(out=gbc[gi][:], in_=rowp_d[gi:gi + 1, :].partition_broadcast(128)).then_inc(s, 16),
                  writes=[bf("gbc")], dkey="c5%d" % gi)
        S.add("dve", lambda e: e.tensor_copy(out=ident_b[:], in_=ident_f[:]), reads=[bf("identf")], writes=[bf("identb")])
        S.add("dve", lambda e: e.memset(onesm[:], 1.0 / D), writes=[bf("onesm")])
        S.add("dve", lambda e: e.memset(regA[:], 0.0), writes=[bf("QTz"), bf("sT"), bf("mT")])
        S.add("dve", lambda e: e.memset(KT[:], 0.0), writes=[bf("KT%d" % i) for i in range(NB + 1)])
        S.add("dve", lambda e: e.memset(Vaug[:, 0, :], 0.0), writes=[bf("V0")])
        S.add("dve", lambda e: e.memset(Vaug[:, 1:NB + 1, :], 1.0), writes=[bf("V%d" % i) for i in range(1, NB + 1)])
        S.add("dve", lambda e: e.memset(halo_h[:], 0.0), writes=[bf("halo_h")])
        S.add("dve", lambda e: e.scalar_tensor_tensor(out=gqkp[:], in0=gqk[:, 0:1], scalar=HD ** -0.5, in1=gqk[:, 1:2], op0=ALU.mult, op1=ALU.mult), reads=[bf("gqk")], writes=[bf("gqkp")])
        S.add("act", lambda e: e.activation(out=es[:], in_=es[:], func=AF.Exp), reads=[bf("es")], writes=[bf("es")])
        for hg in range(4):
            S.add("sp", lambda e, s, hg=hg: e.dma_start(out=bias_sb[:], in_=bias_d[:, hg * 1024:(hg + 1) * 1024]).then_inc(s, 16),
                  writes=[bf("bias_sb"), bf("hT")], dkey="c4")
            S.add("act", lambda e, hg=hg: e.activation(out=EB[:, hg * 1024:(hg + 1) * 1024], in_=bias_sb[:], func=AF.Exp),
                  reads=[bf("bias_sb"), bf("hT")], writes=[bf("EB")])
        for kc in range(8):
            bk = nextbank()
            S.add("pe", lambda e, kc=kc, bk=bk: e.transpose(out=ps[bk][:, 0:NCOLP], in_=rowp[:, kc * 128:(kc + 1) * 128],
                                                        identity=ident_f[0:NCOLP, 0:NCOLP]),
                  reads=[bf("rowp"), bf("identf"), bf("hT")], writes=[pB[bk]])
            S.add("dve", lambda e, kc=kc, bk=bk: e.tensor_copy(out=colp[:, kc, :], in_=ps[bk][:, 0:NCOLP]),
                  reads=[pB[bk]], writes=[bf("colp")])
        S.add("dve", lambda e: e.tensor_scalar(out=colh[:], in0=colp[:], scalar1=0.5, scalar2=None, op0=ALU.mult),
              reads=[bf("colp")], writes=[bf("colh")])
        CP = [bf("colp"), bf("colh")]

        def unit_srcs():
            def colpanel(w, c0):
                return w.rearrange("(kc p) n -> p kc n", p=128)[:, :, c0:c0 + 512]
            u = []
            for c0 in (0, 512, 1024):
                u.append(colpanel(w_in_d, c0))
            for j in range(2):
                u.append(colpanel(w_in_d, V_END + j * 512))
                u.append(colpanel(w_in_d, V_END + 1024 + j * 512))
            for j in range(2):
                u.append(colpanel(w_ao_d, j * 512))
                u.append(colpanel(w_in_d, GLU_END + j * 512))
            for j in range(2):
                u.append(colpanel(w_co_d, j * 512))
                u.append(colpanel(w_in_d, GLU_END + 1024 + j * 512))
            for j in range(2):
                u.append(colpanel(w_out_d, j * 512))
            for f in range(8):
                u.append(colpanel(w_ff1_d, f * 512))
            for j2 in range(2):
                for g in range(4):
                    u.append(w_ff2_d[g * 1024:(g + 1) * 1024, j2 * 512:(j2 + 1) * 512].rearrange("(kc p) n -> p kc n", p=128))
            return u
        USRC = unit_srcs()
        NU = len(USRC)
        assert NU == 33
        total_units = ntiles * NU
        issued = [0]
        taken = [0]
        released = [0]

        def issue_fetch():
            n = issued[0]
            if n >= total_units:
                return
            issued[0] += 1
            tile, u = divmod(n, NU)
            s = n % NSLOT
            wb = bf("wsl%d" % s)
            if tile == 0:
                S.add("pool", lambda e, sem, u=u, s=s: e.dma_start(out=wsl[s][:], in_=USRC[u]).then_inc(sem, 16),
                      writes=[wb], dkey="wq%d" % s)
                if ntiles > 1:
                    S.add("sp", lambda e, sem, u=u, s=s: e.dma_start(out=wscr[u], in_=wsl[s][:].rearrange("p k n -> p (k n)")).then_inc(sem, 16),
                          reads=[wb], writes=[bf("scr%d" % u)], dkey="ws%d" % s)
            else:
                S.add("sp", lambda e, sem, u=u, s=s: e.dma_start(out=wsl[s][:].rearrange("p k n -> p (k n)"), in_=wscr[u]).then_inc(sem, 16),
                      reads=[bf("scr%d" % u)], writes=[wb], dkey="wl%d" % s)

        def topup():
            while issued[0] < min(total_units, released[0] + NSLOT):
                issue_fetch()

        def next_unit():
            topup()
            assert taken[0] < issued[0]
            s = taken[0] % NSLOT
            taken[0] += 1
            return wsl[s], bf("wsl%d" % s)

        def release(n):
            released[0] += n
            topup()

        def mm_group(bank, col0, ncol, lhs_fn, rhs_fn, nk, reads, extra_w=()):
            def fn(e):
                i = None
                for k in range(nk):
                    i = e.matmul(ps[bank][:, col0:col0 + ncol], lhsT=lhs_fn(k), rhs=rhs_fn(k), start=(k == 0), stop=(k == nk - 1))
                return i
            return S.add("pe", fn, reads=reads, writes=[pB[bank]] + list(extra_w))

        UT = [bf("uT%d" % j) for j in range(NB)]

        def rms_sq(j, x_t, xbufs):
            S.add("act", lambda e, j=j: e.activation(out=junk, in_=x_t[:, j, :], func=AF.Square, accum_out=ss1[:, j:j + 1]),
                  reads=[xbufs[j]], writes=[bf("sq"), bf("ss1_%d" % j)])

        def rms_head(c0=0, c1=NB):
            rb = [bf("r1_%d" % j) for j in range(c0, c1)]
            S.add("act", lambda e: e.activation(out=r1[:, c0:c1], in_=ss1[:, c0:c1], func=AF.Sqrt, bias=EPS, scale=1.0 / D),
                  reads=[bf("ss1_%d" % j) for j in range(c0, c1)], writes=rb)
            S.add("dve", lambda e: e.reciprocal(out=r1[:, c0:c1], in_=r1[:, c0:c1]), reads=rb, writes=rb)

        def rms_finish(gi, x_t, xbufs, blocks=range(NB), head=True):
            if head:
                rms_head()
            for j in blocks:
                a = alt("xn")
                S.add("dve", lambda e, j=j, a=a: e.scalar_tensor_tensor(out=xnb[a][:], in0=x_t[:, j, :], scalar=r1[:, j:j + 1], in1=gbc[gi][:], op0=ALU.mult, op1=ALU.mult),
                      reads=[xbufs[j], bf("r1_%d" % j), bf("gbc")], writes=[bf("xn%d" % a)])
                bk = nextbank()

                def tr(e, bk=bk, a=a):
                    i = None
                    for kc in range(8):
                        i = e.transpose(out=psb[bk][:, kc * 128:(kc + 1) * 128], in_=xnb[a][:, kc * 128:(kc + 1) * 128], identity=ident_b[:])
                    return i
                S.add("pe", tr, reads=[bf("xn%d" % a), bf("identb")], writes=[pB[bk]])
                S.add("act", lambda e, j=j, bk=bk: e.activation(out=uT[:, :, j * 128:(j + 1) * 128], in_=psb[bk][:].rearrange("p (c t) -> p c t", c=8), func=AF.Copy),
                      reads=[pB[bk]], writes=[UT[j]])

        out_tokens = []
        for tile in range(ntiles):
            t0 = tile * T
            x_t = x_tb[tile % 2]
            xb = [bf("x%d_%d" % (tile % 2, j)) for j in range(NB)]

            def load_x(tl):
                xt2 = x_tb[tl % 2]
                xb2 = [bf("x%d_%d" % (tl % 2, j)) for j in range(NB)]
                tt0 = tl * T
                S.add("sp", lambda e, s: e.dma_start(out=xt2[:], in_=x_d[tt0:tt0 + T, :].rearrange("(j p) d -> p j d", p=128)).then_inc(s, 16),
                      writes=xb2, dkey="xl%d" % (tl % 2))

            def rms1_of(tl, part="all", blocks=range(NB)):
                xt2 = x_tb[tl % 2]
                xb2 = [bf("x%d_%d" % (tl % 2, j)) for j in range(NB)]
                if part in ("all", "head"):
                    for j in range(NB):
                        rms_sq(j, xt2, xb2)
                if part == "all":
                    rms_finish(0, xt2, xb2)
                elif part == "head":
                    rms_head()
                else:
                    rms_finish(0, xt2, xb2, blocks=blocks, head=False)
            if tile == 0:
                load_x(0)
                rms1_of(0)

            def build_dg(c):
                a = c % 2
                S.add("pool", lambda e, a=a, c=c: e.tensor_tensor(out=dg[a][:], in0=ident_b[:].unsqueeze(1).to_broadcast([128, CW, 128]),
                                                                   in1=colh[:, c, 5:5 + CW].unsqueeze(2).to_broadcast([128, CW, 128]), op=ALU.mult),
                      reads=[bf("identb")] + CP, writes=[bf("dg%d" % a)])

            build_dg(0)
            build_dg(1)
            rot[0] = 4
            wq0, wq0b = next_unit()
            wq1, wq1b = next_unit()
            wkv, wkvb = next_unit()
            qkv_banks = {}

            def qkv_mm(j):
                banks = []
                for (w, wb) in ((wq0, wq0b), (wq1, wq1b), (wkv, wkvb)):
                    bk = nextbank()
                    banks.append(bk)
                    mm_group(bk, 0, 512, lambda k, j=j: uT[:, k, j * 128:(j + 1) * 128], lambda k, w=w: w[:, k, :], 8, [UT[j], wb])
                qkv_banks[j] = banks
                reserved.update(banks)

            def qkv_post(j, part="ab"):
                bq0, bq1, bkv = qkv_banks[j]
                if "a" in part:
                    qkv_post_a(j, bq0, bq1, bkv)
                if "b" in part:
                    qkv_post_b(j, bq0, bq1, bkv)

            def qkv_post_a(j, bq0, bq1, bkv):
                for bi, bk in enumerate((bq0, bq1)):
                    S.add("act", lambda e, bk=bk: e.activation(out=sq[:], in_=ps[bk][:], func=AF.Square), reads=[pB[bk]], writes=[bf("sq")])
                    S.add("dve", lambda e, bi=bi: e.tensor_reduce(out=ssq[:, bi * 8:(bi + 1) * 8], in_=sq[:].rearrange("p (h d) -> p h d", d=HD), axis=AX.X, op=ALU.add),
                          reads=[bf("sq")], writes=[bf("ssq")])
                S.add("act", lambda e, bk=bkv: e.activation(out=sq[:, 0:256], in_=ps[bk][:, 0:256], func=AF.Square), reads=[pB[bkv]], writes=[bf("sq")])
                S.add("dve", lambda e: e.tensor_reduce(out=ssq[:, 16:20], in_=sq[:, 0:256].rearrange("p (h d) -> p h d", d=HD), axis=AX.X, op=ALU.add),
                      reads=[bf("sq")], writes=[bf("ssq")])
                S.add("act", lambda e: e.activation(out=rqk[:], in_=ssq[:], func=AF.Sqrt, bias=EPS, scale=1.0 / HD), reads=[bf("ssq")], writes=[bf("rqk")])
                S.add("dve", lambda e: e.reciprocal(out=rqk[:], in_=rqk[:]), reads=[bf("rqk")], writes=[bf("rqk")])
                for hi, bk in enumerate((bq0, bq1)):
                    S.add("dve", lambda e, hi=hi, bk=bk: e.tensor_tensor(
                        out=qn[:, hi * 512:(hi + 1) * 512].rearrange("p (lo mid d) -> p mid lo d", lo=4, mid=2, d=HD),
                        in0=ps[bk][:].rearrange("p (mid lo d) -> p mid lo d", mid=2, lo=4, d=HD),
                        in1=rqk[:, hi * 8:(hi + 1) * 8].rearrange("p (mid lo) -> p mid lo", mid=2).unsqueeze(3).to_broadcast([128, 2, 4, HD]),
                        op=ALU.mult), reads=[pB[bk], bf("rqk")], writes=[bf("qn")])
                S.add("dve", lambda e, bk=bkv: e.tensor_tensor(
                    out=kn[:].rearrange("p (h d) -> p h d", d=HD), in0=ps[bk][:, 0:256].rearrange("p (h d) -> p h d", d=HD),
                    in1=rqk[:, 16:20].unsqueeze(2).to_broadcast([128, 4, HD]), op=ALU.mult), reads=[pB[bkv], bf("rqk")], writes=[bf("kn")])
                S.add("act", lambda e, j=j, bk=bkv: e.activation(out=Vaug[:, j + 1, :].rearrange("p (h d) -> p h d", d=65)[:, :, 0:64],
                                                                 in_=ps[bk][:, 256:512].rearrange("p (h d) -> p h d", d=HD), func=AF.Copy),
                      reads=[pB[bkv]], writes=[bf("V%d" % (j + 1))])

            def qkv_post_b(j, bq0, bq1, bkv):
                bt = nextbank()

                def trq(e, bt=bt):
                    i = None
                    for c in range(8):
                        i = e.transpose(out=psb[bt][:, c * 128:(c + 1) * 128], in_=qn[:, c * 128:(c + 1) * 128], identity=ident_b[:])
                    return i
                S.add("pe", trq, reads=[bf("qn"), bf("identb")], writes=[pB[bt]])
                for mid in range(2):
                    eng = "act" if mid == 0 else "dve"
                    dst = QTz[mid * 64:(mid + 1) * 64, :, j * 128:(j + 1) * 128].rearrange("p (hi m lo) t -> p hi m lo t", hi=2, m=2, lo=4)[:, :, mid, :, :]
                    src = psb[bt][mid * 64:(mid + 1) * 64, :].rearrange("p (hi lo t) -> p hi lo t", hi=2, lo=4)
                    if eng == "act":
                        S.add("act", lambda e, dst=dst, src=src: e.activation(out=dst, in_=src, func=AF.Copy), reads=[pB[bt]], writes=[bf("QTz")])
                    else:
                        S.add("dve", lambda e, dst=dst, src=src: e.tensor_copy(out=dst, in_=src), reads=[pB[bt]], writes=[bf("QTz")])
                bt2 = nextbank()

                def trk(e, bt2=bt2):
                    i = None
                    for c in range(2):
                        i = e.transpose(out=psb[bt2][:, c * 128:(c + 1) * 128], in_=kn[:, c * 128:(c + 1) * 128], identity=ident_b[:])
                    return i
                S.add("pe", trk, reads=[bf("kn"), bf("identb")], writes=[pB[bt2]])
                S.add("act", lambda e, j=j, bt2=bt2: e.activation(out=KT[:, :, (j + 1) * 128:(j + 2) * 128], in_=psb[bt2][:, 0:256].rearrange("p (g t) -> p g t", g=2),
                                                                   func=AF.Identity, scale=gqkp[:, 0:1]),
                      reads=[pB[bt2], bf("gqkp")], writes=[bf("KT%d" % (j + 1))])
                reserved.difference_update(qkv_banks[j])

            def glu_gen():
                S.add("pool", lambda e: e.tensor_copy(out=hbuf[:, :, 0:32], in_=halo_h[:]), reads=[bf("halo_h")], writes=[bf("hbuf"), bf("hT")] + [bf("hT_%d" % f) for f in range(8)])
                for half in range(2):
                    wa, wab = next_unit()
                    wg, wgb = next_unit()
                    for o4 in range(4):
                        c = half * 4 + o4
                        ba = nextbank()
                        mm_group(ba, 0, T, lambda k, wa=wa, o4=o4: wa[:, k, o4 * 128:(o4 + 1) * 128], lambda k: uT[:, k, :], 8, UT + [wab])
                        bg = nextbank()
                        mm_group(bg, 0, T, lambda k, wg=wg, o4=o4: wg[:, k, o4 * 128:(o4 + 1) * 128], lambda k: uT[:, k, :], 8, UT + [wgb])
                        a = alt("tn")
                        S.add("act", lambda e, bg=bg, a=a: e.activation(out=tnb[a][:], in_=ps[bg][:], func=AF.Tanh, scale=0.5), reads=[pB[bg]], writes=[bf("tn%d" % a)])
                        S.add("dve", lambda e, ba=ba, a=a, c=c: e.scalar_tensor_tensor(out=hbuf[:, c, 32:32 + T], in0=tnb[a][:], scalar=1.0, in1=ps[ba][:], op0=ALU.add, op1=ALU.mult),
                              reads=[bf("tn%d" % a), pB[ba]], writes=[bf("hbuf")])
                        yield
                    release(2)
                S.add("pool", lambda e: e.tensor_copy(out=halo_h[:], in_=hbuf[:, :, T:T + 32]), reads=[bf("hbuf")], writes=[bf("halo_h")])


            def drain(g, n=100):
                for _ in range(n):
                    try:
                        next(g)
                    except StopIteration:
                        return
            qkv_mm(0)
            qkv_mm(1)
            qkv_post(0)
            qkv_mm(2)
            qkv_post(1)
            qkv_mm(3)
            release(3)
            qkv_post(2)
            qkv_post(3, "a")
            gg = glu_gen()
            drain(gg, 3)
            qkv_post(3, "b")
            drain(gg)
            def gated_proj_gen(actT, actbuf, first, held=None, nhold=0):
                nh = [nhold]
                for half in range(2):
                    wp, wpb = next_unit()
                    wg, wgb = next_unit()
                    for o4 in range(4):
                        oc = half * 4 + o4
                        bp = nextbank()
                        mm_group(bp, 0, T, lambda k, wp=wp, o4=o4: wp[:, k, o4 * 128:(o4 + 1) * 128], lambda k: actT[:, k, :], 8, [actbuf, wpb])
                        bg = nextbank()
                        mm_group(bg, 0, T, lambda k, wg=wg, o4=o4: wg[:, k, o4 * 128:(o4 + 1) * 128], lambda k: uT[:, k, :], 8, UT + [wgb])
                        def evac(bp=bp, bg=bg, oc=oc):
                            a = alt("tn")
                            S.add("act", lambda e, bg=bg, a=a: e.activation(out=tnb[a][:], in_=ps[bg][:], func=AF.Tanh, scale=0.5), reads=[pB[bg]], writes=[bf("tn%d" % a)])
                            if first:
                                S.add("dve", lambda e, bp=bp, a=a, oc=oc: e.scalar_tensor_tensor(out=mT[:, oc, :], in0=tnb[a][:], scalar=1.0, in1=ps[bp][:], op0=ALU.add, op1=ALU.mult),
                                      reads=[bf("tn%d" % a), pB[bp]], writes=[bf("mT"), bf("QTz")])
                            else:
                                S.add("dve", lambda e, bp=bp, a=a: e.scalar_tensor_tensor(out=tnb[a][:], in0=tnb[a][:], scalar=1.0, in1=ps[bp][:], op0=ALU.add, op1=ALU.mult),
                                      reads=[bf("tn%d" % a), pB[bp]], writes=[bf("tn%d" % a)])
                                S.add("dve", lambda e, a=a, oc=oc: e.tensor_tensor(out=mT[:, oc, :], in0=tnb[a][:], in1=mT[:, oc, :], op=ALU.add),
                                      reads=[bf("tn%d" % a), bf("mT")], writes=[bf("mT")])
                        if held is not None and nh[0] > 0:
                            nh[0] -= 1
                            held.append(evac)
                        else:
                            evac()
                        yield
                    release(2)

            def drain(g, n=100):
                for _ in range(n):
                    try:
                        next(g)
                    except StopIteration:
                        return

            CB = 4

            def conv_steps():
                for c in range(8):
                    a = c % 2
                    for j0 in range(0, CW, 4):
                        j1 = min(CW, j0 + 4)

                        def cm(e, c=c, a=a, j0=j0, j1=j1):
                            i = None
                            for jt in range(j0, j1):
                                i = e.matmul(ps[CB][:], lhsT=dg[a][:, jt, :], rhs=hbuf[:, c, 2 + jt:2 + jt + T], start=(jt == 0), stop=(jt == CW - 1))
                            return i
                        S.add("pe", cm, reads=[bf("dg%d" % a), bf("hbuf")], writes=[pB[CB]])
                        if j1 == CW:
                            S.add("act", lambda e, c=c: e.activation(out=cv[:, c, :], in_=ps[CB][:], func=AF.Identity, bias=colp[:, c, 2:3]),
                                  reads=[pB[CB]] + CP, writes=[bf("cv%d" % c)])
                            if c + 2 < 8:
                                build_dg(c + 2)
                        yield
            cgen = conv_steps()
            pool_now[0] = (0, 1, 2, 3)

            def conv_step(n=1):
                for _ in range(n):
                    try:
                        next(cgen)
                    except StopIteration:
                        return
            OB = (5, 6, 7)

            def ohead(h):
                return OB[h // 7], (h % 7) * 65
            deferred = []
            for j in range(NB):
                pend = []
                for pr in range(8):
                    if pr == 3 and deferred:
                        deferred.pop(0)()
                    h0 = 2 * pr
                    bk = nextbank()

                    def smm(e, bk=bk, h0=h0, j=j):
                        i = None
                        for hl in range(2):
                            h = h0 + hl
                            gp = (h // 4) // 2
                            for kb in range(2):
                                c = (hl * 2 + kb) * 128
                                i = e.matmul(ps[bk][:, c:c + 128], lhsT=KT[:, gp, (j + kb) * 128:(j + kb + 1) * 128],
                                             rhs=QTz[:, h, j * 128:(j + 1) * 128], start=True, stop=True)
                        return i
                    S.add("pe", smm, reads=[bf("QTz"), bf("KT%d" % j), bf("KT%d" % (j + 1))], writes=[pB[bk]])
                    a = alt("E", 3)
                    S.add("act", lambda e, bk=bk, a=a: e.activation(out=Eb[a][:], in_=ps[bk][:], func=AF.Exp), reads=[pB[bk]], writes=[bf("E%d" % a)])
                    S.add("dve", lambda e, a=a, h0=h0: e.tensor_tensor(out=PTb[a][:], in0=Eb[a][:], in1=EB[:, h0 * 256:(h0 + 2) * 256], op=ALU.mult),
                          reads=[bf("E%d" % a), bf("EB")], writes=[bf("PT%d" % a)])
                    pend.append((h0, a))
                    conv_step(2)
                    if len(pend) == 3 or pr == 7:
                        todo = pend[:1] if pr < 7 else pend
                        pend = pend[1:] if pr < 7 else []
                        for (hh0, aa) in todo:
                            def pv(e, hh0=hh0, aa=aa, j=j):
                                i = None
                                for hl in range(2):
                                    h = hh0 + hl
                                    g = h // 4
                                    ob, off = ohead(h)
                                    for kb in range(2):
                                        c = (hl * 2 + kb) * 128
                                        i = e.matmul(ps[ob][:, off:off + 65], lhsT=PTb[aa][:, c:c + 128], rhs=Vaug[:, j + kb, g * 65:(g + 1) * 65],
                                                     start=(kb == 0), stop=(kb == 1))
                                return i
                            obs = sorted(set(ohead(hh0 + hl)[0] for hl in range(2)))
                            S.add("pe", pv, reads=[bf("PT%d" % aa), bf("V%d" % j), bf("V%d" % (j + 1))], writes=[pB[o] for o in obs])
                for bi, ob in enumerate(OB):
                    hs = bi * 7
                    nh = 7 if bi < 2 else 2
                    ov = ps[ob][:, 0:nh * 65].rearrange("p (h d) -> p h d", d=65)
                    S.add("dve", lambda e, ov=ov, hs=hs, nh=nh: e.tensor_tensor(out=den[:, hs:hs + nh].unsqueeze(2), in0=ov[:, :, 64:65], in1=es[:, hs:hs + nh].unsqueeze(2), op=ALU.add),
                          reads=[pB[ob], bf("es")], writes=[bf("den")])
                    S.add("dve", lambda e, hs=hs, nh=nh: e.reciprocal(out=rden[:, hs:hs + nh], in_=den[:, hs:hs + nh]), reads=[bf("den")], writes=[bf("rden")])
                    S.add("dve", lambda e, ov=ov, hs=hs, nh=nh: e.tensor_tensor(out=attn_n[:, hs * 64:(hs + nh) * 64].rearrange("p (h d) -> p h d", d=HD), in0=ov[:, :, 0:64],
                                                                               in1=rden[:, hs:hs + nh].unsqueeze(2).to_broadcast([128, nh, HD]), op=ALU.mult),
                          reads=[pB[ob], bf("rden")], writes=[bf("attn_n")])
                def finish_blk(j=j):
                    bt = nextbank()

                    def tra(e, bt=bt):
                        i = None
                        for c in range(8):
                            i = e.transpose(out=psb[bt][:, c * 128:(c + 1) * 128], in_=attn_n[:, c * 128:(c + 1) * 128], identity=ident_b[:])
                        return i
                    S.add("pe", tra, reads=[bf("attn_n"), bf("identb")], writes=[pB[bt]])
                    S.add("act", lambda e, j=j, bt=bt: e.activation(out=attnT[:, :, j * 128:(j + 1) * 128], in_=psb[bt][:].rearrange("p (c t) -> p c t", c=8), func=AF.Copy),
                          reads=[pB[bt]], writes=[bf("attnT")])
                deferred.append(finish_blk)
            while deferred:
                deferred.pop(0)()
            conv_step(1000)
            S.add("pool", lambda e: e.tensor_copy(out=KT[:, :, 0:128], in_=KT[:, :, NB * 128:(NB + 1) * 128]), reads=[bf("KT%d" % NB)], writes=[bf("KT0")])
            S.add("pool", lambda e: e.tensor_copy(out=Vaug[:, 0, :], in_=Vaug[:, NB, :]), reads=[bf("V%d" % NB)], writes=[bf("V0")])

            pool_now[0] = (0, 1, 2, 3, 6, 7)
            BM, BQ = 4, 5
            for c in range(8):
                a = alt("cvb")
                S.add("dve", lambda e, c=c, a=a: e.tensor_copy(out=cvb[a][:], in_=cv[:, c, :]), reads=[bf("cv%d" % c)], writes=[bf("cvb%d" % a)])
                S.add("act", lambda e, c=c, a=a: e.activation(out=sqb[a][:], in_=cv[:, c, :], func=AF.Square), reads=[bf("cv%d" % c)], writes=[bf("sqb%d" % a)])
                S.add("pe", lambda e, c=c, a=a: e.matmul(ps[BM][:], lhsT=onesm[:], rhs=cvb[a][:], start=(c == 0), stop=(c == 7)),
                      reads=[bf("cvb%d" % a), bf("onesm")], writes=[pB[BM]])
                S.add("pe", lambda e, c=c, a=a: e.matmul(ps[BQ][:], lhsT=onesm[:], rhs=sqb[a][:], start=(c == 0), stop=(c == 7)),
                      reads=[bf("sqb%d" % a), bf("onesm")], writes=[pB[BQ]])
            held = []
            ga = gated_proj_gen(attnT, bf("attnT"), True, held, 3)
            drain(ga, 3)
            S.add("act", lambda e: e.activation(out=mean_sb[:], in_=ps[BM][:], func=AF.Copy), reads=[pB[BM]], writes=[bf("mean")])
            S.add("dve", lambda e: e.tensor_tensor(out=nmr_sb[:], in0=mean_sb[:], in1=mean_sb[:], op=ALU.mult), reads=[bf("mean")], writes=[bf("nmr")])
            S.add("dve", lambda e: e.tensor_tensor(out=rstd_sb[:], in0=ps[BQ][:], in1=nmr_sb[:], op=ALU.subtract), reads=[pB[BQ], bf("nmr")], writes=[bf("rstd")])
            S.add("dve", lambda e: e.tensor_scalar(out=rstd_sb[:], in0=rstd_sb[:], scalar1=0.0, scalar2=None, op0=ALU.max), reads=[bf("rstd")], writes=[bf("rstd")])
            S.add("act", lambda e: e.activation(out=rstd_sb[:], in_=rstd_sb[:], func=AF.Sqrt, bias=EPS, scale=1.0), reads=[bf("rstd")], writes=[bf("rstd")])
            S.add("dve", lambda e: e.reciprocal(out=rstd_sb[:], in_=rstd_sb[:]), reads=[bf("rstd")], writes=[bf("rstd")])
            S.add("dve", lambda e: e.scalar_tensor_tensor(out=nmr_sb[:], in0=mean_sb[:], scalar=-1.0, in1=rstd_sb[:], op0=ALU.mult, op1=ALU.mult),
                  reads=[bf("mean"), bf("rstd")], writes=[bf("nmr")])
            for ev in held:
                ev()
            ybs = [(ybuf[:], bf("ybuf")), (xnb[0][:].bitcast(F32), bf("xn0"))]
            zbs = [(zbuf[:], bf("zbuf")), (xnb[1][:].bitcast(F32), bf("xn1"))]
            def ln_pre(c):
                yb, ybB = ybs[c % 2]
                S.add("pool", lambda e, c=c, yb=yb: e.tensor_tensor(out=yb, in0=cv[:, c, :], in1=rstd_sb[:], op=ALU.mult), reads=[bf("cv%d" % c), bf("rstd")], writes=[ybB])
                S.add("dve", lambda e, yb=yb: e.tensor_tensor(out=yb, in0=yb, in1=nmr_sb[:], op=ALU.add), reads=[ybB, bf("nmr")], writes=[ybB])
            ln_pre(0)
            for c in range(8):
                yb, ybB = ybs[c % 2]
                zb, zbB = zbs[c % 2]
                if c + 1 < 8:
                    ln_pre(c + 1)
                S.add("act", lambda e, c=c, yb=yb, zb=zb: e.activation(out=zb, in_=yb, func=AF.Identity, scale=colh[:, c, 3:4], bias=colh[:, c, 4:5]),
                      reads=[ybB] + CP, writes=[zbB])
                a = alt("tn")
                S.add("act", lambda e, a=a, zb=zb: e.activation(out=tnb[a][:], in_=zb, func=AF.Tanh), reads=[zbB], writes=[bf("tn%d" % a)])
                S.add("dve", lambda e, a=a, c=c, zb=zb: e.scalar_tensor_tensor(out=sT[:, c, :], in0=tnb[a][:], scalar=1.0, in1=zb, op0=ALU.add, op1=ALU.mult),
                      reads=[bf("tn%d" % a), zbB], writes=[bf("sT"), bf("QTz")])
                drain(ga, 1)
            drain(ga)


            drain(gated_proj_gen(sT, bf("sT"), False))
            pool_now[0] = tuple(range(8))

            wo = [next_unit(), next_unit()]
            for j in range(NB):
                for half in range(2):
                    w, wb = wo[half]
                    bk = nextbank()
                    mm_group(bk, 0, 512, lambda k, j=j: mT[:, k, j * 128:(j + 1) * 128], lambda k, w=w: w[:, k, :], 8, [bf("mT"), wb])
                    S.add("dve", lambda e, j=j, half=half, bk=bk, x_t=x_t: e.scalar_tensor_tensor(out=x_t[:, j, half * 512:(half + 1) * 512], in0=ps[bk][:], scalar=0.5,
                                                                                      in1=x_t[:, j, half * 512:(half + 1) * 512], op0=ALU.mult, op1=ALU.add),
                          reads=[pB[bk], xb[j]], writes=[xb[j]])
                rms_sq(j, x_t, xb)
                if j == 2:
                    rms_head(0, 3)
            release(2)
            if tile + 1 < ntiles:
                load_x(tile + 1)
            if tile + 1 < ntiles:
                S.add("pool", lambda e: e.memset(regA[:], 0.0), reads=[], writes=[bf("QTz"), bf("sT"), bf("mT")])

            rms_finish(1, x_t, xb, blocks=[0, 1, 2], head=False)
            rms_head(3, 4)
            rms_finish(1, x_t, xb, blocks=[3], head=False)

            for f in range(8):
                if f == 4 and tile + 1 < ntiles:
                    rms1_of(tile + 1, "head")
                w, wb = next_unit()
                for o4 in range(4):
                    fc = f * 4 + o4
                    bk = nextbank()
                    mm_group(bk, 0, T, lambda k, w=w, o4=o4: w[:, k, o4 * 128:(o4 + 1) * 128], lambda k: uT[:, k, :], 8, UT + [wb])
                    a = alt("rl")
                    S.add("act", lambda e, bk=bk, a=a: e.activation(out=rl[a][:], in_=ps[bk][:], func=AF.Relu), reads=[pB[bk]], writes=[bf("rl%d" % a)])
                    S.add("pool", lambda e, a=a, fc=fc: e.tensor_tensor(out=hT[:, fc, :], in0=rl[a][:], in1=rl[a][:], op=ALU.mult),
                          reads=[bf("rl%d" % a)], writes=[bf("hT"), bf("hT_%d" % f), bf("hbuf")] + [bf("cv%d" % c) for c in range(8)])
                release(1)

            for j2 in range(2):
                for g in range(4):
                    if j2 == 0 and tile + 1 < ntiles:
                        pool_now[0] = (0, 1, 2, 3)
                        rms1_of(tile + 1, "body", blocks=[g])
                        pool_now[0] = tuple(range(8))
                    w, wb = next_unit()
                    for j in range(NB):
                        bk = (4 if j2 == 0 else 0) + j

                        def f2(e, w=w, g=g, j=j, bk=bk):
                            i = None
                            for k in range(8):
                                i = e.matmul(ps[bk][:], lhsT=hT[:, g * 8 + k, j * 128:(j + 1) * 128], rhs=w[:, k, :], start=(g == 0 and k == 0), stop=(g == 3 and k == 7))
                            return i
                        S.add("pe", f2, reads=[bf("hT_%d" % (2 * g)), bf("hT_%d" % (2 * g + 1)), wb], writes=[pB[bk]])
                    release(1)
                for j in range(NB):
                    bk = (4 if j2 == 0 else 0) + j
                    S.add("dve", lambda e, j=j, j2=j2, bk=bk, x_t=x_t: e.tensor_tensor(out=x_t[:, j, j2 * 512:(j2 + 1) * 512], in0=ps[bk][:], in1=x_t[:, j, j2 * 512:(j2 + 1) * 512], op=ALU.add),
                          reads=[pB[bk], xb[j]], writes=[xb[j]])
            tk = S.add("sp", lambda e, s, t0=t0, x_t=x_t: e.dma_start(out=out_d[t0:t0 + T, :].rearrange("(j p) d -> p j d", p=128), in_=x_t[:]).then_inc(s, 16),
                       reads=xb, dkey="st%d" % (tile % 2))
            out_tokens.append(tk)
        S.add("sp", None, extra=out_tokens[-2:])
        S.emit(nc, st)
    return nc


def t5_bucket(dist):
    n = np.maximum(dist, 0)
    max_exact = 16
    nf = np.maximum(n, 1).astype(np.float32)
    large = max_exact + (np.log(nf / np.float32(max_exact)) / np.float32(np.log(128 / max_exact)) * np.float32(32 - max_exact)).astype(np.int32)
    large = np.minimum(large, 31)
    return np.where(n < max_exact, n, large)


def host_layout(inputs, ntiles=SEQ // T, ncores=8):
    f = lambda a: np.ascontiguousarray(np.asarray(a, dtype=np.float32))
    x = f(inputs["x"])
    rowp = np.concatenate([f(inputs["norm_mix_g"]), f(inputs["norm_mlp_g"]), f(inputs["b_dw"]), f(inputs["conv_ln_g"]),
                           f(inputs["conv_ln_b"]), f(inputs["w_dw"])[0]], axis=0)
    gq = f(inputs["q_norm_g"])[0]
    gk = f(inputs["k_norm_g"])[0]
    gqk = np.stack([np.tile(gq, 2), np.tile(gk, 2)], axis=1)
    gqk = np.ascontiguousarray(gqk)
    sinks = np.ascontiguousarray(np.broadcast_to(f(inputs["attn_sinks"])[0][None, :], (128, NQ)))
    rb = f(inputs["rel_bias"])
    k = np.arange(128)[:, None]
    q = np.arange(128)[None, :]
    biasT = np.empty((128, NQ, 2, 128), np.float32)
    for kb in range(2):
        dist = q + 128 - (kb * 128 + k)
        valid = (dist >= 0) & (dist < 128)
        g = rb[t5_bucket(dist)]
        g = np.where(valid[:, :, None], g, np.float32(-1e30))
        biasT[:, :, kb, :] = np.transpose(g, (0, 2, 1))
    biasT = np.ascontiguousarray(biasT.reshape(128, NQ * 256))
    common = {
        "w_in": f(inputs["w_in"])[0], "w_attn_o": f(inputs["w_attn_o"])[0], "w_conv_out": f(inputs["w_conv_out"])[0],
        "w_out": f(inputs["w_out"])[0], "w_ff1": f(inputs["w_ff1"])[0], "w_ff2": f(inputs["w_ff2"])[0],
        "rowp": np.ascontiguousarray(rowp), "gqk": gqk, "sinks": sinks, "biasT": biasT,
        "ident": np.eye(128, dtype=np.float32),
    }
    maps = []
    for c in range(ncores):
        m = dict(common)
        m["x"] = np.ascontiguousarray(x[c, :ntiles * T])
        maps.append(m)
    return maps


def kernel(**inputs):
    ntiles = SEQ // T
    nc = build(ntiles)
    maps = host_layout(inputs, ntiles, 8)
    res = run_bass_kernel_spmd(nc, maps, core_ids=list(range(8)))
    return np.stack([r["out"] for r in res.results], axis=0).astype(np.float32)
```

```python
import numpy as np
from contextlib import ExitStack
import concourse.bass as bass
import concourse.mybir as mybir
from concourse.bass_utils import run_bass_kernel_spmd

F32 = mybir.dt.float32
BF16 = mybir.dt.bfloat16
AF = mybir.ActivationFunctionType
ALU = mybir.AluOpType
AX = mybir.AxisListType

D = 1024
SEQ = 8192
NB = 4
T = NB * 128
HD = 64
NQ = 16
NKV = 4
CW = 31
DFF = 4096
EPS = 1e-6
V_END = 1536
GLU_END = V_END + 2048
IN_W = GLU_END + 2048
NCOLP = 5 + CW
NSLOT = 4
ENGS = ("pe", "act", "dve", "pool", "sp")


class Buf:
    __slots__ = ("name", "w", "r")

    def __init__(self, name):
        self.name = name
        self.w = None
        self.r = []


class Sched:
    def __init__(self):
        self.ops = {e: [] for e in ENGS}
        self.cnt = {}
        self.seen = {e: {} for e in ENGS}
        self.keys = []

    def _bump(self, key, n):
        if key not in self.cnt:
            self.cnt[key] = 0
            self.keys.append(key)
        self.cnt[key] += n
        return (key, self.cnt[key])

    def add(self, eng, fn, reads=(), writes=(), dkey=None, ndma=1, extra=()):
        deps = {}

        def need(t):
            if t is None:
                return
            k, v = t
            if eng == "pe" and k == "pe":
                return
            if deps.get(k, 0) < v:
                deps[k] = v
        for b in reads:
            need(b.w)
        for b in writes:
            need(b.w)
            for t in b.r:
                need(t)
        for t in extra:
            need(t)
        waits = []
        seen = self.seen[eng]
        for k, v in deps.items():
            if seen.get(k, 0) >= v:
                continue
            seen[k] = v
            waits.append((k, v))
        if fn is None:
            tok = None
        elif dkey is not None:
            tok = self._bump(dkey, 16 * ndma)
        else:
            tok = self._bump(eng, 1)
        self.ops[eng].append((waits, fn, tok, dkey is not None))
        if tok is not None:
            for b in reads:
                b.r.append(tok)
            for b in writes:
                b.w = tok
                b.r = []
        return tok

    def emit(self, nc, stack):
        sems = {}
        for i, k in enumerate(self.keys):
            sems[k] = stack.enter_context(nc.semaphore("s%d" % i))
        block = stack.enter_context(nc.Block())
        handles = {"pe": block.tensor, "act": block.scalar, "dve": block.vector,
                   "pool": block.gpsimd, "sp": block.sync}
        for eng in ENGS:
            ops = self.ops[eng]
            if not ops:
                continue

            def body(e, ops=ops):
                for waits, fn, tok, is_dma in ops:
                    for k, v in waits:
                        e.wait_ge(sems[k], v)
                    if fn is None:
                        continue
                    if is_dma:
                        fn(e, sems[tok[0]])
                    else:
                        fn(e).then_inc(sems[tok[0]], 1)
            handles[eng](body)


def build(ntiles):
    seq = ntiles * T
    nc = bass.Bass("TRN2", target_bir_lowering=False)
    x_d = nc.dram_tensor("x", [seq, D], F32, kind="ExternalInput").ap()
    w_in_d = nc.dram_tensor("w_in", [D, IN_W], F32, kind="ExternalInput").ap()
    w_ao_d = nc.dram_tensor("w_attn_o", [D, D], F32, kind="ExternalInput").ap()
    w_co_d = nc.dram_tensor("w_conv_out", [D, D], F32, kind="ExternalInput").ap()
    w_out_d = nc.dram_tensor("w_out", [D, D], F32, kind="ExternalInput").ap()
    w_ff1_d = nc.dram_tensor("w_ff1", [D, DFF], F32, kind="ExternalInput").ap()
    w_ff2_d = nc.dram_tensor("w_ff2", [DFF, D], F32, kind="ExternalInput").ap()
    rowp_d = nc.dram_tensor("rowp", [NCOLP, D], F32, kind="ExternalInput").ap()
    gqk_d = nc.dram_tensor("gqk", [128, 2], F32, kind="ExternalInput").ap()
    sink_d = nc.dram_tensor("sinks", [128, NQ], F32, kind="ExternalInput").ap()
    bias_d = nc.dram_tensor("biasT", [128, NQ * 2 * 128], F32, kind="ExternalInput").ap()
    ident_d = nc.dram_tensor("ident", [128, 128], F32, kind="ExternalInput").ap()
    out_d = nc.dram_tensor("out", [seq, D], F32, kind="ExternalOutput").ap()
    wscr = nc.dram_tensor("wscr", [33, 128, 8 * 512], BF16).ap()

    S = Sched()
    with ExitStack() as st:
        def sb(name, shape, dt):
            return st.enter_context(nc.sbuf_tensor(name, shape, dt))

        x_tb = [sb("x_t%d" % i, [128, NB, D], F32) for i in range(2)]
        uT = sb("uT", [128, 8, T], BF16)
        xnb = [sb("xn%d" % i, [128, D], BF16) for i in range(2)]
        sq = sb("sq", [128, 512], F32)
        junk = sq[:].bitcast(BF16)
        gbc = [sb("gbc%d" % i, [128, D], F32) for i in range(2)]
        qn = sb("qn", [128, D], BF16)
        kn = sb("kn", [128, 256], BF16)
        regA = sb("regA", [128, NQ * T], BF16)
        QTz = regA[:].rearrange("p (h t) -> p h t", h=NQ)
        sT = regA[:, 0:8 * T].rearrange("p (c t) -> p c t", c=8)
        mT = regA[:, 8 * T:16 * T].rearrange("p (c t) -> p c t", c=8)
        KT = sb("KT", [128, 2, (NB + 1) * 128], BF16)
        Vaug = sb("Vaug", [128, NB + 1, NKV * 65], BF16)
        Eb = [sb("E%d" % i, [128, 512], BF16) for i in range(3)]
        PTb = [sb("PT%d" % i, [128, 512], BF16) for i in range(3)]
        EB = sb("EB", [128, NQ * 256], BF16)
        attn_n = sb("attn_n", [128, D], BF16)
        attnT = sb("attnT", [128, 8, T], BF16)
        tnb = [sb("tn%d" % i, [128, 512], F32) for i in range(2)]
        regB = sb("regB", [128, 16 * 1024], BF16)
        hT = regB[:, 0:32 * T].rearrange("p (c t) -> p c t", c=32)
        regBf = regB[:].bitcast(F32)
        hbuf = regB[:, 0:8 * (T + 32)].rearrange("p (c t) -> p c t", c=8)
        cv = regBf[:, 4 * (T + 32):4 * (T + 32) + 8 * T].rearrange("p (c t) -> p c t", c=8)
        dg = [sb("dg%d" % i, [128, CW, 128], BF16) for i in range(2)]
        cvb = [sb("cvb%d" % i, [128, T], BF16) for i in range(2)]
        sqb = [sb("sqb%d" % i, [128, T], BF16) for i in range(2)]
        mean_sb = sb("mean_sb", [128, T], F32)
        rstd_sb = sb("rstd_sb", [128, T], F32)
        nmr_sb = sb("nmr_sb", [128, T], F32)
        ybuf = sb("ybuf", [128, T], F32)
        zbuf = sb("zbuf", [128, T], F32)
        rl = [sb("rl%d" % i, [128, T], BF16) for i in range(2)]
        wsl = [sb("wsl%d" % i, [128, 8, 512], BF16) for i in range(NSLOT)]
        ident_f = sb("ident_f", [128, 128], F32)
        ident_b = sb("ident_b", [128, 128], BF16)
        onesm = sb("onesm", [128, 128], BF16)
        rowp = regBf[0:NCOLP, 4096:4096 + D]
        colp = sb("colp", [128, 8, NCOLP], F32)
        colh = sb("colh", [128, 8, NCOLP], F32)
        gqk = sb("gqk_sb", [128, 2], F32)
        gqkp = sb("gqkp", [128, 1], F32)
        es = sb("es", [128, NQ], F32)
        bias_sb = regBf[:, 0:1024]
        ss1 = sb("ss1", [128, NB], F32)
        r1 = sb("r1", [128, NB], F32)
        ssq = sb("ssq", [128, 20], F32)
        rqk = sb("rqk", [128, 20], F32)
        den = sb("den", [128, NQ], F32)
        rden = sb("rden", [128, NQ], F32)
        halo_h = sb("halo_h", [128, 8, 32], BF16)
        ps = [st.enter_context(nc.psum_tensor("ps%d" % i, [128, 512], F32)) for i in range(8)]
        psb = [p[:].bitcast(BF16) for p in ps]

        B = {}

        def bf(name):
            if name not in B:
                B[name] = Buf(name)
            return B[name]
        pB = [bf("ps%d" % i) for i in range(8)]
        rot = [0]

        pool_now = [tuple(range(8))]

        reserved = set()

        def nextbank():
            p = pool_now[0]
            while True:
                i = p[rot[0] % len(p)]
                rot[0] += 1
                if i not in reserved:
                    return i
        tgl = {}

        def alt(name, n=2):
            tgl[name] = tgl.get(name, -1) + 1
            return tgl[name] % n

        S.add("sp", lambda e, s: e.dma_start(out=ident_f[:], in_=ident_d).then_inc(s, 16), writes=[bf("identf")], dkey="c0")
        S.add("sp", lambda e, s: e.dma_start(out=rowp, in_=rowp_d).then_inc(s, 16), writes=[bf("rowp"), bf("hT")], dkey="c1")
        S.add("sp", lambda e, s: e.dma_start(out=gqk[:], in_=gqk_d).then_inc(s, 16), writes=[bf("gqk")], dkey="c2")
        S.add("sp", lambda e, s: e.dma_start(out=es[:], in_=sink_d).then_inc(s, 16), writes=[bf("es")], dkey="c3")
        for gi in range(2):
            S.add("sp", lambda e, s, gi=gi: e.dma_start(out=gbc[gi][:], in_=rowp_d[gi:gi + 1, :].partition_broadcast(128)).then_inc(s, 16),
                  writes=[bf("gbc")], dkey="c5%d" % gi)
        S.add("dve", lambda e: e.tensor_copy(out=ident_b[:], in_=ident_f[:]), reads=[bf("identf")], writes=[bf("identb")])
        S.add("dve", lambda e: e.memset(onesm[:], 1.0 / D), writes=[bf("onesm")])
        S.add("dve", lambda e: e.memset(regA[:], 0.0), writes=[bf("QTz"), bf("sT"), bf("mT")])
        S.add("dve", lambda e: e.memset(KT[:], 0.0), writes=[bf("KT%d" % i) for i in range(NB + 1)])
        S.add("dve", lambda e: e.memset(Vaug[:, 0, :], 0.0), writes=[bf("V0")])
        S.add("dve", lambda e: e.memset(Vaug[:, 1:NB + 1, :], 1.0), writes=[bf("V%d" % i) for i in range(1, NB + 1)])
        S.add("dve", lambda e: e.memset(halo_h[:], 0.0), writes=[bf("halo_h")])
        S.add("dve", lambda e: e.scalar_tensor_tensor(out=gqkp[:], in0=gqk[:, 0:1], scalar=HD ** -0.5, in1=gqk[:, 1:2], op0=ALU.mult, op1=ALU.mult), reads=[bf("gqk")], writes=[bf("gqkp")])
        S.add("act", lambda e: e.activation(out=es[:], in_=es[:], func=AF.Exp), reads=[bf("es")], writes=[bf("es")])
        for hg in range(4):
            S.add("sp", lambda e, s, hg=hg: e.dma_start(out=bias_sb[:], in_=bias_d[:, hg * 1024:(hg + 1) * 1024]).then_inc(s, 16),
                  writes=[bf("bias_sb"), bf("hT")], dkey="c4")
            S.add("act", lambda e, hg=hg: e.activation(out=EB[:, hg * 1024:(hg + 1) * 1024], in_=bias_sb[:], func=AF.Exp),
                  reads=[bf("bias_sb"), bf("hT")], writes=[bf("EB")])
        for kc in range(8):
            bk = nextbank()
            S.add("pe", lambda e, kc=kc, bk=bk: e.transpose(out=ps[bk][:, 0:NCOLP], in_=rowp[:, kc * 128:(kc + 1) * 128],
                                                        identity=ident_f[0:NCOLP, 0:NCOLP]),
                  reads=[bf("rowp"), bf("identf"), bf("hT")], writes=[pB[bk]])
            S.add("dve", lambda e, kc=kc, bk=bk: e.tensor_copy(out=colp[:, kc, :], in_=ps[bk][:, 0:NCOLP]),
                  reads=[pB[bk]], writes=[bf("colp")])
        S.add("dve", lambda e: e.tensor_scalar(out=colh[:], in0=colp[:], scalar1=0.5, scalar2=None, op0=ALU.mult),
              reads=[bf("colp")], writes=[bf("colh")])
        CP = [bf("colp"), bf("colh")]

        def unit_srcs():
            def colpanel(w, c0):
                return w.rearrange("(kc p) n -> p kc n", p=128)[:, :, c0:c0 + 512]
            u = []
            for c0 in (0, 512, 1024):
                u.append(colpanel(w_in_d, c0))
            for j in range(2):
                u.append(colpanel(w_in_d, V_END + j * 512))
                u.append(colpanel(w_in_d, V_END + 1024 + j * 512))
            for j in range(2):
                u.append(colpanel(w_ao_d, j * 512))
                u.append(colpanel(w_in_d, GLU_END + j * 512))
            for j in range(2):
                u.append(colpanel(w_co_d, j * 512))
                u.append(colpanel(w_in_d, GLU_END + 1024 + j * 512))
            for j in range(2):
                u.append(colpanel(w_out_d, j * 512))
            for f in range(8):
                u.append(colpanel(w_ff1_d, f * 512))
            for j2 in range(2):
                for g in range(4):
                    u.append(w_ff2_d[g * 1024:(g + 1) * 1024, j2 * 512:(j2 + 1) * 512].rearrange("(kc p) n -> p kc n", p=128))
            return u
        USRC = unit_srcs()
        NU = len(USRC)
        assert NU == 33
        total_units = ntiles * NU
        issued = [0]
        taken = [0]
        released = [0]

        def issue_fetch():
            n = issued[0]
            if n >= total_units:
                return
            issued[0] += 1
            tile, u = divmod(n, NU)
            s = n % NSLOT
            wb = bf("wsl%d" % s)
            if tile == 0:
                S.add("pool", lambda e, sem, u=u, s=s: e.dma_start(out=wsl[s][:], in_=USRC[u]).then_inc(sem, 16),
                      writes=[wb], dkey="wq%d" % s)
                if ntiles > 1:
                    S.add("sp", lambda e, sem, u=u, s=s: e.dma_start(out=wscr[u], in_=wsl[s][:].rearrange("p k n -> p (k n)")).then_inc(sem, 16),
                          reads=[wb], writes=[bf("scr%d" % u)], dkey="ws%d" % s)
            else:
                S.add("sp", lambda e, sem, u=u, s=s: e.dma_start(out=wsl[s][:].rearrange("p k n -> p (k n)"), in_=wscr[u]).then_inc(sem, 16),
                      reads=[bf("scr%d" % u)], writes=[wb], dkey="wl%d" % s)

        def topup():
            while issued[0] < min(total_units, released[0] + NSLOT):
                issue_fetch()

        def next_unit():
            topup()
            assert taken[0] < issued[0]
            s = taken[0] % NSLOT
            taken[0] += 1
            return wsl[s], bf("wsl%d" % s)

        def release(n):
            released[0] += n
            topup()

        def mm_group(bank, col0, ncol, lhs_fn, rhs_fn, nk, reads, extra_w=()):
            def fn(e):
                i = None
                for k in range(nk):
                    i = e.matmul(ps[bank][:, col0:col0 + ncol], lhsT=lhs_fn(k), rhs=rhs_fn(k), start=(k == 0), stop=(k == nk - 1))
                return i
            return S.add("pe", fn, reads=reads, writes=[pB[bank]] + list(extra_w))

        UT = [bf("uT%d" % j) for j in range(NB)]

        def rms_sq(j, x_t, xbufs):
            S.add("act", lambda e, j=j: e.activation(out=junk, in_=x_t[:, j, :], func=AF.Square, accum_out=ss1[:, j:j + 1]),
                  reads=[xbufs[j]], writes=[bf("sq"), bf("ss1_%d" % j)])

        def rms_head(c0=0, c1=NB):
            rb = [bf("r1_%d" % j) for j in range(c0, c1)]
            S.add("act", lambda e: e.activation(out=r1[:, c0:c1], in_=ss1[:, c0:c1], func=AF.Sqrt, bias=EPS, scale=1.0 / D),
                  reads=[bf("ss1_%d" % j) for j in range(c0, c1)], writes=rb)
            S.add("dve", lambda e: e.reciprocal(out=r1[:, c0:c1], in_=r1[:, c0:c1]), reads=rb, writes=rb)

        def rms_finish(gi, x_t, xbufs, blocks=range(NB), head=True):
            if head:
                rms_head()
            for j in blocks:
                a = alt("xn")
                S.add("dve", lambda e, j=j, a=a: e.scalar_tensor_tensor(out=xnb[a][:], in0=x_t[:, j, :], scalar=r1[:, j:j + 1], in1=gbc[gi][:], op0=ALU.mult, op1=ALU.mult),
                      reads=[xbufs[j], bf("r1_%d" % j), bf("gbc")], writes=[bf("xn%d" % a)])
                bk = nextbank()

                def tr(e, bk=bk, a=a):
                    i = None
                    for kc in range(8):
                        i = e.transpose(out=psb[bk][:, kc * 128:(kc + 1) * 128], in_=xnb[a][:, kc * 128:(kc + 1) * 128], identity=ident_b[:])
                    return i
                S.add("pe", tr, reads=[bf("xn%d" % a), bf("identb")], writes=[pB[bk]])
                S.add("act", lambda e, j=j, bk=bk: e.activation(out=uT[:, :, j * 128:(j + 1) * 128], in_=psb[bk][:].rearrange("p (c t) -> p c t", c=8), func=AF.Copy),
                      reads=[pB[bk]], writes=[UT[j]])

        out_tokens = []
        for tile in range(ntiles):
            t0 = tile * T
            x_t = x_tb[tile % 2]
            xb = [bf("x%d_%d" % (tile % 2, j)) for j in range(NB)]

            def load_x(tl):
                xt2 = x_tb[tl % 2]
                xb2 = [bf("x%d_%d" % (tl % 2, j)) for j in range(NB)]
                tt0 = tl * T
                S.add("sp", lambda e, s: e.dma_start(out=xt2[:], in_=x_d[tt0:tt0 + T, :].rearrange("(j p) d -> p j d", p=128)).then_inc(s, 16),
                      writes=xb2, dkey="xl%d" % (tl % 2))

            def rms1_of(tl, part="all", blocks=range(NB)):
                xt2 = x_tb[tl % 2]
                xb2 = [bf("x%d_%d" % (tl % 2, j)) for j in range(NB)]
                if part in ("all", "head"):
                    for j in range(NB):
                        rms_sq(j, xt2, xb2)
                if part == "all":
                    rms_finish(0, xt2, xb2)
                elif part == "head":
                    rms_head()
                else:
                    rms_finish(0, xt2, xb2, blocks=blocks, head=False)
            if tile == 0:
                load_x(0)
                rms1_of(0)

            def build_dg(c):
                a = c % 2
                S.add("pool", lambda e, a=a, c=c: e.tensor_tensor(out=dg[a][:], in0=ident_b[:].unsqueeze(1).to_broadcast([128, CW, 128]),
                                                                   in1=colh[:, c, 5:5 + CW].unsqueeze(2).to_broadcast([128, CW, 128]), op=ALU.mult),
                      reads=[bf("identb")] + CP, writes=[bf("dg%d" % a)])

            build_dg(0)
            build_dg(1)
            rot[0] = 4
            wq0, wq0b = next_unit()
            wq1, wq1b = next_unit()
            wkv, wkvb = next_unit()
            qkv_banks = {}

            def qkv_mm(j):
                banks = []
                for (w, wb) in ((wq0, wq0b), (wq1, wq1b), (wkv, wkvb)):
                    bk = nextbank()
                    banks.append(bk)
                    mm_group(bk, 0, 512, lambda k, j=j: uT[:, k, j * 128:(j + 1) * 128], lambda k, w=w: w[:, k, :], 8, [UT[j], wb])
                qkv_banks[j] = banks
                reserved.update(banks)

            def qkv_post(j, part="ab"):
                bq0, bq1, bkv = qkv_banks[j]
                if "a" in part:
                    qkv_post_a(j, bq0, bq1, bkv)
                if "b" in part:
                    qkv_post_b(j, bq0, bq1, bkv)

            def qkv_post_a(j, bq0, bq1, bkv):
                for bi, bk in enumerate((bq0, bq1)):
                    S.add("act", lambda e, bk=bk: e.activation(out=sq[:], in_=ps[bk][:], func=AF.Square), reads=[pB[bk]], writes=[bf("sq")])
                    S.add("dve", lambda e, bi=bi: e.tensor_reduce(out=ssq[:, bi * 8:(bi + 1) * 8], in_=sq[:].rearrange("p (h d) -> p h d", d=HD), axis=AX.X, op=ALU.add),
                          reads=[bf("sq")], writes=[bf("ssq")])
                S.add("act", lambda e, bk=bkv: e.activation(out=sq[:, 0:256], in_=ps[bk][:, 0:256], func=AF.Square), reads=[pB[bkv]], writes=[bf("sq")])
                S.add("dve", lambda e: e.tensor_reduce(out=ssq[:, 16:20], in_=sq[:, 0:256].rearrange("p (h d) -> p h d", d=HD), axis=AX.X, op=ALU.add),
                      reads=[bf("sq")], writes=[bf("ssq")])
                S.add("act", lambda e: e.activation(out=rqk[:], in_=ssq[:], func=AF.Sqrt, bias=EPS, scale=1.0 / HD), reads=[bf("ssq")], writes=[bf("rqk")])
                S.add("dve", lambda e: e.reciprocal(out=rqk[:], in_=rqk[:]), reads=[bf("rqk")], writes=[bf("rqk")])
                for hi, bk in enumerate((bq0, bq1)):
                    S.add("dve", lambda e, hi=hi, bk=bk: e.tensor_tensor(
                        out=qn[:, hi * 512:(hi + 1) * 512].rearrange("p (lo mid d) -> p mid lo d", lo=4, mid=2, d=HD),
                        in0=ps[bk][:].rearrange("p (mid lo d) -> p mid lo d", mid=2, lo=4, d=HD),
                        in1=rqk[:, hi * 8:(hi + 1) * 8].rearrange("p (mid lo) -> p mid lo", mid=2).unsqueeze(3).to_broadcast([128, 2, 4, HD]),
                        op=ALU.mult), reads=[pB[bk], bf("rqk")], writes=[bf("qn")])
                S.add("dve", lambda e, bk=bkv: e.tensor_tensor(
                    out=kn[:].rearrange("p (h d) -> p h d", d=HD), in0=ps[bk][:, 0:256].rearrange("p (h d) -> p h d", d=HD),
                    in1=rqk[:, 16:20].unsqueeze(2).to_broadcast([128, 4, HD]), op=ALU.mult), reads=[pB[bkv], bf("rqk")], writes=[bf("kn")])
                S.add("act", lambda e, j=j, bk=bkv: e.activation(out=Vaug[:, j + 1, :].rearrange("p (h d) -> p h d", d=65)[:, :, 0:64],
                                                                 in_=ps[bk][:, 256:512].rearrange("p (h d) -> p h d", d=HD), func=AF.Copy),
                      reads=[pB[bkv]], writes=[bf("V%d" % (j + 1))])

            def qkv_post_b(j, bq0, bq1, bkv):
                bt = nextbank()

                def trq(e, bt=bt):
                    i = None
                    for c in range(8):
                        i = e.transpose(out=psb[bt][:, c * 128:(c + 1) * 128], in_=qn[:, c * 128:(c + 1) * 128], identity=ident_b[:])
                    return i
                S.add("pe", trq, reads=[bf("qn"), bf("identb")], writes=[pB[bt]])
                for mid in range(2):
                    eng = "act" if mid == 0 else "dve"
                    dst = QTz[mid * 64:(mid + 1) * 64, :, j * 128:(j + 1) * 128].rearrange("p (hi m lo) t -> p hi m lo t", hi=2, m=2, lo=4)[:, :, mid, :, :]
                    src = psb[bt][mid * 64:(mid + 1) * 64, :].rearrange("p (hi lo t) -> p hi lo t", hi=2, lo=4)
                    if eng == "act":
                        S.add("act", lambda e, dst=dst, src=src: e.activation(out=dst, in_=src, func=AF.Copy), reads=[pB[bt]], writes=[bf("QTz")])
                    else:
                        S.add("dve", lambda e, dst=dst, src=src: e.tensor_copy(out=dst, in_=src), reads=[pB[bt]], writes=[bf("QTz")])
                bt2 = nextbank()

                def trk(e, bt2=bt2):
                    i = None
                    for c in range(2):
                        i = e.transpose(out=psb[bt2][:, c * 128:(c + 1) * 128], in_=kn[:, c * 128:(c + 1) * 128], identity=ident_b[:])
                    return i
                S.add("pe", trk, reads=[bf("kn"), bf("identb")], writes=[pB[bt2]])
                S.add("act", lambda e, j=j, bt2=bt2: e.activation(out=KT[:, :, (j + 1) * 128:(j + 2) * 128], in_=psb[bt2][:, 0:256].rearrange("p (g t) -> p g t", g=2),
                                                                   func=AF.Identity, scale=gqkp[:, 0:1]),
                      reads=[pB[bt2], bf("gqkp")], writes=[bf("KT%d" % (j + 1))])
                reserved.difference_update(qkv_banks[j])

            def glu_gen():
                S.add("pool", lambda e: e.tensor_copy(out=hbuf[:, :, 0:32], in_=halo_h[:]), reads=[bf("halo_h")], writes=[bf("hbuf%d" % c_) for c_ in range(8)] + [bf("hT")] + [bf("hT_%d" % f) for f in range(8)])
                for half in range(2):
                    wa, wab = next_unit()
                    wg, wgb = next_unit()
                    for o4 in range(4):
                        c = half * 4 + o4
                        ba = nextbank()
                        mm_group(ba, 0, T, lambda k, wa=wa, o4=o4: wa[:, k, o4 * 128:(o4 + 1) * 128], lambda k: uT[:, k, :], 8, UT + [wab])
                        bg = nextbank()
                        mm_group(bg, 0, T, lambda k, wg=wg, o4=o4: wg[:, k, o4 * 128:(o4 + 1) * 128], lambda k: uT[:, k, :], 8, UT + [wgb])
                        a = alt("tn")
                        S.add("act", lambda e, bg=bg, a=a: e.activation(out=tnb[a][:], in_=ps[bg][:], func=AF.Tanh, scale=0.5), reads=[pB[bg]], writes=[bf("tn%d" % a)])
                        S.add("dve", lambda e, ba=ba, a=a, c=c: e.scalar_tensor_tensor(out=hbuf[:, c, 32:32 + T], in0=tnb[a][:], scalar=1.0, in1=ps[ba][:], op0=ALU.add, op1=ALU.mult),
                              reads=[bf("tn%d" % a), pB[ba]], writes=[bf("hbuf%d" % c)])
                        yield
                    release(2)
                S.add("pool", lambda e: e.tensor_copy(out=halo_h[:], in_=hbuf[:, :, T:T + 32]), reads=[bf("hbuf%d" % c_) for c_ in range(8)], writes=[bf("halo_h")])


            def drain(g, n=100):
                for _ in range(n):
                    try:
                        next(g)
                    except StopIteration:
                        return
            qkv_mm(0)
            qkv_mm(1)
            qkv_post(0)
            qkv_mm(2)
            qkv_post(1)
            qkv_mm(3)
            release(3)
            qkv_post(2)
            qkv_post(3, "a")
            gg = glu_gen()
            drain(gg, 3)
            qkv_post(3, "b")
            drain(gg)
            def gated_proj_gen(actT, actbuf, first, held=None, nhold=0):
                nh = [nhold]
                for half in range(2):
                    wp, wpb = next_unit()
                    wg, wgb = next_unit()
                    for o4 in range(4):
                        oc = half * 4 + o4
                        bp = nextbank()
                        mm_group(bp, 0, T, lambda k, wp=wp, o4=o4: wp[:, k, o4 * 128:(o4 + 1) * 128], lambda k: actT[:, k, :], 8, [actbuf, wpb])
                        bg = nextbank()
                        mm_group(bg, 0, T, lambda k, wg=wg, o4=o4: wg[:, k, o4 * 128:(o4 + 1) * 128], lambda k: uT[:, k, :], 8, UT + [wgb])
                        def evac(bp=bp, bg=bg, oc=oc):
                            a = alt("tn")
                            S.add("act", lambda e, bg=bg, a=a: e.activation(out=tnb[a][:], in_=ps[bg][:], func=AF.Tanh, scale=0.5), reads=[pB[bg]], writes=[bf("tn%d" % a)])
                            if first:
                                S.add("dve", lambda e, bp=bp, a=a, oc=oc: e.scalar_tensor_tensor(out=mT[:, oc, :], in0=tnb[a][:], scalar=1.0, in1=ps[bp][:], op0=ALU.add, op1=ALU.mult),
                                      reads=[bf("tn%d" % a), pB[bp]], writes=[bf("mT"), bf("QTz")])
                            else:
                                S.add("dve", lambda e, bp=bp, a=a: e.scalar_tensor_tensor(out=tnb[a][:], in0=tnb[a][:], scalar=1.0, in1=ps[bp][:], op0=ALU.add, op1=ALU.mult),
                                      reads=[bf("tn%d" % a), pB[bp]], writes=[bf("tn%d" % a)])
                                S.add("dve", lambda e, a=a, oc=oc: e.tensor_tensor(out=mT[:, oc, :], in0=tnb[a][:], in1=mT[:, oc, :], op=ALU.add),
                                      reads=[bf("tn%d" % a), bf("mT")], writes=[bf("mT")])
                        if held is not None and nh[0] > 0:
                            nh[0] -= 1
                            held.append(evac)
                        else:
                            evac()
                        yield
                    release(2)

            def drain(g, n=100):
                for _ in range(n):
                    try:
                        next(g)
                    except StopIteration:
                        return

            CB = 4

            def conv_steps():
                for c in range(8):
                    a = c % 2
                    for j0 in range(0, CW, 4):
                        j1 = min(CW, j0 + 4)

                        def cm(e, c=c, a=a, j0=j0, j1=j1):
                            i = None
                            for jt in range(j0, j1):
                                i = e.matmul(ps[CB][:], lhsT=dg[a][:, jt, :], rhs=hbuf[:, c, 2 + jt:2 + jt + T], start=(jt == 0), stop=(jt == CW - 1))
                            return i
                        S.add("pe", cm, reads=[bf("dg%d" % a), bf("hbuf%d" % c)], writes=[pB[CB]])
                        if j1 == CW:
                            S.add("act", lambda e, c=c: e.activation(out=cv[:, c, :], in_=ps[CB][:], func=AF.Identity, bias=colp[:, c, 2:3]),
                                  reads=[pB[CB]] + CP, writes=[bf("cv%d" % c)])
                            if c + 2 < 8:
                                build_dg(c + 2)
                        yield
            cgen = conv_steps()
            pool_now[0] = (0, 1, 2, 3)

            def conv_step(n=1):
                for _ in range(n):
                    try:
                        next(cgen)
                    except StopIteration:
                        return
            OB = (5, 6, 7)

            def ohead(h):
                return OB[h // 7], (h % 7) * 65
            deferred = []
            for j in range(NB):
                pend = []
                for pr in range(8):
                    if pr == 3 and deferred:
                        deferred.pop(0)()
                    h0 = 2 * pr
                    bk = nextbank()

                    def smm(e, bk=bk, h0=h0, j=j):
                        i = None
                        for hl in range(2):
                            h = h0 + hl
                            gp = (h // 4) // 2
                            for kb in range(2):
                                c = (hl * 2 + kb) * 128
                                i = e.matmul(ps[bk][:, c:c + 128], lhsT=KT[:, gp, (j + kb) * 128:(j + kb + 1) * 128],
                                             rhs=QTz[:, h, j * 128:(j + 1) * 128], start=True, stop=True)
                        return i
                    S.add("pe", smm, reads=[bf("QTz"), bf("KT%d" % j), bf("KT%d" % (j + 1))], writes=[pB[bk]])
                    a = alt("E", 3)
                    S.add("act", lambda e, bk=bk, a=a: e.activation(out=Eb[a][:], in_=ps[bk][:], func=AF.Exp), reads=[pB[bk]], writes=[bf("E%d" % a)])
                    S.add("dve", lambda e, a=a, h0=h0: e.tensor_tensor(out=PTb[a][:], in0=Eb[a][:], in1=EB[:, h0 * 256:(h0 + 2) * 256], op=ALU.mult),
                          reads=[bf("E%d" % a), bf("EB")], writes=[bf("PT%d" % a)])
                    pend.append((h0, a))
                    conv_step(2)
                    if len(pend) == 3 or pr == 7:
                        todo = pend[:1] if pr < 7 else pend
                        pend = pend[1:] if pr < 7 else []
                        for (hh0, aa) in todo:
                            def pv(e, hh0=hh0, aa=aa, j=j):
                                i = None
                                for hl in range(2):
                                    h = hh0 + hl
                                    g = h // 4
                                    ob, off = ohead(h)
                                    for kb in range(2):
                                        c = (hl * 2 + kb) * 128
                                        i = e.matmul(ps[ob][:, off:off + 65], lhsT=PTb[aa][:, c:c + 128], rhs=Vaug[:, j + kb, g * 65:(g + 1) * 65],
                                                     start=(kb == 0), stop=(kb == 1))
                                return i
                            obs = sorted(set(ohead(hh0 + hl)[0] for hl in range(2)))
                            S.add("pe", pv, reads=[bf("PT%d" % aa), bf("V%d" % j), bf("V%d" % (j + 1))], writes=[pB[o] for o in obs])
                for bi, ob in enumerate(OB):
                    hs = bi * 7
                    nh = 7 if bi < 2 else 2
                    ov = ps[ob][:, 0:nh * 65].rearrange("p (h d) -> p h d", d=65)
                    S.add("dve", lambda e, ov=ov, hs=hs, nh=nh: e.tensor_tensor(out=den[:, hs:hs + nh].unsqueeze(2), in0=ov[:, :, 64:65], in1=es[:, hs:hs + nh].unsqueeze(2), op=ALU.add),
                          reads=[pB[ob], bf("es")], writes=[bf("den")])
                    S.add("dve", lambda e, hs=hs, nh=nh: e.reciprocal(out=rden[:, hs:hs + nh], in_=den[:, hs:hs + nh]), reads=[bf("den")], writes=[bf("rden")])
                    S.add("dve", lambda e, ov=ov, hs=hs, nh=nh: e.tensor_tensor(out=attn_n[:, hs * 64:(hs + nh) * 64].rearrange("p (h d) -> p h d", d=HD), in0=ov[:, :, 0:64],
                                                                               in1=rden[:, hs:hs + nh].unsqueeze(2).to_broadcast([128, nh, HD]), op=ALU.mult),
                          reads=[pB[ob], bf("rden")], writes=[bf("attn_n")])
                def finish_blk(j=j):
                    bt = nextbank()

                    def tra(e, bt=bt):
                        i = None
                        for c in range(8):
                            i = e.transpose(out=psb[bt][:, c * 128:(c + 1) * 128], in_=attn_n[:, c * 128:(c + 1) * 128], identity=ident_b[:])
                        return i
                    S.add("pe", tra, reads=[bf("attn_n"), bf("identb")], writes=[pB[bt]])
                    S.add("act", lambda e, j=j, bt=bt: e.activation(out=attnT[:, :, j * 128:(j + 1) * 128], in_=psb[bt][:].rearrange("p (c t) -> p c t", c=8), func=AF.Copy),
                          reads=[pB[bt]], writes=[bf("attnT")])
                deferred.append(finish_blk)
            while deferred:
                deferred.pop(0)()
            conv_step(1000)
            S.add("pool", lambda e: e.tensor_copy(out=KT[:, :, 0:128], in_=KT[:, :, NB * 128:(NB + 1) * 128]), reads=[bf("KT%d" % NB)], writes=[bf("KT0")])
            S.add("pool", lambda e: e.tensor_copy(out=Vaug[:, 0, :], in_=Vaug[:, NB, :]), reads=[bf("V%d" % NB)], writes=[bf("V0")])

            pool_now[0] = (0, 1, 2, 3, 6, 7)
            BM, BQ = 4, 5
            for c in range(8):
                a = alt("cvb")
                S.add("dve", lambda e, c=c, a=a: e.tensor_copy(out=cvb[a][:], in_=cv[:, c, :]), reads=[bf("cv%d" % c)], writes=[bf("cvb%d" % a)])
                S.add("act", lambda e, c=c, a=a: e.activation(out=sqb[a][:], in_=cv[:, c, :], func=AF.Square), reads=[bf("cv%d" % c)], writes=[bf("sqb%d" % a)])
                S.add("pe", lambda e, c=c, a=a: e.matmul(ps[BM][:], lhsT=onesm[:], rhs=cvb[a][:], start=(c == 0), stop=(c == 7)),
                      reads=[bf("cvb%d" % a), bf("onesm")], writes=[pB[BM]])
                S.add("pe", lambda e, c=c, a=a: e.matmul(ps[BQ][:], lhsT=onesm[:], rhs=sqb[a][:], start=(c == 0), stop=(c == 7)),
                      reads=[bf("sqb%d" % a), bf("onesm")], writes=[pB[BQ]])
            held = []
            ga = gated_proj_gen(attnT, bf("attnT"), True, held, 3)
            drain(ga, 3)
            S.add("act", lambda e: e.activation(out=mean_sb[:], in_=ps[BM][:], func=AF.Copy), reads=[pB[BM]], writes=[bf("mean")])
            S.add("dve", lambda e: e.tensor_tensor(out=nmr_sb[:], in0=mean_sb[:], in1=mean_sb[:], op=ALU.mult), reads=[bf("mean")], writes=[bf("nmr")])
            S.add("dve", lambda e: e.tensor_tensor(out=rstd_sb[:], in0=ps[BQ][:], in1=nmr_sb[:], op=ALU.subtract), reads=[pB[BQ], bf("nmr")], writes=[bf("rstd")])
            S.add("dve", lambda e: e.tensor_scalar(out=rstd_sb[:], in0=rstd_sb[:], scalar1=0.0, scalar2=None, op0=ALU.max), reads=[bf("rstd")], writes=[bf("rstd")])
            S.add("act", lambda e: e.activation(out=rstd_sb[:], in_=rstd_sb[:], func=AF.Sqrt, bias=EPS, scale=1.0), reads=[bf("rstd")], writes=[bf("rstd")])
            S.add("dve", lambda e: e.reciprocal(out=rstd_sb[:], in_=rstd_sb[:]), reads=[bf("rstd")], writes=[bf("rstd")])
            S.add("dve", lambda e: e.scalar_tensor_tensor(out=nmr_sb[:], in0=mean_sb[:], scalar=-1.0, in1=rstd_sb[:], op0=ALU.mult, op1=ALU.mult),
                  reads=[bf("mean"), bf("rstd")], writes=[bf("nmr")])
            for ev in held:
                ev()
            ybs = [(ybuf[:], bf("ybuf")), (xnb[0][:].bitcast(F32), bf("xn0"))]
            zbs = [(zbuf[:], bf("zbuf")), (xnb[1][:].bitcast(F32), bf("xn1"))]
            def ln_pre(c):
                yb, ybB = ybs[c % 2]
                S.add("pool", lambda e, c=c, yb=yb: e.tensor_tensor(out=yb, in0=cv[:, c, :], in1=rstd_sb[:], op=ALU.mult), reads=[bf("cv%d" % c), bf("rstd")], writes=[ybB])
                S.add("dve", lambda e, yb=yb: e.tensor_tensor(out=yb, in0=yb, in1=nmr_sb[:], op=ALU.add), reads=[ybB, bf("nmr")], writes=[ybB])
            ln_pre(0)
            for c in range(8):
                yb, ybB = ybs[c % 2]
                zb, zbB = zbs[c % 2]
                if c + 1 < 8:
                    ln_pre(c + 1)
                S.add("act", lambda e, c=c, yb=yb, zb=zb: e.activation(out=zb, in_=yb, func=AF.Identity, scale=colh[:, c, 3:4], bias=colh[:, c, 4:5]),
                      reads=[ybB] + CP, writes=[zbB])
                a = alt("tn")
                S.add("act", lambda e, a=a, zb=zb: e.activation(out=tnb[a][:], in_=zb, func=AF.Tanh), reads=[zbB], writes=[bf("tn%d" % a)])
                S.add("dve", lambda e, a=a, c=c, zb=zb: e.scalar_tensor_tensor(out=sT[:, c, :], in0=tnb[a][:], scalar=1.0, in1=zb, op0=ALU.add, op1=ALU.mult),
                      reads=[bf("tn%d" % a), zbB], writes=[bf("sT"), bf("QTz")])
                drain(ga, 1)
            drain(ga)


            drain(gated_proj_gen(sT, bf("sT"), False))
            pool_now[0] = tuple(range(8))

            wo = [next_unit(), next_unit()]
            for j in range(NB):
                for half in range(2):
                    w, wb = wo[half]
                    bk = nextbank()
                    mm_group(bk, 0, 512, lambda k, j=j: mT[:, k, j * 128:(j + 1) * 128], lambda k, w=w: w[:, k, :], 8, [bf("mT"), wb])
                    S.add("dve", lambda e, j=j, half=half, bk=bk, x_t=x_t: e.scalar_tensor_tensor(out=x_t[:, j, half * 512:(half + 1) * 512], in0=ps[bk][:], scalar=0.5,
                                                                                      in1=x_t[:, j, half * 512:(half + 1) * 512], op0=ALU.mult, op1=ALU.add),
                          reads=[pB[bk], xb[j]], writes=[xb[j]])
                rms_sq(j, x_t, xb)
                if j == 2:
                    rms_head(0, 3)
            release(2)
            if tile + 1 < ntiles:
                load_x(tile + 1)
            if tile + 1 < ntiles:
                S.add("pool", lambda e: e.memset(regA[:], 0.0), reads=[], writes=[bf("QTz"), bf("sT"), bf("mT")])

            rms_finish(1, x_t, xb, blocks=[0, 1, 2], head=False)
            rms_head(3, 4)
            rms_finish(1, x_t, xb, blocks=[3], head=False)

            for f in range(8):
                if f == 4 and tile + 1 < ntiles:
                    rms1_of(tile + 1, "head")
                w, wb = next_unit()
                for o4 in range(4):
                    fc = f * 4 + o4
                    bk = nextbank()
                    mm_group(bk, 0, T, lambda k, w=w, o4=o4: w[:, k, o4 * 128:(o4 + 1) * 128], lambda k: uT[:, k, :], 8, UT + [wb])
                    a = alt("rl")
                    S.add("act", lambda e, bk=bk, a=a: e.activation(out=rl[a][:], in_=ps[bk][:], func=AF.Relu), reads=[pB[bk]], writes=[bf("rl%d" % a)])
                    S.add("pool", lambda e, a=a, fc=fc: e.tensor_tensor(out=hT[:, fc, :], in0=rl[a][:], in1=rl[a][:], op=ALU.mult),
                          reads=[bf("rl%d" % a)], writes=[bf("hT"), bf("hT_%d" % f)] + [bf("hbuf%d" % c_) for c_ in range(8)] + [bf("cv%d" % c) for c in range(8)])
                release(1)

            for j2 in range(2):
                for g in range(4):
                    if j2 == 0 and tile + 1 < ntiles:
                        pool_now[0] = (0, 1, 2, 3)
                        rms1_of(tile + 1, "body", blocks=[g])
                        pool_now[0] = tuple(range(8))
                    w, wb = next_unit()
                    for j in range(NB):
                        bk = (4 if j2 == 0 else 0) + j

                        def f2(e, w=w, g=g, j=j, bk=bk):
                            i = None
                            for k in range(8):
                                i = e.matmul(ps[bk][:], lhsT=hT[:, g * 8 + k, j * 128:(j + 1) * 128], rhs=w[:, k, :], start=(g == 0 and k == 0), stop=(g == 3 and k == 7))
                            return i
                        S.add("pe", f2, reads=[bf("hT_%d" % (2 * g)), bf("hT_%d" % (2 * g + 1)), wb], writes=[pB[bk]])
                    release(1)
                for j in range(NB):
                    bk = (4 if j2 == 0 else 0) + j
                    S.add("dve", lambda e, j=j, j2=j2, bk=bk, x_t=x_t: e.tensor_tensor(out=x_t[:, j, j2 * 512:(j2 + 1) * 512], in0=ps[bk][:], in1=x_t[:, j, j2 * 512:(j2 + 1) * 512], op=ALU.add),
                          reads=[pB[bk], xb[j]], writes=[xb[j]])
            tk = S.add("sp", lambda e, s, t0=t0, x_t=x_t: e.dma_start(out=out_d[t0:t0 + T, :].rearrange("(j p) d -> p j d", p=128), in_=x_t[:]).then_inc(s, 16),
                       reads=xb, dkey="st%d" % (tile % 2))
            out_tokens.append(tk)
        S.add("sp", None, extra=out_tokens[-2:])
        S.emit(nc, st)
    return nc


def t5_bucket(dist):
    n = np.maximum(dist, 0)
    max_exact = 16
    nf = np.maximum(n, 1).astype(np.float32)
    large = max_exact + (np.log(nf / np.float32(max_exact)) / np.float32(np.log(128 / max_exact)) * np.float32(32 - max_exact)).astype(np.int32)
    large = np.minimum(large, 31)
    return np.where(n < max_exact, n, large)


def host_layout(inputs, ntiles=SEQ // T, ncores=8):
    f = lambda a: np.ascontiguousarray(np.asarray(a, dtype=np.float32))
    x = f(inputs["x"])
    rowp = np.concatenate([f(inputs["norm_mix_g"]), f(inputs["norm_mlp_g"]), f(inputs["b_dw"]), f(inputs["conv_ln_g"]),
                           f(inputs["conv_ln_b"]), f(inputs["w_dw"])[0]], axis=0)
    gq = f(inputs["q_norm_g"])[0]
    gk = f(inputs["k_norm_g"])[0]
    gqk = np.stack([np.tile(gq, 2), np.tile(gk, 2)], axis=1)
    gqk = np.ascontiguousarray(gqk)
    sinks = np.ascontiguousarray(np.broadcast_to(f(inputs["attn_sinks"])[0][None, :], (128, NQ)))
    rb = f(inputs["rel_bias"])
    k = np.arange(128)[:, None]
    q = np.arange(128)[None, :]
    biasT = np.empty((128, NQ, 2, 128), np.float32)
    for kb in range(2):
        dist = q + 128 - (kb * 128 + k)
        valid = (dist >= 0) & (dist < 128)
        g = rb[t5_bucket(dist)]
        g = np.where(valid[:, :, None], g, np.float32(-1e30))
        biasT[:, :, kb, :] = np.transpose(g, (0, 2, 1))
    biasT = np.ascontiguousarray(biasT.reshape(128, NQ * 256))
    common = {
        "w_in": f(inputs["w_in"])[0], "w_attn_o": f(inputs["w_attn_o"])[0], "w_conv_out": f(inputs["w_conv_out"])[0],
        "w_out": f(inputs["w_out"])[0], "w_ff1": f(inputs["w_ff1"])[0], "w_ff2": f(inputs["w_ff2"])[0],
        "rowp": np.ascontiguousarray(rowp), "gqk": gqk, "sinks": sinks, "biasT": biasT,
        "ident": np.eye(128, dtype=np.float32),
    }
    maps = []
    for c in range(ncores):
        m = dict(common)
        m["x"] = np.ascontiguousarray(x[c, :ntiles * T])
        maps.append(m)
    return maps


def kernel(**inputs):
    ntiles = SEQ // T
    nc = build(ntiles)
    maps = host_layout(inputs, ntiles, 8)
    res = run_bass_kernel_spmd(nc, maps, core_ids=list(range(8)))
    return np.stack([r["out"] for r in res.results], axis=0).astype(np.float32)
```

```python
import numpy as np
from contextlib import ExitStack
import concourse.bass as bass
import concourse.mybir as mybir
from concourse.bass_utils import run_bass_kernel_spmd

F32 = mybir.dt.float32
BF16 = mybir.dt.bfloat16
AF = mybir.ActivationFunctionType
ALU = mybir.AluOpType
AX = mybir.AxisListType

D = 1024
SEQ = 8192
NB = 4
T = NB * 128
HD = 64
NQ = 16
NKV = 4
CW = 31
DFF = 4096
EPS = 1e-6
V_END = 1536
GLU_END = V_END + 2048
IN_W = GLU_END + 2048
NCOLP = 5 + CW
NSLOT = 4
ENGS = ("pe", "act", "dve", "pool", "sp")


class Buf:
    __slots__ = ("name", "w", "r")

    def __init__(self, name):
        self.name = name
        self.w = None
        self.r = []


class Sched:
    def __init__(self):
        self.ops = {e: [] for e in ENGS}
        self.cnt = {}
        self.seen = {e: {} for e in ENGS}
        self.keys = []

    def _bump(self, key, n):
        if key not in self.cnt:
            self.cnt[key] = 0
            self.keys.append(key)
        self.cnt[key] += n
        return (key, self.cnt[key])

    def add(self, eng, fn, reads=(), writes=(), dkey=None, ndma=1, extra=()):
        deps = {}

        def need(t):
            if t is None:
                return
            k, v = t
            if eng == "pe" and k == "pe":
                return
            if deps.get(k, 0) < v:
                deps[k] = v
        for b in reads:
            need(b.w)
        for b in writes:
            need(b.w)
            for t in b.r:
                need(t)
        for t in extra:
            need(t)
        waits = []
        seen = self.seen[eng]
        for k, v in deps.items():
            if seen.get(k, 0) >= v:
                continue
            seen[k] = v
            waits.append((k, v))
        if fn is None:
            tok = None
        elif dkey is not None:
            tok = self._bump(dkey, 16 * ndma)
        else:
            tok = self._bump(eng, 1)
        self.ops[eng].append((waits, fn, tok, dkey is not None))
        if tok is not None:
            for b in reads:
                b.r.append(tok)
            for b in writes:
                b.w = tok
                b.r = []
        return tok

    def emit(self, nc, stack):
        sems = {}
        for i, k in enumerate(self.keys):
            sems[k] = stack.enter_context(nc.semaphore("s%d" % i))
        block = stack.enter_context(nc.Block())
        handles = {"pe": block.tensor, "act": block.scalar, "dve": block.vector,
                   "pool": block.gpsimd, "sp": block.sync}
        for eng in ENGS:
            ops = self.ops[eng]
            if not ops:
                continue

            def body(e, ops=ops):
                for waits, fn, tok, is_dma in ops:
                    for k, v in waits:
                        e.wait_ge(sems[k], v)
                    if fn is None:
                        continue
                    if is_dma:
                        fn(e, sems[tok[0]])
                    else:
                        fn(e).then_inc(sems[tok[0]], 1)
            handles[eng](body)


def build(ntiles):
    seq = ntiles * T
    nc = bass.Bass("TRN2", target_bir_lowering=False)
    x_d = nc.dram_tensor("x", [seq, D], F32, kind="ExternalInput").ap()
    w_in_d = nc.dram_tensor("w_in", [D, IN_W], F32, kind="ExternalInput").ap()
    w_ao_d = nc.dram_tensor("w_attn_o", [D, D], F32, kind="ExternalInput").ap()
    w_co_d = nc.dram_tensor("w_conv_out", [D, D], F32, kind="ExternalInput").ap()
    w_out_d = nc.dram_tensor("w_out", [D, D], F32, kind="ExternalInput").ap()
    w_ff1_d = nc.dram_tensor("w_ff1", [D, DFF], F32, kind="ExternalInput").ap()
    w_ff2_d = nc.dram_tensor("w_ff2", [DFF, D], F32, kind="ExternalInput").ap()
    rowp_d = nc.dram_tensor("rowp", [NCOLP, D], F32, kind="ExternalInput").ap()
    gqk_d = nc.dram_tensor("gqk", [128, 2], F32, kind="ExternalInput").ap()
    sink_d = nc.dram_tensor("sinks", [128, NQ], F32, kind="ExternalInput").ap()
    bias_d = nc.dram_tensor("biasT", [128, NQ * 2 * 128], F32, kind="ExternalInput").ap()
    ident_d = nc.dram_tensor("ident", [128, 128], F32, kind="ExternalInput").ap()
    out_d = nc.dram_tensor("out", [seq, D], F32, kind="ExternalOutput").ap()
    wscr = nc.dram_tensor("wscr", [33, 128, 8 * 512], BF16).ap()

    S = Sched()
    with ExitStack() as st:
        def sb(name, shape, dt):
            return st.enter_context(nc.sbuf_tensor(name, shape, dt))

        x_tb = [sb("x_t%d" % i, [128, NB, D], F32) for i in range(2)]
        uT = sb("uT", [128, 8, T], BF16)
        xnb = [sb("xn%d" % i, [128, D], BF16) for i in range(2)]
        sq = sb("sq", [128, 512], F32)
        junk = sq[:].bitcast(BF16)
        gbc = [sb("gbc%d" % i, [128, D], F32) for i in range(2)]
        qn = sb("qn", [128, D], BF16)
        kn = sb("kn", [128, 256], BF16)
        regA = sb("regA", [128, NQ * T], BF16)
        QTz = regA[:].rearrange("p (h t) -> p h t", h=NQ)
        sT = regA[:, 0:8 * T].rearrange("p (c t) -> p c t", c=8)
        mT = regA[:, 8 * T:16 * T].rearrange("p (c t) -> p c t", c=8)
        KT = sb("KT", [128, 2, (NB + 1) * 128], BF16)
        Vaug = sb("Vaug", [128, NB + 1, NKV * 65], BF16)
        Eb = [sb("E%d" % i, [128, 512], BF16) for i in range(3)]
        PTb = [sb("PT%d" % i, [128, 512], BF16) for i in range(3)]
        EB = sb("EB", [128, NQ * 256], BF16)
        attn_n = sb("attn_n", [128, D], BF16)
        attnT = sb("attnT", [128, 8, T], BF16)
        tnb = [sb("tn%d" % i, [128, 512], F32) for i in range(2)]
        regB = sb("regB", [128, 16 * 1024], BF16)
        hT = regB[:, 0:32 * T].rearrange("p (c t) -> p c t", c=32)
        regBf = regB[:].bitcast(F32)
        hbuf = regB[:, 0:8 * (T + 32)].rearrange("p (c t) -> p c t", c=8)
        cv = regBf[:, 4 * (T + 32):4 * (T + 32) + 8 * T].rearrange("p (c t) -> p c t", c=8)
        dg = [sb("dg%d" % i, [128, CW, 128], BF16) for i in range(2)]
        cvb = [sb("cvb%d" % i, [128, T], BF16) for i in range(2)]
        sqb = [sb("sqb%d" % i, [128, T], BF16) for i in range(2)]
        mean_sb = sb("mean_sb", [128, T], F32)
        rstd_sb = sb("rstd_sb", [128, T], F32)
        nmr_sb = sb("nmr_sb", [128, T], F32)
        ybuf = sb("ybuf", [128, T], F32)
        zbuf = sb("zbuf", [128, T], F32)
        rl = [sb("rl%d" % i, [128, T], BF16) for i in range(2)]
        wsl = [sb("wsl%d" % i, [128, 8, 512], BF16) for i in range(NSLOT)]
        ident_f = sb("ident_f", [128, 128], F32)
        ident_b = sb("ident_b", [128, 128], BF16)
        onesm = sb("onesm", [128, 128], BF16)
        rowp = regBf[0:NCOLP, 4096:4096 + D]
        colp = sb("colp", [128, 8, NCOLP], F32)
        colh = sb("colh", [128, 8, NCOLP], F32)
        gqk = sb("gqk_sb", [128, 2], F32)
        gqkp = sb("gqkp", [128, 1], F32)
        es = sb("es", [128, NQ], F32)
        bias_sb = regBf[:, 0:1024]
        ss1 = sb("ss1", [128, NB], F32)
        r1 = sb("r1", [128, NB], F32)
        ssq = sb("ssq", [128, 20], F32)
        rqk = sb("rqk", [128, 20], F32)
        den = sb("den", [128, NQ], F32)
        rden = sb("rden", [128, NQ], F32)
        halo_h = sb("halo_h", [128, 8, 32], BF16)
        ps = [st.enter_context(nc.psum_tensor("ps%d" % i, [128, 512], F32)) for i in range(8)]
        psb = [p[:].bitcast(BF16) for p in ps]

        B = {}

        def bf(name):
            if name not in B:
                B[name] = Buf(name)
            return B[name]
        pB = [bf("ps%d" % i) for i in range(8)]
        rot = [0]

        pool_now = [tuple(range(8))]

        reserved = set()

        def nextbank():
            p = pool_now[0]
            while True:
                i = p[rot[0] % len(p)]
                rot[0] += 1
                if i not in reserved:
                    return i
        tgl = {}

        def alt(name, n=2):
            tgl[name] = tgl.get(name, -1) + 1
            return tgl[name] % n

        S.add("sp", lambda e, s: e.dma_start(out=ident_f[:], in_=ident_d).then_inc(s, 16), writes=[bf("identf")], dkey="c0")
        S.add("sp", lambda e, s: e.dma_start(out=rowp, in_=rowp_d).then_inc(s, 16), writes=[bf("rowp"), bf("hT")], dkey="c1")
        S.add("sp", lambda e, s: e.dma_start(out=gqk[:], in_=gqk_d).then_inc(s, 16), writes=[bf("gqk")], dkey="c2")
        S.add("sp", lambda e, s: e.dma_start(out=es[:], in_=sink_d).then_inc(s, 16), writes=[bf("es")], dkey="c3")
        for gi in range(2):
            S.add("sp", lambda e, s, gi=gi: e.dma_start(out=gbc[gi][:], in_=rowp_d[gi:gi + 1, :].partition_broadcast(128)).then_inc(s, 16),
                  writes=[bf("gbc")], dkey="c5%d" % gi)
        S.add("dve", lambda e: e.tensor_copy(out=ident_b[:], in_=ident_f[:]), reads=[bf("identf")], writes=[bf("identb")])
        S.add("dve", lambda e: e.memset(onesm[:], 1.0 / D), writes=[bf("onesm")])
        S.add("dve", lambda e: e.memset(regA[:], 0.0), writes=[bf("QTz"), bf("sT"), bf("mT")])
        S.add("dve", lambda e: e.memset(KT[:], 0.0), writes=[bf("KT%d" % i) for i in range(NB + 1)])
        S.add("dve", lambda e: e.memset(Vaug[:, 0, :], 0.0), writes=[bf("V0")])
        S.add("dve", lambda e: e.memset(Vaug[:, 1:NB + 1, :], 1.0), writes=[bf("V%d" % i) for i in range(1, NB + 1)])
        S.add("dve", lambda e: e.memset(halo_h[:], 0.0), writes=[bf("halo_h")])
        S.add("dve", lambda e: e.scalar_tensor_tensor(out=gqkp[:], in0=gqk[:, 0:1], scalar=HD ** -0.5, in1=gqk[:, 1:2], op0=ALU.mult, op1=ALU.mult), reads=[bf("gqk")], writes=[bf("gqkp")])
        S.add("act", lambda e: e.activation(out=es[:], in_=es[:], func=AF.Exp), reads=[bf("es")], writes=[bf("es")])
        for hg in range(4):
            S.add("sp", lambda e, s, hg=hg: e.dma_start(out=bias_sb[:], in_=bias_d[:, hg * 1024:(hg + 1) * 1024]).then_inc(s, 16),
                  writes=[bf("bias_sb"), bf("hT")], dkey="c4")
            S.add("act", lambda e, hg=hg: e.activation(out=EB[:, hg * 1024:(hg + 1) * 1024], in_=bias_sb[:], func=AF.Exp),
                  reads=[bf("bias_sb"), bf("hT")], writes=[bf("EB")])
        for kc in range(8):
            bk = nextbank()
            S.add("pe", lambda e, kc=kc, bk=bk: e.transpose(out=ps[bk][:, 0:NCOLP], in_=rowp[:, kc * 128:(kc + 1) * 128],
                                                        identity=ident_f[0:NCOLP, 0:NCOLP]),
                  reads=[bf("rowp"), bf("identf"), bf("hT")], writes=[pB[bk]])
            S.add("dve", lambda e, kc=kc, bk=bk: e.tensor_copy(out=colp[:, kc, :], in_=ps[bk][:, 0:NCOLP]),
                  reads=[pB[bk]], writes=[bf("colp")])
        S.add("dve", lambda e: e.tensor_scalar(out=colh[:], in0=colp[:], scalar1=0.5, scalar2=None, op0=ALU.mult),
              reads=[bf("colp")], writes=[bf("colh")])
        CP = [bf("colp"), bf("colh")]

        def unit_srcs():
            def colpanel(w, c0):
                return w.rearrange("(kc p) n -> p kc n", p=128)[:, :, c0:c0 + 512]
            u = []
            for c0 in (0, 512, 1024):
                u.append(colpanel(w_in_d, c0))
            for j in range(2):
                u.append(colpanel(w_in_d, V_END + j * 512))
                u.append(colpanel(w_in_d, V_END + 1024 + j * 512))
            for j in range(2):
                u.append(colpanel(w_ao_d, j * 512))
                u.append(colpanel(w_in_d, GLU_END + j * 512))
            for j in range(2):
                u.append(colpanel(w_co_d, j * 512))
                u.append(colpanel(w_in_d, GLU_END + 1024 + j * 512))
            for j in range(2):
                u.append(colpanel(w_out_d, j * 512))
            for f in range(8):
                u.append(colpanel(w_ff1_d, f * 512))
            for j2 in range(2):
                for g in range(4):
                    u.append(w_ff2_d[g * 1024:(g + 1) * 1024, j2 * 512:(j2 + 1) * 512].rearrange("(kc p) n -> p kc n", p=128))
            return u
        USRC = unit_srcs()
        NU = len(USRC)
        assert NU == 33
        total_units = ntiles * NU
        issued = [0]
        taken = [0]
        released = [0]

        def issue_fetch():
            n = issued[0]
            if n >= total_units:
                return
            issued[0] += 1
            tile, u = divmod(n, NU)
            s = n % NSLOT
            wb = bf("wsl%d" % s)
            if tile == 0:
                S.add("pool", lambda e, sem, u=u, s=s: e.dma_start(out=wsl[s][:], in_=USRC[u]).then_inc(sem, 16),
                      writes=[wb], dkey="wq%d" % s)
                if ntiles > 1:
                    S.add("sp", lambda e, sem, u=u, s=s: e.dma_start(out=wscr[u], in_=wsl[s][:].rearrange("p k n -> p (k n)")).then_inc(sem, 16),
                          reads=[wb], writes=[bf("scr%d" % u)], dkey="ws%d" % s)
            else:
                S.add("sp", lambda e, sem, u=u, s=s: e.dma_start(out=wsl[s][:].rearrange("p k n -> p (k n)"), in_=wscr[u]).then_inc(sem, 16),
                      reads=[bf("scr%d" % u)], writes=[wb], dkey="wl%d" % s)

        def topup():
            while issued[0] < min(total_units, released[0] + NSLOT):
                issue_fetch()

        def next_unit():
            topup()
            assert taken[0] < issued[0]
            s = taken[0] % NSLOT
            taken[0] += 1
            return wsl[s], bf("wsl%d" % s)

        def release(n):
            released[0] += n
            topup()

        def mm_group(bank, col0, ncol, lhs_fn, rhs_fn, nk, reads, extra_w=()):
            def fn(e):
                i = None
                for k in range(nk):
                    i = e.matmul(ps[bank][:, col0:col0 + ncol], lhsT=lhs_fn(k), rhs=rhs_fn(k), start=(k == 0), stop=(k == nk - 1))
                return i
            return S.add("pe", fn, reads=reads, writes=[pB[bank]] + list(extra_w))

        UT = [bf("uT%d" % j) for j in range(NB)]

        def rms_sq(j, x_t, xbufs):
            S.add("act", lambda e, j=j: e.activation(out=junk, in_=x_t[:, j, :], func=AF.Square, accum_out=ss1[:, j:j + 1]),
                  reads=[xbufs[j]], writes=[bf("sq"), bf("ss1_%d" % j)])

        def rms_head(c0=0, c1=NB):
            rb = [bf("r1_%d" % j) for j in range(c0, c1)]
            S.add("act", lambda e: e.activation(out=r1[:, c0:c1], in_=ss1[:, c0:c1], func=AF.Sqrt, bias=EPS, scale=1.0 / D),
                  reads=[bf("ss1_%d" % j) for j in range(c0, c1)], writes=rb)
            S.add("dve", lambda e: e.reciprocal(out=r1[:, c0:c1], in_=r1[:, c0:c1]), reads=rb, writes=rb)

        def rms_finish(gi, x_t, xbufs, blocks=range(NB), head=True):
            if head:
                rms_head()
            for j in blocks:
                a = alt("xn")
                S.add("dve", lambda e, j=j, a=a: e.scalar_tensor_tensor(out=xnb[a][:], in0=x_t[:, j, :], scalar=r1[:, j:j + 1], in1=gbc[gi][:], op0=ALU.mult, op1=ALU.mult),
                      reads=[xbufs[j], bf("r1_%d" % j), bf("gbc")], writes=[bf("xn%d" % a)])
                bk = nextbank()

                def tr(e, bk=bk, a=a):
                    i = None
                    for kc in range(8):
                        i = e.transpose(out=psb[bk][:, kc * 128:(kc + 1) * 128], in_=xnb[a][:, kc * 128:(kc + 1) * 128], identity=ident_b[:])
                    return i
                S.add("pe", tr, reads=[bf("xn%d" % a), bf("identb")], writes=[pB[bk]])
                S.add("act", lambda e, j=j, bk=bk: e.activation(out=uT[:, :, j * 128:(j + 1) * 128], in_=psb[bk][:].rearrange("p (c t) -> p c t", c=8), func=AF.Copy),
                      reads=[pB[bk]], writes=[UT[j]])

        out_tokens = []
        for tile in range(ntiles):
            t0 = tile * T
            x_t = x_tb[tile % 2]
            xb = [bf("x%d_%d" % (tile % 2, j)) for j in range(NB)]

            def load_x(tl):
                xt2 = x_tb[tl % 2]
                xb2 = [bf("x%d_%d" % (tl % 2, j)) for j in range(NB)]
                tt0 = tl * T
                S.add("sp", lambda e, s: e.dma_start(out=xt2[:], in_=x_d[tt0:tt0 + T, :].rearrange("(j p) d -> p j d", p=128)).then_inc(s, 16),
                      writes=xb2, dkey="xl%d" % (tl % 2))

            def rms1_of(tl, part="all", blocks=range(NB)):
                xt2 = x_tb[tl % 2]
                xb2 = [bf("x%d_%d" % (tl % 2, j)) for j in range(NB)]
                if part in ("all", "head"):
                    for j in range(NB):
                        rms_sq(j, xt2, xb2)
                if part == "all":
                    rms_finish(0, xt2, xb2)
                elif part == "head":
                    rms_head()
                else:
                    rms_finish(0, xt2, xb2, blocks=blocks, head=False)
            if tile == 0:
                load_x(0)
                rms1_of(0)

            def build_dg(c):
                a = c % 2
                S.add("pool", lambda e, a=a, c=c: e.tensor_tensor(out=dg[a][:], in0=ident_b[:].unsqueeze(1).to_broadcast([128, CW, 128]),
                                                                   in1=colh[:, c, 5:5 + CW].unsqueeze(2).to_broadcast([128, CW, 128]), op=ALU.mult),
                      reads=[bf("identb")] + CP, writes=[bf("dg%d" % a)])

            build_dg(0)
            build_dg(1)
            rot[0] = 4
            wq0, wq0b = next_unit()
            wq1, wq1b = next_unit()
            wkv, wkvb = next_unit()
            qkv_banks = {}

            def qkv_mm(j):
                banks = []
                for (w, wb) in ((wq0, wq0b), (wq1, wq1b), (wkv, wkvb)):
                    bk = nextbank()
                    banks.append(bk)
                    mm_group(bk, 0, 512, lambda k, j=j: uT[:, k, j * 128:(j + 1) * 128], lambda k, w=w: w[:, k, :], 8, [UT[j], wb])
                qkv_banks[j] = banks
                reserved.update(banks)

            def qkv_post(j, part="ab"):
                bq0, bq1, bkv = qkv_banks[j]
                if "a" in part:
                    qkv_post_a(j, bq0, bq1, bkv)
                if "b" in part:
                    qkv_post_b(j, bq0, bq1, bkv)

            def qkv_post_a(j, bq0, bq1, bkv):
                for bi, bk in enumerate((bq0, bq1)):
                    S.add("act", lambda e, bk=bk: e.activation(out=sq[:], in_=ps[bk][:], func=AF.Square), reads=[pB[bk]], writes=[bf("sq")])
                    S.add("dve", lambda e, bi=bi: e.tensor_reduce(out=ssq[:, bi * 8:(bi + 1) * 8], in_=sq[:].rearrange("p (h d) -> p h d", d=HD), axis=AX.X, op=ALU.add),
                          reads=[bf("sq")], writes=[bf("ssq")])
                S.add("act", lambda e, bk=bkv: e.activation(out=sq[:, 0:256], in_=ps[bk][:, 0:256], func=AF.Square), reads=[pB[bkv]], writes=[bf("sq")])
                S.add("dve", lambda e: e.tensor_reduce(out=ssq[:, 16:20], in_=sq[:, 0:256].rearrange("p (h d) -> p h d", d=HD), axis=AX.X, op=ALU.add),
                      reads=[bf("sq")], writes=[bf("ssq")])
                S.add("act", lambda e: e.activation(out=rqk[:], in_=ssq[:], func=AF.Sqrt, bias=EPS, scale=1.0 / HD), reads=[bf("ssq")], writes=[bf("rqk")])
                S.add("dve", lambda e: e.reciprocal(out=rqk[:], in_=rqk[:]), reads=[bf("rqk")], writes=[bf("rqk")])
                for hi, bk in enumerate((bq0, bq1)):
                    S.add("dve", lambda e, hi=hi, bk=bk: e.tensor_tensor(
                        out=qn[:, hi * 512:(hi + 1) * 512].rearrange("p (lo mid d) -> p mid lo d", lo=4, mid=2, d=HD),
                        in0=ps[bk][:].rearrange("p (mid lo d) -> p mid lo d", mid=2, lo=4, d=HD),
                        in1=rqk[:, hi * 8:(hi + 1) * 8].rearrange("p (mid lo) -> p mid lo", mid=2).unsqueeze(3).to_broadcast([128, 2, 4, HD]),
                        op=ALU.mult), reads=[pB[bk], bf("rqk")], writes=[bf("qn")])
                S.add("dve", lambda e, bk=bkv: e.tensor_tensor(
                    out=kn[:].rearrange("p (h d) -> p h d", d=HD), in0=ps[bk][:, 0:256].rearrange("p (h d) -> p h d", d=HD),
                    in1=rqk[:, 16:20].unsqueeze(2).to_broadcast([128, 4, HD]), op=ALU.mult), reads=[pB[bkv], bf("rqk")], writes=[bf("kn")])
                S.add("act", lambda e, j=j, bk=bkv: e.activation(out=Vaug[:, j + 1, :].rearrange("p (h d) -> p h d", d=65)[:, :, 0:64],
                                                                 in_=ps[bk][:, 256:512].rearrange("p (h d) -> p h d", d=HD), func=AF.Copy),
                      reads=[pB[bkv]], writes=[bf("V%d" % (j + 1))])

            def qkv_post_b(j, bq0, bq1, bkv):
                bt = nextbank()

                def trq(e, bt=bt):
                    i = None
                    for c in range(8):
                        i = e.transpose(out=psb[bt][:, c * 128:(c + 1) * 128], in_=qn[:, c * 128:(c + 1) * 128], identity=ident_b[:])
                    return i
                S.add("pe", trq, reads=[bf("qn"), bf("identb")], writes=[pB[bt]])
                for mid in range(2):
                    eng = "act" if mid == 0 else "dve"
                    dst = QTz[mid * 64:(mid + 1) * 64, :, j * 128:(j + 1) * 128].rearrange("p (hi m lo) t -> p hi m lo t", hi=2, m=2, lo=4)[:, :, mid, :, :]
                    src = psb[bt][mid * 64:(mid + 1) * 64, :].rearrange("p (hi lo t) -> p hi lo t", hi=2, lo=4)
                    if eng == "act":
                        S.add("act", lambda e, dst=dst, src=src: e.activation(out=dst, in_=src, func=AF.Copy), reads=[pB[bt]], writes=[bf("QTz")])
                    else:
                        S.add("dve", lambda e, dst=dst, src=src: e.tensor_copy(out=dst, in_=src), reads=[pB[bt]], writes=[bf("QTz")])
                bt2 = nextbank()

                def trk(e, bt2=bt2):
                    i = None
                    for c in range(2):
                        i = e.transpose(out=psb[bt2][:, c * 128:(c + 1) * 128], in_=kn[:, c * 128:(c + 1) * 128], identity=ident_b[:])
                    return i
                S.add("pe", trk, reads=[bf("kn"), bf("identb")], writes=[pB[bt2]])
                S.add("act", lambda e, j=j, bt2=bt2: e.activation(out=KT[:, :, (j + 1) * 128:(j + 2) * 128], in_=psb[bt2][:, 0:256].rearrange("p (g t) -> p g t", g=2),
                                                                   func=AF.Identity, scale=gqkp[:, 0:1]),
                      reads=[pB[bt2], bf("gqkp")], writes=[bf("KT%d" % (j + 1))])
                reserved.difference_update(qkv_banks[j])

            def glu_gen():
                S.add("pool", lambda e: e.tensor_copy(out=hbuf[:, :, 0:32], in_=halo_h[:]), reads=[bf("halo_h")], writes=[bf("hbuf%d" % c_) for c_ in range(8)] + [bf("hT")] + [bf("hT_%d" % f) for f in range(8)])
                for half in range(2):
                    wa, wab = next_unit()
                    wg, wgb = next_unit()
                    for o4 in range(4):
                        c = half * 4 + o4
                        ba = nextbank()
                        mm_group(ba, 0, T, lambda k, wa=wa, o4=o4: wa[:, k, o4 * 128:(o4 + 1) * 128], lambda k: uT[:, k, :], 8, UT + [wab])
                        bg = nextbank()
                        mm_group(bg, 0, T, lambda k, wg=wg, o4=o4: wg[:, k, o4 * 128:(o4 + 1) * 128], lambda k: uT[:, k, :], 8, UT + [wgb])
                        a = alt("tn")
                        S.add("act", lambda e, bg=bg, a=a: e.activation(out=tnb[a][:], in_=ps[bg][:], func=AF.Tanh, scale=0.5), reads=[pB[bg]], writes=[bf("tn%d" % a)])
                        S.add("dve", lambda e, ba=ba, a=a, c=c: e.scalar_tensor_tensor(out=hbuf[:, c, 32:32 + T], in0=tnb[a][:], scalar=1.0, in1=ps[ba][:], op0=ALU.add, op1=ALU.mult),
                              reads=[bf("tn%d" % a), pB[ba]], writes=[bf("hbuf%d" % c)])
                        yield
                    release(2)
                S.add("pool", lambda e: e.tensor_copy(out=halo_h[:], in_=hbuf[:, :, T:T + 32]), reads=[bf("hbuf%d" % c_) for c_ in range(8)], writes=[bf("halo_h")])


            def drain(g, n=100):
                for _ in range(n):
                    try:
                        next(g)
                    except StopIteration:
                        return
            qkv_mm(0)
            qkv_mm(1)
            qkv_post(0)
            qkv_mm(2)
            qkv_post(1)
            qkv_mm(3)
            release(3)
            qkv_post(2)
            qkv_post(3, "a")
            gg = glu_gen()
            drain(gg, 3)
            qkv_post(3, "b")
            drain(gg)
            def gated_proj_gen(actT, actbuf, first, held=None, nhold=0):
                nh = [nhold]
                for half in range(2):
                    wp, wpb = next_unit()
                    wg, wgb = next_unit()
                    for o4 in range(4):
                        oc = half * 4 + o4
                        bp = nextbank()
                        mm_group(bp, 0, T, lambda k, wp=wp, o4=o4: wp[:, k, o4 * 128:(o4 + 1) * 128], lambda k: actT[:, k, :], 8, [actbuf, wpb])
                        bg = nextbank()
                        mm_group(bg, 0, T, lambda k, wg=wg, o4=o4: wg[:, k, o4 * 128:(o4 + 1) * 128], lambda k: uT[:, k, :], 8, UT + [wgb])
                        def evac(bp=bp, bg=bg, oc=oc):
                            a = alt("tn")
                            S.add("act", lambda e, bg=bg, a=a: e.activation(out=tnb[a][:], in_=ps[bg][:], func=AF.Tanh, scale=0.5), reads=[pB[bg]], writes=[bf("tn%d" % a)])
                            if first:
                                S.add("dve", lambda e, bp=bp, a=a, oc=oc: e.scalar_tensor_tensor(out=mT[:, oc, :], in0=tnb[a][:], scalar=1.0, in1=ps[bp][:], op0=ALU.add, op1=ALU.mult),
                                      reads=[bf("tn%d" % a), pB[bp]], writes=[bf("mT"), bf("QTz")])
                            else:
                                S.add("dve", lambda e, bp=bp, a=a: e.scalar_tensor_tensor(out=tnb[a][:], in0=tnb[a][:], scalar=1.0, in1=ps[bp][:], op0=ALU.add, op1=ALU.mult),
                                      reads=[bf("tn%d" % a), pB[bp]], writes=[bf("tn%d" % a)])
                                S.add("dve", lambda e, a=a, oc=oc: e.tensor_tensor(out=mT[:, oc, :], in0=tnb[a][:], in1=mT[:, oc, :], op=ALU.add),
                                      reads=[bf("tn%d" % a), bf("mT")], writes=[bf("mT")])
                        if held is not None and nh[0] > 0:
                            nh[0] -= 1
                            held.append(evac)
                        else:
                            evac()
                        yield
                    release(2)

            def drain(g, n=100):
                for _ in range(n):
                    try:
                        next(g)
                    except StopIteration:
                        return

            CB = 4

            def conv_steps():
                for c in range(8):
                    a = c % 2
                    for j0 in range(0, CW, 4):
                        j1 = min(CW, j0 + 4)

                        def cm(e, c=c, a=a, j0=j0, j1=j1):
                            i = None
                            for jt in range(j0, j1):
                                i = e.matmul(ps[CB][:], lhsT=dg[a][:, jt, :], rhs=hbuf[:, c, 2 + jt:2 + jt + T], start=(jt == 0), stop=(jt == CW - 1))
                            return i
                        S.add("pe", cm, reads=[bf("dg%d" % a), bf("hbuf%d" % c)], writes=[pB[CB]])
                        if j1 == CW:
                            S.add("act", lambda e, c=c: e.activation(out=cv[:, c, :], in_=ps[CB][:], func=AF.Identity, bias=colp[:, c, 2:3]),
                                  reads=[pB[CB]] + CP, writes=[bf("cv%d" % c)])
                            if c + 2 < 8:
                                build_dg(c + 2)
                        yield
            cgen = conv_steps()
            pool_now[0] = (0, 1, 2, 3)

            def conv_step(n=1):
                for _ in range(n):
                    try:
                        next(cgen)
                    except StopIteration:
                        return
            OB = (5, 6, 7)

            def ohead(h):
                return OB[h // 7], (h % 7) * 65
            deferred = []
            for j in range(NB):
                pend = []
                for pr in range(8):
                    if pr == 3 and deferred:
                        deferred.pop(0)()
                    h0 = 2 * pr
                    bk = nextbank()

                    def smm(e, bk=bk, h0=h0, j=j):
                        i = None
                        for hl in range(2):
                            h = h0 + hl
                            gp = (h // 4) // 2
                            for kb in range(2):
                                c = (hl * 2 + kb) * 128
                                i = e.matmul(ps[bk][:, c:c + 128], lhsT=KT[:, gp, (j + kb) * 128:(j + kb + 1) * 128],
                                             rhs=QTz[:, h, j * 128:(j + 1) * 128], start=True, stop=True)
                        return i
                    S.add("pe", smm, reads=[bf("QTz"), bf("KT%d" % j), bf("KT%d" % (j + 1))], writes=[pB[bk]])
                    a = alt("E", 3)
                    S.add("act", lambda e, bk=bk, a=a: e.activation(out=Eb[a][:], in_=ps[bk][:], func=AF.Exp), reads=[pB[bk]], writes=[bf("E%d" % a)])
                    S.add("dve", lambda e, a=a, h0=h0: e.tensor_tensor(out=PTb[a][:], in0=Eb[a][:], in1=EB[:, h0 * 256:(h0 + 2) * 256], op=ALU.mult),
                          reads=[bf("E%d" % a), bf("EB")], writes=[bf("PT%d" % a)])
                    pend.append((h0, a))
                    conv_step(2)
                    if len(pend) == 3 or pr == 7:
                        todo = pend[:1] if pr < 7 else pend
                        pend = pend[1:] if pr < 7 else []
                        for (hh0, aa) in todo:
                            def pv(e, hh0=hh0, aa=aa, j=j):
                                i = None
                                for hl in range(2):
                                    h = hh0 + hl
                                    g = h // 4
                                    ob, off = ohead(h)
                                    for kb in range(2):
                                        c = (hl * 2 + kb) * 128
                                        i = e.matmul(ps[ob][:, off:off + 65], lhsT=PTb[aa][:, c:c + 128], rhs=Vaug[:, j + kb, g * 65:(g + 1) * 65],
                                                     start=(kb == 0), stop=(kb == 1))
                                return i
                            obs = sorted(set(ohead(hh0 + hl)[0] for hl in range(2)))
                            S.add("pe", pv, reads=[bf("PT%d" % aa), bf("V%d" % j), bf("V%d" % (j + 1))], writes=[pB[o] for o in obs])
                for bi, ob in enumerate(OB):
                    hs = bi * 7
                    nh = 7 if bi < 2 else 2
                    ov = ps[ob][:, 0:nh * 65].rearrange("p (h d) -> p h d", d=65)
                    S.add("dve", lambda e, ov=ov, hs=hs, nh=nh: e.tensor_tensor(out=den[:, hs:hs + nh].unsqueeze(2), in0=ov[:, :, 64:65], in1=es[:, hs:hs + nh].unsqueeze(2), op=ALU.add),
                          reads=[pB[ob], bf("es")], writes=[bf("den")])
                    S.add("dve", lambda e, hs=hs, nh=nh: e.reciprocal(out=rden[:, hs:hs + nh], in_=den[:, hs:hs + nh]), reads=[bf("den")], writes=[bf("rden")])
                    S.add("dve", lambda e, ov=ov, hs=hs, nh=nh: e.tensor_tensor(out=attn_n[:, hs * 64:(hs + nh) * 64].rearrange("p (h d) -> p h d", d=HD), in0=ov[:, :, 0:64],
                                                                               in1=rden[:, hs:hs + nh].unsqueeze(2).to_broadcast([128, nh, HD]), op=ALU.mult),
                          reads=[pB[ob], bf("rden")], writes=[bf("attn_n")])
                def finish_blk(j=j):
                    bt = nextbank()

                    def tra(e, bt=bt):
                        i = None
                        for c in range(8):
                            i = e.transpose(out=psb[bt][:, c * 128:(c + 1) * 128], in_=attn_n[:, c * 128:(c + 1) * 128], identity=ident_b[:])
                        return i
                    S.add("pe", tra, reads=[bf("attn_n"), bf("identb")], writes=[pB[bt]])
                    S.add("act", lambda e, j=j, bt=bt: e.activation(out=attnT[:, :, j * 128:(j + 1) * 128], in_=psb[bt][:].rearrange("p (c t) -> p c t", c=8), func=AF.Copy),
                          reads=[pB[bt]], writes=[bf("attnT")])
                deferred.append(finish_blk)
            while deferred:
                deferred.pop(0)()
            conv_step(1000)
            S.add("pool", lambda e: e.tensor_copy(out=KT[:, :, 0:128], in_=KT[:, :, NB * 128:(NB + 1) * 128]), reads=[bf("KT%d" % NB)], writes=[bf("KT0")])
            S.add("pool", lambda e: e.tensor_copy(out=Vaug[:, 0, :], in_=Vaug[:, NB, :]), reads=[bf("V%d" % NB)], writes=[bf("V0")])

            pool_now[0] = (0, 1, 2, 3, 6, 7)
            BM, BQ = 4, 5
            for c in range(8):
                a = alt("cvb")
                S.add("dve", lambda e, c=c, a=a: e.tensor_copy(out=cvb[a][:], in_=cv[:, c, :]), reads=[bf("cv%d" % c)], writes=[bf("cvb%d" % a)])
                S.add("act", lambda e, c=c, a=a: e.activation(out=sqb[a][:], in_=cv[:, c, :], func=AF.Square), reads=[bf("cv%d" % c)], writes=[bf("sqb%d" % a)])
                S.add("pe", lambda e, c=c, a=a: e.matmul(ps[BM][:], lhsT=onesm[:], rhs=cvb[a][:], start=(c == 0), stop=(c == 7)),
                      reads=[bf("cvb%d" % a), bf("onesm")], writes=[pB[BM]])
                S.add("pe", lambda e, c=c, a=a: e.matmul(ps[BQ][:], lhsT=onesm[:], rhs=sqb[a][:], start=(c == 0), stop=(c == 7)),
                      reads=[bf("sqb%d" % a), bf("onesm")], writes=[pB[BQ]])
            held = []
            ga = gated_proj_gen(attnT, bf("attnT"), True, held, 3)
            drain(ga, 3)
            S.add("act", lambda e: e.activation(out=mean_sb[:], in_=ps[BM][:], func=AF.Copy), reads=[pB[BM]], writes=[bf("mean")])
            S.add("dve", lambda e: e.tensor_tensor(out=nmr_sb[:], in0=mean_sb[:], in1=mean_sb[:], op=ALU.mult), reads=[bf("mean")], writes=[bf("nmr")])
            S.add("dve", lambda e: e.tensor_tensor(out=rstd_sb[:], in0=ps[BQ][:], in1=nmr_sb[:], op=ALU.subtract), reads=[pB[BQ], bf("nmr")], writes=[bf("rstd")])
            S.add("dve", lambda e: e.tensor_scalar(out=rstd_sb[:], in0=rstd_sb[:], scalar1=0.0, scalar2=None, op0=ALU.max), reads=[bf("rstd")], writes=[bf("rstd")])
            S.add("act", lambda e: e.activation(out=rstd_sb[:], in_=rstd_sb[:], func=AF.Ln, bias=EPS, scale=1.0), reads=[bf("rstd")], writes=[bf("rstd")])
            S.add("act", lambda e: e.activation(out=rstd_sb[:], in_=rstd_sb[:], func=AF.Exp, scale=-0.5), reads=[bf("rstd")], writes=[bf("rstd")])
            S.add("dve", lambda e: e.scalar_tensor_tensor(out=nmr_sb[:], in0=mean_sb[:], scalar=-1.0, in1=rstd_sb[:], op0=ALU.mult, op1=ALU.mult),
                  reads=[bf("mean"), bf("rstd")], writes=[bf("nmr")])
            for ev in held:
                ev()
            ybs = [(ybuf[:], bf("ybuf")), (xnb[0][:].bitcast(F32), bf("xn0"))]
            zbs = [(zbuf[:], bf("zbuf")), (xnb[1][:].bitcast(F32), bf("xn1"))]
            def ln_pre(c):
                yb, ybB = ybs[c % 2]
                S.add("pool", lambda e, c=c, yb=yb: e.tensor_tensor(out=yb, in0=cv[:, c, :], in1=rstd_sb[:], op=ALU.mult), reads=[bf("cv%d" % c), bf("rstd")], writes=[ybB])
                S.add("dve", lambda e, yb=yb: e.tensor_tensor(out=yb, in0=yb, in1=nmr_sb[:], op=ALU.add), reads=[ybB, bf("nmr")], writes=[ybB])
            ln_pre(0)
            for c in range(8):
                yb, ybB = ybs[c % 2]
                zb, zbB = zbs[c % 2]
                if c + 1 < 8:
                    ln_pre(c + 1)
                S.add("act", lambda e, c=c, yb=yb, zb=zb: e.activation(out=zb, in_=yb, func=AF.Identity, scale=colh[:, c, 3:4], bias=colh[:, c, 4:5]),
                      reads=[ybB] + CP, writes=[zbB])
                a = alt("tn")
                S.add("act", lambda e, a=a, zb=zb: e.activation(out=tnb[a][:], in_=zb, func=AF.Tanh), reads=[zbB], writes=[bf("tn%d" % a)])
                S.add("dve", lambda e, a=a, c=c, zb=zb: e.scalar_tensor_tensor(out=sT[:, c, :], in0=tnb[a][:], scalar=1.0, in1=zb, op0=ALU.add, op1=ALU.mult),
                      reads=[bf("tn%d" % a), zbB], writes=[bf("sT"), bf("QTz")])
                drain(ga, 1)
            drain(ga)


            drain(gated_proj_gen(sT, bf("sT"), False))
            pool_now[0] = tuple(range(8))

            wo = [next_unit(), next_unit()]
            for j in range(NB):
                for half in range(2):
                    w, wb = wo[half]
                    bk = nextbank()
                    mm_group(bk, 0, 512, lambda k, j=j: mT[:, k, j * 128:(j + 1) * 128], lambda k, w=w: w[:, k, :], 8, [bf("mT"), wb])
                    S.add("dve", lambda e, j=j, half=half, bk=bk, x_t=x_t: e.scalar_tensor_tensor(out=x_t[:, j, half * 512:(half + 1) * 512], in0=ps[bk][:], scalar=0.5,
                                                                                      in1=x_t[:, j, half * 512:(half + 1) * 512], op0=ALU.mult, op1=ALU.add),
                          reads=[pB[bk], xb[j]], writes=[xb[j]])
                rms_sq(j, x_t, xb)
                if j == 2:
                    rms_head(0, 3)
            release(2)
            if tile + 1 < ntiles:
                load_x(tile + 1)
            if tile + 1 < ntiles:
                S.add("pool", lambda e: e.memset(regA[:], 0.0), reads=[], writes=[bf("QTz"), bf("sT"), bf("mT")])

            rms_finish(1, x_t, xb, blocks=[0, 1, 2], head=False)
            rms_head(3, 4)
            rms_finish(1, x_t, xb, blocks=[3], head=False)

            for f in range(8):
                if f == 4 and tile + 1 < ntiles:
                    rms1_of(tile + 1, "head")
                w, wb = next_unit()
                for o4 in range(4):
                    fc = f * 4 + o4
                    bk = nextbank()
                    mm_group(bk, 0, T, lambda k, w=w, o4=o4: w[:, k, o4 * 128:(o4 + 1) * 128], lambda k: uT[:, k, :], 8, UT + [wb])
                    a = alt("rl")
                    S.add("act", lambda e, bk=bk, a=a: e.activation(out=rl[a][:], in_=ps[bk][:], func=AF.Relu), reads=[pB[bk]], writes=[bf("rl%d" % a)])
                    S.add("pool", lambda e, a=a, fc=fc: e.tensor_tensor(out=hT[:, fc, :], in0=rl[a][:], in1=rl[a][:], op=ALU.mult),
                          reads=[bf("rl%d" % a)], writes=[bf("hT"), bf("hT_%d" % f)] + [bf("hbuf%d" % c_) for c_ in range(8)] + [bf("cv%d" % c) for c in range(8)])
                release(1)

            for j2 in range(2):
                for g in range(4):
                    if j2 == 0 and tile + 1 < ntiles:
                        pool_now[0] = (0, 1, 2, 3)
                        rms1_of(tile + 1, "body", blocks=[g])
                        pool_now[0] = tuple(range(8))
                    w, wb = next_unit()
                    for j in range(NB):
                        bk = (4 if j2 == 0 else 0) + j

                        def f2(e, w=w, g=g, j=j, bk=bk):
                            i = None
                            for k in range(8):
                                i = e.matmul(ps[bk][:], lhsT=hT[:, g * 8 + k, j * 128:(j + 1) * 128], rhs=w[:, k, :], start=(g == 0 and k == 0), stop=(g == 3 and k == 7))
                            return i
                        S.add("pe", f2, reads=[bf("hT_%d" % (2 * g)), bf("hT_%d" % (2 * g + 1)), wb], writes=[pB[bk]])
                    release(1)
                for j in range(NB):
                    bk = (4 if j2 == 0 else 0) + j
                    S.add("dve", lambda e, j=j, j2=j2, bk=bk, x_t=x_t: e.tensor_tensor(out=x_t[:, j, j2 * 512:(j2 + 1) * 512], in0=ps[bk][:], in1=x_t[:, j, j2 * 512:(j2 + 1) * 512], op=ALU.add),
                          reads=[pB[bk], xb[j]], writes=[xb[j]])
            tk = S.add("sp", lambda e, s, t0=t0, x_t=x_t: e.dma_start(out=out_d[t0:t0 + T, :].rearrange("(j p) d -> p j d", p=128), in_=x_t[:]).then_inc(s, 16),
                       reads=xb, dkey="st%d" % (tile % 2))
            out_tokens.append(tk)
        S.add("sp", None, extra=out_tokens[-2:])
        S.emit(nc, st)
    return nc


def t5_bucket(dist):
    n = np.maximum(dist, 0)
    max_exact = 16
    nf = np.maximum(n, 1).astype(np.float32)
    large = max_exact + (np.log(nf / np.float32(max_exact)) / np.float32(np.log(128 / max_exact)) * np.float32(32 - max_exact)).astype(np.int32)
    large = np.minimum(large, 31)
    return np.where(n < max_exact, n, large)


def host_layout(inputs, ntiles=SEQ // T, ncores=8):
    f = lambda a: np.ascontiguousarray(np.asarray(a, dtype=np.float32))
    x = f(inputs["x"])
    rowp = np.concatenate([f(inputs["norm_mix_g"]), f(inputs["norm_mlp_g"]), f(inputs["b_dw"]), f(inputs["conv_ln_g"]),
                           f(inputs["conv_ln_b"]), f(inputs["w_dw"])[0]], axis=0)
    gq = f(inputs["q_norm_g"])[0]
    gk = f(inputs["k_norm_g"])[0]
    gqk = np.stack([np.tile(gq, 2), np.tile(gk, 2)], axis=1)
    gqk = np.ascontiguousarray(gqk)
    sinks = np.ascontiguousarray(np.broadcast_to(f(inputs["attn_sinks"])[0][None, :], (128, NQ)))
    rb = f(inputs["rel_bias"])
    k = np.arange(128)[:, None]
    q = np.arange(128)[None, :]
    biasT = np.empty((128, NQ, 2, 128), np.float32)
    for kb in range(2):
        dist = q + 128 - (kb * 128 + k)
        valid = (dist >= 0) & (dist < 128)
        g = rb[t5_bucket(dist)]
        g = np.where(valid[:, :, None], g, np.float32(-1e30))
        biasT[:, :, kb, :] = np.transpose(g, (0, 2, 1))
    biasT = np.ascontiguousarray(biasT.reshape(128, NQ * 256))
    common = {
        "w_in": f(inputs["w_in"])[0], "w_attn_o": f(inputs["w_attn_o"])[0], "w_conv_out": f(inputs["w_conv_out"])[0],
        "w_out": f(inputs["w_out"])[0], "w_ff1": f(inputs["w_ff1"])[0], "w_ff2": f(inputs["w_ff2"])[0],
        "rowp": np.ascontiguousarray(rowp), "gqk": gqk, "sinks": sinks, "biasT": biasT,
        "ident": np.eye(128, dtype=np.float32),
    }
    maps = []
    for c in range(ncores):
        m = dict(common)
        m["x"] = np.ascontiguousarray(x[c, :ntiles * T])
        maps.append(m)
    return maps


def kernel(**inputs):
    ntiles = SEQ // T
    nc = build(ntiles)
    maps = host_layout(inputs, ntiles, 8)
    res = run_bass_kernel_spmd(nc, maps, core_ids=list(range(8)))
    return np.stack([r["out"] for r in res.results], axis=0).astype(np.float32)
```

```python
import numpy as np
from contextlib import ExitStack
import concourse.bass as bass
import concourse.mybir as mybir
from concourse.bass_utils import run_bass_kernel_spmd

F32 = mybir.dt.float32
BF16 = mybir.dt.bfloat16
AF = mybir.ActivationFunctionType
ALU = mybir.AluOpType
AX = mybir.AxisListType

D = 1024
SEQ = 8192
NB = 4
T = NB * 128
HD = 64
NQ = 16
NKV = 4
CW = 31
DFF = 4096
EPS = 1e-6
V_END = 1536
GLU_END = V_END + 2048
IN_W = GLU_END + 2048
NCOLP = 5 + CW
NSLOT = 4
ENGS = ("pe", "act", "dve", "pool", "sp")


class Buf:
    __slots__ = ("name", "w", "r")

    def __init__(self, name):
        self.name = name
        self.w = None
        self.r = []


class Sched:
    def __init__(self):
        self.ops = {e: [] for e in ENGS}
        self.cnt = {}
        self.seen = {e: {} for e in ENGS}
        self.keys = []

    def _bump(self, key, n):
        if key not in self.cnt:
            self.cnt[key] = 0
            self.keys.append(key)
        self.cnt[key] += n
        return (key, self.cnt[key])

    def add(self, eng, fn, reads=(), writes=(), dkey=None, ndma=1, extra=()):
        deps = {}

        def need(t):
            if t is None:
                return
            k, v = t
            if eng == "pe" and k == "pe":
                return
            if deps.get(k, 0) < v:
                deps[k] = v
        for b in reads:
            need(b.w)
        for b in writes:
            need(b.w)
            for t in b.r:
                need(t)
        for t in extra:
            need(t)
        waits = []
        seen = self.seen[eng]
        for k, v in deps.items():
            if seen.get(k, 0) >= v:
                continue
            seen[k] = v
            waits.append((k, v))
        if fn is None:
            tok = None
        elif dkey is not None:
            tok = self._bump(dkey, 16 * ndma)
        else:
            tok = self._bump(eng, 1)
        self.ops[eng].append((waits, fn, tok, dkey is not None))
        if tok is not None:
            for b in reads:
                b.r.append(tok)
            for b in writes:
                b.w = tok
                b.r = []
        return tok

    def emit(self, nc, stack):
        sems = {}
        for i, k in enumerate(self.keys):
            sems[k] = stack.enter_context(nc.semaphore("s%d" % i))
        block = stack.enter_context(nc.Block())
        handles = {"pe": block.tensor, "act": block.scalar, "dve": block.vector,
                   "pool": block.gpsimd, "sp": block.sync}
        for eng in ENGS:
            ops = self.ops[eng]
            if not ops:
                continue

            def body(e, ops=ops):
                for waits, fn, tok, is_dma in ops:
                    for k, v in waits:
                        e.wait_ge(sems[k], v)
                    if fn is None:
                        continue
                    if is_dma:
                        fn(e, sems[tok[0]])
                    else:
                        fn(e).then_inc(sems[tok[0]], 1)
            handles[eng](body)


def build(ntiles):
    seq = ntiles * T
    nc = bass.Bass("TRN2", target_bir_lowering=False)
    x_d = nc.dram_tensor("x", [seq, D], F32, kind="ExternalInput").ap()
    w_in_d = nc.dram_tensor("w_in", [D, IN_W], F32, kind="ExternalInput").ap()
    w_ao_d = nc.dram_tensor("w_attn_o", [D, D], F32, kind="ExternalInput").ap()
    w_co_d = nc.dram_tensor("w_conv_out", [D, D], F32, kind="ExternalInput").ap()
    w_out_d = nc.dram_tensor("w_out", [D, D], F32, kind="ExternalInput").ap()
    w_ff1_d = nc.dram_tensor("w_ff1", [D, DFF], F32, kind="ExternalInput").ap()
    w_ff2_d = nc.dram_tensor("w_ff2", [DFF, D], F32, kind="ExternalInput").ap()
    rowp_d = nc.dram_tensor("rowp", [NCOLP, D], F32, kind="ExternalInput").ap()
    gqk_d = nc.dram_tensor("gqk", [128, 2], F32, kind="ExternalInput").ap()
    sink_d = nc.dram_tensor("sinks", [128, NQ], F32, kind="ExternalInput").ap()
    bias_d = nc.dram_tensor("biasT", [128, NQ * 2 * 128], F32, kind="ExternalInput").ap()
    ident_d = nc.dram_tensor("ident", [128, 128], F32, kind="ExternalInput").ap()
    out_d = nc.dram_tensor("out", [seq, D], F32, kind="ExternalOutput").ap()
    wscr = nc.dram_tensor("wscr", [33, 128, 8 * 512], BF16).ap()

    S = Sched()
    with ExitStack() as st:
        def sb(name, shape, dt):
            return st.enter_context(nc.sbuf_tensor(name, shape, dt))

        x_tb = [sb("x_t%d" % i, [128, NB, D], F32) for i in range(2)]
        uT = sb("uT", [128, 8, T], BF16)
        xnb = [sb("xn%d" % i, [128, D], BF16) for i in range(2)]
        sq = sb("sq", [128, 512], F32)
        sq2 = sb("sq2", [128, 512], F32)
        junk = sq[:].bitcast(BF16)
        gbc = [sb("gbc%d" % i, [128, D], F32) for i in range(2)]
        qn = sb("qn", [128, D], BF16)
        kn = sb("kn", [128, 256], BF16)
        regA = sb("regA", [128, NQ * T], BF16)
        QTz = regA[:].rearrange("p (h t) -> p h t", h=NQ)
        sT = regA[:, 0:8 * T].rearrange("p (c t) -> p c t", c=8)
        mT = regA[:, 8 * T:16 * T].rearrange("p (c t) -> p c t", c=8)
        KT = sb("KT", [128, 2, (NB + 1) * 128], BF16)
        Vaug = sb("Vaug", [128, NB + 1, NKV * 65], BF16)
        Eb = [sb("E%d" % i, [128, 512], BF16) for i in range(3)]
        PTb = [sb("PT%d" % i, [128, 512], BF16) for i in range(3)]
        EB = sb("EB", [128, NQ * 256], BF16)
        attn_n = sb("attn_n", [128, D], BF16)
        attnT = sb("attnT", [128, 8, T], BF16)
        tnb = [sb("tn%d" % i, [128, 512], F32) for i in range(2)]
        regB = sb("regB", [128, 16 * 1024], BF16)
        hT = regB[:, 0:32 * T].rearrange("p (c t) -> p c t", c=32)
        regBf = regB[:].bitcast(F32)
        hbuf = regB[:, 0:8 * (T + 32)].rearrange("p (c t) -> p c t", c=8)
        cv = regBf[:, 4 * (T + 32):4 * (T + 32) + 8 * T].rearrange("p (c t) -> p c t", c=8)
        dg = [sb("dg%d" % i, [128, CW, 128], BF16) for i in range(2)]
        cvb = [sb("cvb%d" % i, [128, T], BF16) for i in range(2)]
        sqb = [sb("sqb%d" % i, [128, T], BF16) for i in range(2)]
        mean_sb = sb("mean_sb", [128, T], F32)
        rstd_sb = sb("rstd_sb", [128, T], F32)
        nmr_sb = sb("nmr_sb", [128, T], F32)
        ybuf = sb("ybuf", [128, T], F32)
        zbuf = sb("zbuf", [128, T], F32)
        rl = [sb("rl%d" % i, [128, T], BF16) for i in range(2)]
        wsl = [sb("wsl%d" % i, [128, 8, 512], BF16) for i in range(NSLOT)]
        ident_f = sb("ident_f", [128, 128], F32)
        ident_b = sb("ident_b", [128, 128], BF16)
        onesm = sb("onesm", [128, 128], BF16)
        rowp = regBf[0:NCOLP, 4096:4096 + D]
        colp = sb("colp", [128, 8, NCOLP], F32)
        colh = sb("colh", [128, 8, NCOLP], F32)
        gqk = sb("gqk_sb", [128, 2], F32)
        gqkp = sb("gqkp", [128, 1], F32)
        es = sb("es", [128, NQ], F32)
        bias_sb = regBf[:, 0:1024]
        ss1 = sb("ss1", [128, NB], F32)
        r1 = sb("r1", [128, NB], F32)
        ssq = sb("ssq", [128, 20], F32)
        rqk = sb("rqk", [128, 20], F32)
        den = sb("den", [128, NQ], F32)
        rden = sb("rden", [128, NQ], F32)
        halo_h = sb("halo_h", [128, 8, 32], BF16)
        ps = [st.enter_context(nc.psum_tensor("ps%d" % i, [128, 512], F32)) for i in range(8)]
        psb = [p[:].bitcast(BF16) for p in ps]

        B = {}

        def bf(name):
            if name not in B:
                B[name] = Buf(name)
            return B[name]
        pB = [bf("ps%d" % i) for i in range(8)]
        rot = [0]

        pool_now = [tuple(range(8))]

        reserved = set()

        def nextbank():
            p = pool_now[0]
            while True:
                i = p[rot[0] % len(p)]
                rot[0] += 1
                if i not in reserved:
                    return i
        tgl = {}

        def alt(name, n=2):
            tgl[name] = tgl.get(name, -1) + 1
            return tgl[name] % n

        S.add("sp", lambda e, s: e.dma_start(out=ident_f[:], in_=ident_d).then_inc(s, 16), writes=[bf("identf")], dkey="c0")
        S.add("sp", lambda e, s: e.dma_start(out=rowp, in_=rowp_d).then_inc(s, 16), writes=[bf("rowp"), bf("hT")], dkey="c1")
        S.add("sp", lambda e, s: e.dma_start(out=gqk[:], in_=gqk_d).then_inc(s, 16), writes=[bf("gqk")], dkey="c2")
        S.add("sp", lambda e, s: e.dma_start(out=es[:], in_=sink_d).then_inc(s, 16), writes=[bf("es")], dkey="c3")
        for gi in range(2):
            S.add("sp", lambda e, s, gi=gi: e.dma_start(out=gbc[gi][:], in_=rowp_d[gi:gi + 1, :].partition_broadcast(128)).then_inc(s, 16),
                  writes=[bf("gbc")], dkey="c5%d" % gi)
        S.add("dve", lambda e: e.tensor_copy(out=ident_b[:], in_=ident_f[:]), reads=[bf("identf")], writes=[bf("identb")])
        S.add("dve", lambda e: e.memset(onesm[:], 1.0 / D), writes=[bf("onesm")])
        S.add("dve", lambda e: e.memset(regA[:], 0.0), writes=[bf("QTz"), bf("sT"), bf("mT")])
        S.add("dve", lambda e: e.memset(KT[:], 0.0), writes=[bf("KT%d" % i) for i in range(NB + 1)])
        S.add("dve", lambda e: e.memset(Vaug[:, 0, :], 0.0), writes=[bf("V0")])
        S.add("dve", lambda e: e.memset(Vaug[:, 1:NB + 1, :], 1.0), writes=[bf("V%d" % i) for i in range(1, NB + 1)])
        S.add("dve", lambda e: e.memset(halo_h[:], 0.0), writes=[bf("halo_h")])
        S.add("dve", lambda e: e.scalar_tensor_tensor(out=gqkp[:], in0=gqk[:, 0:1], scalar=HD ** -0.5, in1=gqk[:, 1:2], op0=ALU.mult, op1=ALU.mult), reads=[bf("gqk")], writes=[bf("gqkp")])
        S.add("act", lambda e: e.activation(out=es[:], in_=es[:], func=AF.Exp), reads=[bf("es")], writes=[bf("es")])
        for hg in range(4):
            S.add("sp", lambda e, s, hg=hg: e.dma_start(out=bias_sb[:], in_=bias_d[:, hg * 1024:(hg + 1) * 1024]).then_inc(s, 16),
                  writes=[bf("bias_sb"), bf("hT")], dkey="c4")
            S.add("act", lambda e, hg=hg: e.activation(out=EB[:, hg * 1024:(hg + 1) * 1024], in_=bias_sb[:], func=AF.Exp),
                  reads=[bf("bias_sb"), bf("hT")], writes=[bf("EB")])
        for kc in range(8):
            bk = nextbank()
            S.add("pe", lambda e, kc=kc, bk=bk: e.transpose(out=ps[bk][:, 0:NCOLP], in_=rowp[:, kc * 128:(kc + 1) * 128],
                                                        identity=ident_f[0:NCOLP, 0:NCOLP]),
                  reads=[bf("rowp"), bf("identf"), bf("hT")], writes=[pB[bk]])
            S.add("dve", lambda e, kc=kc, bk=bk: e.tensor_copy(out=colp[:, kc, :], in_=ps[bk][:, 0:NCOLP]),
                  reads=[pB[bk]], writes=[bf("colp")])
        S.add("dve", lambda e: e.tensor_scalar(out=colh[:], in0=colp[:], scalar1=0.5, scalar2=None, op0=ALU.mult),
              reads=[bf("colp")], writes=[bf("colh")])
        CP = [bf("colp"), bf("colh")]

        def unit_srcs():
            def colpanel(w, c0):
                return w.rearrange("(kc p) n -> p kc n", p=128)[:, :, c0:c0 + 512]
            u = []
            for c0 in (0, 512, 1024):
                u.append(colpanel(w_in_d, c0))
            for j in range(2):
                u.append(colpanel(w_in_d, V_END + j * 512))
                u.append(colpanel(w_in_d, V_END + 1024 + j * 512))
            for j in range(2):
                u.append(colpanel(w_ao_d, j * 512))
                u.append(colpanel(w_in_d, GLU_END + j * 512))
            for j in range(2):
                u.append(colpanel(w_co_d, j * 512))
                u.append(colpanel(w_in_d, GLU_END + 1024 + j * 512))
            for j in range(2):
                u.append(colpanel(w_out_d, j * 512))
            for f in range(8):
                u.append(colpanel(w_ff1_d, f * 512))
            for j2 in range(2):
                for g in range(4):
                    u.append(w_ff2_d[g * 1024:(g + 1) * 1024, j2 * 512:(j2 + 1) * 512].rearrange("(kc p) n -> p kc n", p=128))
            return u
        USRC = unit_srcs()
        NU = len(USRC)
        assert NU == 33
        total_units = ntiles * NU
        issued = [0]
        taken = [0]
        released = [0]

        def issue_fetch():
            n = issued[0]
            if n >= total_units:
                return
            issued[0] += 1
            tile, u = divmod(n, NU)
            s = n % NSLOT
            wb = bf("wsl%d" % s)
            if tile == 0:
                S.add("pool", lambda e, sem, u=u, s=s: e.dma_start(out=wsl[s][:], in_=USRC[u]).then_inc(sem, 16),
                      writes=[wb], dkey="wq%d" % s)
                if ntiles > 1:
                    S.add("sp", lambda e, sem, u=u, s=s: e.dma_start(out=wscr[u], in_=wsl[s][:].rearrange("p k n -> p (k n)")).then_inc(sem, 16),
                          reads=[wb], writes=[bf("scr%d" % u)], dkey="ws%d" % s)
            else:
                S.add("sp", lambda e, sem, u=u, s=s: e.dma_start(out=wsl[s][:].rearrange("p k n -> p (k n)"), in_=wscr[u]).then_inc(sem, 16),
                      reads=[bf("scr%d" % u)], writes=[wb], dkey="wl%d" % s)

        def topup():
            while issued[0] < min(total_units, released[0] + NSLOT):
                issue_fetch()

        def next_unit():
            topup()
            assert taken[0] < issued[0]
            s = taken[0] % NSLOT
            taken[0] += 1
            return wsl[s], bf("wsl%d" % s)

        def release(n):
            released[0] += n
            topup()

        def mm_group(bank, col0, ncol, lhs_fn, rhs_fn, nk, reads, extra_w=()):
            def fn(e):
                i = None
                for k in range(nk):
                    i = e.matmul(ps[bank][:, col0:col0 + ncol], lhsT=lhs_fn(k), rhs=rhs_fn(k), start=(k == 0), stop=(k == nk - 1))
                return i
            return S.add("pe", fn, reads=reads, writes=[pB[bank]] + list(extra_w))

        UT = [bf("uT%d" % j) for j in range(NB)]

        def rms_sq(j, x_t, xbufs):
            S.add("act", lambda e, j=j: e.activation(out=junk, in_=x_t[:, j, :], func=AF.Square, accum_out=ss1[:, j:j + 1]),
                  reads=[xbufs[j]], writes=[bf("sq"), bf("ss1_%d" % j)])

        def rms_head(c0=0, c1=NB):
            rb = [bf("r1_%d" % j) for j in range(c0, c1)]
            S.add("act", lambda e: e.activation(out=r1[:, c0:c1], in_=ss1[:, c0:c1], func=AF.Sqrt, bias=EPS, scale=1.0 / D),
                  reads=[bf("ss1_%d" % j) for j in range(c0, c1)], writes=rb)
            S.add("dve", lambda e: e.reciprocal(out=r1[:, c0:c1], in_=r1[:, c0:c1]), reads=rb, writes=rb)

        def rms_finish(gi, x_t, xbufs, blocks=range(NB), head=True):
            if head:
                rms_head()
            for j in blocks:
                a = alt("xn")
                S.add("dve", lambda e, j=j, a=a: e.scalar_tensor_tensor(out=xnb[a][:], in0=x_t[:, j, :], scalar=r1[:, j:j + 1], in1=gbc[gi][:], op0=ALU.mult, op1=ALU.mult),
                      reads=[xbufs[j], bf("r1_%d" % j), bf("gbc")], writes=[bf("xn%d" % a)])
                bk = nextbank()

                def tr(e, bk=bk, a=a):
                    i = None
                    for kc in range(8):
                        i = e.transpose(out=psb[bk][:, kc * 128:(kc + 1) * 128], in_=xnb[a][:, kc * 128:(kc + 1) * 128], identity=ident_b[:])
                    return i
                S.add("pe", tr, reads=[bf("xn%d" % a), bf("identb")], writes=[pB[bk]])
                S.add("act", lambda e, j=j, bk=bk: e.activation(out=uT[:, :, j * 128:(j + 1) * 128], in_=psb[bk][:].rearrange("p (c t) -> p c t", c=8), func=AF.Copy),
                      reads=[pB[bk]], writes=[UT[j]])

        out_tokens = []
        for tile in range(ntiles):
            t0 = tile * T
            x_t = x_tb[tile % 2]
            xb = [bf("x%d_%d" % (tile % 2, j)) for j in range(NB)]

            def load_x(tl):
                xt2 = x_tb[tl % 2]
                xb2 = [bf("x%d_%d" % (tl % 2, j)) for j in range(NB)]
                tt0 = tl * T
                S.add("sp", lambda e, s: e.dma_start(out=xt2[:], in_=x_d[tt0:tt0 + T, :].rearrange("(j p) d -> p j d", p=128)).then_inc(s, 16),
                      writes=xb2, dkey="xl%d" % (tl % 2))

            def rms1_of(tl, part="all", blocks=range(NB)):
                xt2 = x_tb[tl % 2]
                xb2 = [bf("x%d_%d" % (tl % 2, j)) for j in range(NB)]
                if part in ("all", "head"):
                    for j in range(NB):
                        rms_sq(j, xt2, xb2)
                if part == "all":
                    rms_finish(0, xt2, xb2)
                elif part == "head":
                    rms_head()
                else:
                    rms_finish(0, xt2, xb2, blocks=blocks, head=False)
            if tile == 0:
                load_x(0)
                rms1_of(0)

            def build_dg(c):
                a = c % 2
                S.add("pool", lambda e, a=a, c=c: e.tensor_tensor(out=dg[a][:], in0=ident_b[:].unsqueeze(1).to_broadcast([128, CW, 128]),
                                                                   in1=colh[:, c, 5:5 + CW].unsqueeze(2).to_broadcast([128, CW, 128]), op=ALU.mult),
                      reads=[bf("identb")] + CP, writes=[bf("dg%d" % a)])

            build_dg(0)
            build_dg(1)
            rot[0] = 4
            wq0, wq0b = next_unit()
            wq1, wq1b = next_unit()
            wkv, wkvb = next_unit()
            qkv_banks = {}

            def qkv_mm(j):
                banks = []
                for (w, wb) in ((wq0, wq0b), (wq1, wq1b), (wkv, wkvb)):
                    bk = nextbank()
                    banks.append(bk)
                    mm_group(bk, 0, 512, lambda k, j=j: uT[:, k, j * 128:(j + 1) * 128], lambda k, w=w: w[:, k, :], 8, [UT[j], wb])
                qkv_banks[j] = banks
                reserved.update(banks)

            def qkv_post(j, part="ab"):
                bq0, bq1, bkv = qkv_banks[j]
                if "a" in part:
                    qkv_post_a(j, bq0, bq1, bkv)
                if "b" in part:
                    qkv_post_b(j, bq0, bq1, bkv)

            def qkv_post_a(j, bq0, bq1, bkv):
                for bi, bk in enumerate((bq0, bq1)):
                    sqx, sqB = ((sq, bf("sq")), (sq2, bf("sq2")))[bi]
                    S.add("act", lambda e, bk=bk, sqx=sqx: e.activation(out=sqx[:], in_=ps[bk][:], func=AF.Square), reads=[pB[bk]], writes=[sqB])
                    S.add("dve", lambda e, bi=bi, sqx=sqx: e.tensor_reduce(out=ssq[:, bi * 8:(bi + 1) * 8], in_=sqx[:].rearrange("p (h d) -> p h d", d=HD), axis=AX.X, op=ALU.add),
                          reads=[sqB], writes=[bf("ssq")])
                S.add("act", lambda e, bk=bkv: e.activation(out=sq[:, 0:256], in_=ps[bk][:, 0:256], func=AF.Square), reads=[pB[bkv]], writes=[bf("sq")])
                S.add("dve", lambda e: e.tensor_reduce(out=ssq[:, 16:20], in_=sq[:, 0:256].rearrange("p (h d) -> p h d", d=HD), axis=AX.X, op=ALU.add),
                      reads=[bf("sq")], writes=[bf("ssq")])
                S.add("act", lambda e: e.activation(out=rqk[:], in_=ssq[:], func=AF.Sqrt, bias=EPS, scale=1.0 / HD), reads=[bf("ssq")], writes=[bf("rqk")])
                S.add("dve", lambda e: e.reciprocal(out=rqk[:], in_=rqk[:]), reads=[bf("rqk")], writes=[bf("rqk")])
                for hi, bk in enumerate((bq0, bq1)):
                    S.add("dve", lambda e, hi=hi, bk=bk: e.tensor_tensor(
                        out=qn[:, hi * 512:(hi + 1) * 512].rearrange("p (lo mid d) -> p mid lo d", lo=4, mid=2, d=HD),
                        in0=ps[bk][:].rearrange("p (mid lo d) -> p mid lo d", mid=2, lo=4, d=HD),
                        in1=rqk[:, hi * 8:(hi + 1) * 8].rearrange("p (mid lo) -> p mid lo", mid=2).unsqueeze(3).to_broadcast([128, 2, 4, HD]),
                        op=ALU.mult), reads=[pB[bk], bf("rqk")], writes=[bf("qn")])
                S.add("dve", lambda e, bk=bkv: e.tensor_tensor(
                    out=kn[:].rearrange("p (h d) -> p h d", d=HD), in0=ps[bk][:, 0:256].rearrange("p (h d) -> p h d", d=HD),
                    in1=rqk[:, 16:20].unsqueeze(2).to_broadcast([128, 4, HD]), op=ALU.mult), reads=[pB[bkv], bf("rqk")], writes=[bf("kn")])
                S.add("act", lambda e, j=j, bk=bkv: e.activation(out=Vaug[:, j + 1, :].rearrange("p (h d) -> p h d", d=65)[:, :, 0:64],
                                                                 in_=ps[bk][:, 256:512].rearrange("p (h d) -> p h d", d=HD), func=AF.Copy),
                      reads=[pB[bkv]], writes=[bf("V%d" % (j + 1))])

            def qkv_post_b(j, bq0, bq1, bkv):
                bt = nextbank()

                def trq(e, bt=bt):
                    i = None
                    for c in range(8):
                        i = e.transpose(out=psb[bt][:, c * 128:(c + 1) * 128], in_=qn[:, c * 128:(c + 1) * 128], identity=ident_b[:])
                    return i
                S.add("pe", trq, reads=[bf("qn"), bf("identb")], writes=[pB[bt]])
                for mid in range(2):
                    eng = "act" if mid == 0 else "dve"
                    dst = QTz[mid * 64:(mid + 1) * 64, :, j * 128:(j + 1) * 128].rearrange("p (hi m lo) t -> p hi m lo t", hi=2, m=2, lo=4)[:, :, mid, :, :]
                    src = psb[bt][mid * 64:(mid + 1) * 64, :].rearrange("p (hi lo t) -> p hi lo t", hi=2, lo=4)
                    if eng == "act":
                        S.add("act", lambda e, dst=dst, src=src: e.activation(out=dst, in_=src, func=AF.Copy), reads=[pB[bt]], writes=[bf("QTz")])
                    else:
                        S.add("dve", lambda e, dst=dst, src=src: e.tensor_copy(out=dst, in_=src), reads=[pB[bt]], writes=[bf("QTz")])
                bt2 = nextbank()

                def trk(e, bt2=bt2):
                    i = None
                    for c in range(2):
                        i = e.transpose(out=psb[bt2][:, c * 128:(c + 1) * 128], in_=kn[:, c * 128:(c + 1) * 128], identity=ident_b[:])
                    return i
                S.add("pe", trk, reads=[bf("kn"), bf("identb")], writes=[pB[bt2]])
                S.add("act", lambda e, j=j, bt2=bt2: e.activation(out=KT[:, :, (j + 1) * 128:(j + 2) * 128], in_=psb[bt2][:, 0:256].rearrange("p (g t) -> p g t", g=2),
                                                                   func=AF.Identity, scale=gqkp[:, 0:1]),
                      reads=[pB[bt2], bf("gqkp")], writes=[bf("KT%d" % (j + 1))])
                reserved.difference_update(qkv_banks[j])

            def glu_gen():
                S.add("pool", lambda e: e.tensor_copy(out=hbuf[:, :, 0:32], in_=halo_h[:]), reads=[bf("halo_h")], writes=[bf("hbuf%d" % c_) for c_ in range(8)] + [bf("hT")] + [bf("hT_%d" % f) for f in range(8)])
                for half in range(2):
                    wa, wab = next_unit()
                    wg, wgb = next_unit()
                    for o4 in range(4):
                        c = half * 4 + o4
                        ba = nextbank()
                        mm_group(ba, 0, T, lambda k, wa=wa, o4=o4: wa[:, k, o4 * 128:(o4 + 1) * 128], lambda k: uT[:, k, :], 8, UT + [wab])
                        bg = nextbank()
                        mm_group(bg, 0, T, lambda k, wg=wg, o4=o4: wg[:, k, o4 * 128:(o4 + 1) * 128], lambda k: uT[:, k, :], 8, UT + [wgb])
                        a = alt("tn")
                        S.add("act", lambda e, bg=bg, a=a: e.activation(out=tnb[a][:], in_=ps[bg][:], func=AF.Tanh, scale=0.5), reads=[pB[bg]], writes=[bf("tn%d" % a)])
                        S.add("dve", lambda e, ba=ba, a=a, c=c: e.scalar_tensor_tensor(out=hbuf[:, c, 32:32 + T], in0=tnb[a][:], scalar=1.0, in1=ps[ba][:], op0=ALU.add, op1=ALU.mult),
                              reads=[bf("tn%d" % a), pB[ba]], writes=[bf("hbuf%d" % c)])
                        yield
                    release(2)
                S.add("pool", lambda e: e.tensor_copy(out=halo_h[:], in_=hbuf[:, :, T:T + 32]), reads=[bf("hbuf%d" % c_) for c_ in range(8)], writes=[bf("halo_h")])


            def drain(g, n=100):
                for _ in range(n):
                    try:
                        next(g)
                    except StopIteration:
                        return
            qkv_mm(0)
            qkv_mm(1)
            qkv_post(0)
            qkv_mm(2)
            qkv_post(1)
            qkv_mm(3)
            release(3)
            qkv_post(2)
            qkv_post(3, "a")
            gg = glu_gen()
            drain(gg, 3)
            qkv_post(3, "b")
            drain(gg)
            def gated_proj_gen(actT, actbuf, first, held=None, nhold=0):
                nh = [nhold]
                for half in range(2):
                    wp, wpb = next_unit()
                    wg, wgb = next_unit()
                    for o4 in range(4):
                        oc = half * 4 + o4
                        bp = nextbank()
                        mm_group(bp, 0, T, lambda k, wp=wp, o4=o4: wp[:, k, o4 * 128:(o4 + 1) * 128], lambda k: actT[:, k, :], 8, [actbuf, wpb])
                        bg = nextbank()
                        mm_group(bg, 0, T, lambda k, wg=wg, o4=o4: wg[:, k, o4 * 128:(o4 + 1) * 128], lambda k: uT[:, k, :], 8, UT + [wgb])
                        def evac(bp=bp, bg=bg, oc=oc):
                            a = alt("tn")
                            S.add("act", lambda e, bg=bg, a=a: e.activation(out=tnb[a][:], in_=ps[bg][:], func=AF.Tanh, scale=0.5), reads=[pB[bg]], writes=[bf("tn%d" % a)])
                            if first:
                                S.add("dve", lambda e, bp=bp, a=a, oc=oc: e.scalar_tensor_tensor(out=mT[:, oc, :], in0=tnb[a][:], scalar=1.0, in1=ps[bp][:], op0=ALU.add, op1=ALU.mult),
                                      reads=[bf("tn%d" % a), pB[bp]], writes=[bf("mT"), bf("QTz")])
                            else:
                                S.add("dve", lambda e, bp=bp, a=a: e.scalar_tensor_tensor(out=tnb[a][:], in0=tnb[a][:], scalar=1.0, in1=ps[bp][:], op0=ALU.add, op1=ALU.mult),
                                      reads=[bf("tn%d" % a), pB[bp]], writes=[bf("tn%d" % a)])
                                S.add("dve", lambda e, a=a, oc=oc: e.tensor_tensor(out=mT[:, oc, :], in0=tnb[a][:], in1=mT[:, oc, :], op=ALU.add),
                                      reads=[bf("tn%d" % a), bf("mT")], writes=[bf("mT")])
                        if held is not None and nh[0] > 0:
                            nh[0] -= 1
                            held.append(evac)
                        else:
                            evac()
                        yield
                    release(2)

            def drain(g, n=100):
                for _ in range(n):
                    try:
                        next(g)
                    except StopIteration:
                        return

            CB = 4

            def conv_steps():
                for c in range(8):
                    a = c % 2
                    for j0 in range(0, CW, 4):
                        j1 = min(CW, j0 + 4)

                        def cm(e, c=c, a=a, j0=j0, j1=j1):
                            i = None
                            for jt in range(j0, j1):
                                i = e.matmul(ps[CB][:], lhsT=dg[a][:, jt, :], rhs=hbuf[:, c, 2 + jt:2 + jt + T], start=(jt == 0), stop=(jt == CW - 1))
                            return i
                        S.add("pe", cm, reads=[bf("dg%d" % a), bf("hbuf%d" % c)], writes=[pB[CB]])
                        if j1 == CW:
                            S.add("act", lambda e, c=c: e.activation(out=cv[:, c, :], in_=ps[CB][:], func=AF.Identity, bias=colp[:, c, 2:3]),
                                  reads=[pB[CB]] + CP, writes=[bf("cv%d" % c)])
                            if c + 2 < 8:
                                build_dg(c + 2)
                        yield
            cgen = conv_steps()
            pool_now[0] = (0, 1, 2, 3)

            def conv_step(n=1):
                for _ in range(n):
                    try:
                        next(cgen)
                    except StopIteration:
                        return
            OB = (5, 6, 7)

            def ohead(h):
                return OB[h // 7], (h % 7) * 65
            deferred = []
            for j in range(NB):
                pend = []
                for pr in range(8):
                    if pr == 3 and deferred:
                        deferred.pop(0)()
                    h0 = 2 * pr
                    bk = nextbank()

                    def smm(e, bk=bk, h0=h0, j=j):
                        i = None
                        for hl in range(2):
                            h = h0 + hl
                            gp = (h // 4) // 2
                            for kb in range(2):
                                c = (hl * 2 + kb) * 128
                                i = e.matmul(ps[bk][:, c:c + 128], lhsT=KT[:, gp, (j + kb) * 128:(j + kb + 1) * 128],
                                             rhs=QTz[:, h, j * 128:(j + 1) * 128], start=True, stop=True)
                        return i
                    S.add("pe", smm, reads=[bf("QTz"), bf("KT%d" % j), bf("KT%d" % (j + 1))], writes=[pB[bk]])
                    a = alt("E", 3)
                    S.add("act", lambda e, bk=bk, a=a: e.activation(out=Eb[a][:], in_=ps[bk][:], func=AF.Exp), reads=[pB[bk]], writes=[bf("E%d" % a)])
                    S.add("dve", lambda e, a=a, h0=h0: e.tensor_tensor(out=PTb[a][:], in0=Eb[a][:], in1=EB[:, h0 * 256:(h0 + 2) * 256], op=ALU.mult),
                          reads=[bf("E%d" % a), bf("EB")], writes=[bf("PT%d" % a)])
                    pend.append((h0, a))
                    conv_step(2)
                    if len(pend) == 3 or pr == 7:
                        todo = pend[:1] if pr < 7 else pend
                        pend = pend[1:] if pr < 7 else []
                        for (hh0, aa) in todo:
                            def pv(e, hh0=hh0, aa=aa, j=j):
                                i = None
                                for hl in range(2):
                                    h = hh0 + hl
                                    g = h // 4
                                    ob, off = ohead(h)
                                    for kb in range(2):
                                        c = (hl * 2 + kb) * 128
                                        i = e.matmul(ps[ob][:, off:off + 65], lhsT=PTb[aa][:, c:c + 128], rhs=Vaug[:, j + kb, g * 65:(g + 1) * 65],
                                                     start=(kb == 0), stop=(kb == 1))
                                return i
                            obs = sorted(set(ohead(hh0 + hl)[0] for hl in range(2)))
                            S.add("pe", pv, reads=[bf("PT%d" % aa), bf("V%d" % j), bf("V%d" % (j + 1))], writes=[pB[o] for o in obs])
                for bi, ob in enumerate(OB):
                    hs = bi * 7
                    nh = 7 if bi < 2 else 2
                    ov = ps[ob][:, 0:nh * 65].rearrange("p (h d) -> p h d", d=65)
                    S.add("dve", lambda e, ov=ov, hs=hs, nh=nh: e.tensor_tensor(out=den[:, hs:hs + nh].unsqueeze(2), in0=ov[:, :, 64:65], in1=es[:, hs:hs + nh].unsqueeze(2), op=ALU.add),
                          reads=[pB[ob], bf("es")], writes=[bf("den")])
                    S.add("dve", lambda e, hs=hs, nh=nh: e.reciprocal(out=rden[:, hs:hs + nh], in_=den[:, hs:hs + nh]), reads=[bf("den")], writes=[bf("rden")])
                    S.add("dve", lambda e, ov=ov, hs=hs, nh=nh: e.tensor_tensor(out=attn_n[:, hs * 64:(hs + nh) * 64].rearrange("p (h d) -> p h d", d=HD), in0=ov[:, :, 0:64],
                                                                               in1=rden[:, hs:hs + nh].unsqueeze(2).to_broadcast([128, nh, HD]), op=ALU.mult),
                          reads=[pB[ob], bf("rden")], writes=[bf("attn_n")])
                def finish_blk(j=j):
                    bt = nextbank()

                    def tra(e, bt=bt):
                        i = None
                        for c in range(8):
                            i = e.transpose(out=psb[bt][:, c * 128:(c + 1) * 128], in_=attn_n[:, c * 128:(c + 1) * 128], identity=ident_b[:])
                        return i
                    S.add("pe", tra, reads=[bf("attn_n"), bf("identb")], writes=[pB[bt]])
                    S.add("act", lambda e, j=j, bt=bt: e.activation(out=attnT[:, :, j * 128:(j + 1) * 128], in_=psb[bt][:].rearrange("p (c t) -> p c t", c=8), func=AF.Copy),
                          reads=[pB[bt]], writes=[bf("attnT")])
                deferred.append(finish_blk)
            while deferred:
                deferred.pop(0)()
            conv_step(1000)
            S.add("pool", lambda e: e.tensor_copy(out=KT[:, :, 0:128], in_=KT[:, :, NB * 128:(NB + 1) * 128]), reads=[bf("KT%d" % NB)], writes=[bf("KT0")])
            S.add("pool", lambda e: e.tensor_copy(out=Vaug[:, 0, :], in_=Vaug[:, NB, :]), reads=[bf("V%d" % NB)], writes=[bf("V0")])

            pool_now[0] = (0, 1, 2, 3, 6, 7)
            BM, BQ = 4, 5
            for c in range(8):
                a = alt("cvb")
                S.add("dve", lambda e, c=c, a=a: e.tensor_copy(out=cvb[a][:], in_=cv[:, c, :]), reads=[bf("cv%d" % c)], writes=[bf("cvb%d" % a)])
                S.add("act", lambda e, c=c, a=a: e.activation(out=sqb[a][:], in_=cv[:, c, :], func=AF.Square), reads=[bf("cv%d" % c)], writes=[bf("sqb%d" % a)])
                S.add("pe", lambda e, c=c, a=a: e.matmul(ps[BM][:], lhsT=onesm[:], rhs=cvb[a][:], start=(c == 0), stop=(c == 7)),
                      reads=[bf("cvb%d" % a), bf("onesm")], writes=[pB[BM]])
                S.add("pe", lambda e, c=c, a=a: e.matmul(ps[BQ][:], lhsT=onesm[:], rhs=sqb[a][:], start=(c == 0), stop=(c == 7)),
                      reads=[bf("sqb%d" % a), bf("onesm")], writes=[pB[BQ]])
            held = []
            ga = gated_proj_gen(attnT, bf("attnT"), True, held, 3)
            drain(ga, 3)
            S.add("act", lambda e: e.activation(out=mean_sb[:], in_=ps[BM][:], func=AF.Copy), reads=[pB[BM]], writes=[bf("mean")])
            S.add("dve", lambda e: e.tensor_tensor(out=nmr_sb[:], in0=mean_sb[:], in1=mean_sb[:], op=ALU.mult), reads=[bf("mean")], writes=[bf("nmr")])
            S.add("dve", lambda e: e.tensor_tensor(out=rstd_sb[:], in0=ps[BQ][:], in1=nmr_sb[:], op=ALU.subtract), reads=[pB[BQ], bf("nmr")], writes=[bf("rstd")])
            S.add("dve", lambda e: e.tensor_scalar(out=rstd_sb[:], in0=rstd_sb[:], scalar1=0.0, scalar2=None, op0=ALU.max), reads=[bf("rstd")], writes=[bf("rstd")])
            S.add("act", lambda e: e.activation(out=rstd_sb[:], in_=rstd_sb[:], func=AF.Ln, bias=EPS, scale=1.0), reads=[bf("rstd")], writes=[bf("rstd")])
            S.add("act", lambda e: e.activation(out=rstd_sb[:], in_=rstd_sb[:], func=AF.Exp, scale=-0.5), reads=[bf("rstd")], writes=[bf("rstd")])
            S.add("dve", lambda e: e.scalar_tensor_tensor(out=nmr_sb[:], in0=mean_sb[:], scalar=-1.0, in1=rstd_sb[:], op0=ALU.mult, op1=ALU.mult),
                  reads=[bf("mean"), bf("rstd")], writes=[bf("nmr")])
            for ev in held:
                ev()
            ybs = [(ybuf[:], bf("ybuf")), (xnb[0][:].bitcast(F32), bf("xn0"))]
            zbs = [(zbuf[:], bf("zbuf")), (xnb[1][:].bitcast(F32), bf("xn1"))]
            def ln_pre(c):
                yb, ybB = ybs[c % 2]
                S.add("pool", lambda e, c=c, yb=yb: e.tensor_tensor(out=yb, in0=cv[:, c, :], in1=rstd_sb[:], op=ALU.mult), reads=[bf("cv%d" % c), bf("rstd")], writes=[ybB])
                S.add("dve", lambda e, yb=yb: e.tensor_tensor(out=yb, in0=yb, in1=nmr_sb[:], op=ALU.add), reads=[ybB, bf("nmr")], writes=[ybB])
            ln_pre(0)
            for c in range(8):
                yb, ybB = ybs[c % 2]
                zb, zbB = zbs[c % 2]
                if c + 1 < 8:
                    ln_pre(c + 1)
                S.add("act", lambda e, c=c, yb=yb, zb=zb: e.activation(out=zb, in_=yb, func=AF.Identity, scale=colh[:, c, 3:4], bias=colh[:, c, 4:5]),
                      reads=[ybB] + CP, writes=[zbB])
                a = alt("tn")
                S.add("act", lambda e, a=a, zb=zb: e.activation(out=tnb[a][:], in_=zb, func=AF.Tanh), reads=[zbB], writes=[bf("tn%d" % a)])
                S.add("dve", lambda e, a=a, c=c, zb=zb: e.scalar_tensor_tensor(out=sT[:, c, :], in0=tnb[a][:], scalar=1.0, in1=zb, op0=ALU.add, op1=ALU.mult),
                      reads=[bf("tn%d" % a), zbB], writes=[bf("sT"), bf("QTz")])
                drain(ga, 1)
            drain(ga)


            drain(gated_proj_gen(sT, bf("sT"), False))
            pool_now[0] = tuple(range(8))

            wo = [next_unit(), next_unit()]
            for j in range(NB):
                for half in range(2):
                    w, wb = wo[half]
                    bk = nextbank()
                    mm_group(bk, 0, 512, lambda k, j=j: mT[:, k, j * 128:(j + 1) * 128], lambda k, w=w: w[:, k, :], 8, [bf("mT"), wb])
                    S.add("dve", lambda e, j=j, half=half, bk=bk, x_t=x_t: e.scalar_tensor_tensor(out=x_t[:, j, half * 512:(half + 1) * 512], in0=ps[bk][:], scalar=0.5,
                                                                                      in1=x_t[:, j, half * 512:(half + 1) * 512], op0=ALU.mult, op1=ALU.add),
                          reads=[pB[bk], xb[j]], writes=[xb[j]])
                rms_sq(j, x_t, xb)
                if j == 2:
                    rms_head(0, 3)
            release(2)
            if tile + 1 < ntiles:
                load_x(tile + 1)
            if tile + 1 < ntiles:
                S.add("pool", lambda e: e.memset(regA[:], 0.0), reads=[], writes=[bf("QTz"), bf("sT"), bf("mT")])

            rms_finish(1, x_t, xb, blocks=[0, 1, 2], head=False)
            rms_head(3, 4)
            rms_finish(1, x_t, xb, blocks=[3], head=False)

            for f in range(8):
                if f == 4 and tile + 1 < ntiles:
                    rms1_of(tile + 1, "head")
                w, wb = next_unit()
                for o4 in range(4):
                    fc = f * 4 + o4
                    bk = nextbank()
                    mm_group(bk, 0, T, lambda k, w=w, o4=o4: w[:, k, o4 * 128:(o4 + 1) * 128], lambda k: uT[:, k, :], 8, UT + [wb])
                    a = alt("rl")
                    S.add("act", lambda e, bk=bk, a=a: e.activation(out=rl[a][:], in_=ps[bk][:], func=AF.Relu), reads=[pB[bk]], writes=[bf("rl%d" % a)])
                    S.add("pool", lambda e, a=a, fc=fc: e.tensor_tensor(out=hT[:, fc, :], in0=rl[a][:], in1=rl[a][:], op=ALU.mult),
                          reads=[bf("rl%d" % a)], writes=[bf("hT"), bf("hT_%d" % f)] + [bf("hbuf%d" % c_) for c_ in range(8)] + [bf("cv%d" % c) for c in range(8)])
                release(1)

            for j2 in range(2):
                for g in range(4):
                    if j2 == 0 and tile + 1 < ntiles:
                        pool_now[0] = (0, 1, 2, 3)
                        rms1_of(tile + 1, "body", blocks=[g])
                        pool_now[0] = tuple(range(8))
                    w, wb = next_unit()
                    for j in range(NB):
                        bk = (4 if j2 == 0 else 0) + j

                        def f2(e, w=w, g=g, j=j, bk=bk):
                            i = None
                            for k in range(8):
                                i = e.matmul(ps[bk][:], lhsT=hT[:, g * 8 + k, j * 128:(j + 1) * 128], rhs=w[:, k, :], start=(g == 0 and k == 0), stop=(g == 3 and k == 7))
                            return i
                        S.add("pe", f2, reads=[bf("hT_%d" % (2 * g)), bf("hT_%d" % (2 * g + 1)), wb], writes=[pB[bk]])
                    release(1)
                for j in range(NB):
                    bk = (4 if j2 == 0 else 0) + j
                    S.add("dve", lambda e, j=j, j2=j2, bk=bk, x_t=x_t: e.tensor_tensor(out=x_t[:, j, j2 * 512:(j2 + 1) * 512], in0=ps[bk][:], in1=x_t[:, j, j2 * 512:(j2 + 1) * 512], op=ALU.add),
                          reads=[pB[bk], xb[j]], writes=[xb[j]])
            tk = S.add("sp", lambda e, s, t0=t0, x_t=x_t: e.dma_start(out=out_d[t0:t0 + T, :].rearrange("(j p) d -> p j d", p=128), in_=x_t[:]).then_inc(s, 16),
                       reads=xb, dkey="st%d" % (tile % 2))
            out_tokens.append(tk)
        S.add("sp", None, extra=out_tokens[-2:])
        S.emit(nc, st)
    return nc


def t5_bucket(dist):
    n = np.maximum(dist, 0)
    max_exact = 16
    nf = np.maximum(n, 1).astype(np.float32)
    large = max_exact + (np.log(nf / np.float32(max_exact)) / np.float32(np.log(128 / max_exact)) * np.float32(32 - max_exact)).astype(np.int32)
    large = np.minimum(large, 31)
    return np.where(n < max_exact, n, large)


def host_layout(inputs, ntiles=SEQ // T, ncores=8):
    f = lambda a: np.ascontiguousarray(np.asarray(a, dtype=np.float32))
    x = f(inputs["x"])
    rowp = np.concatenate([f(inputs["norm_mix_g"]), f(inputs["norm_mlp_g"]), f(inputs["b_dw"]), f(inputs["conv_ln_g"]),
                           f(inputs["conv_ln_b"]), f(inputs["w_dw"])[0]], axis=0)
    gq = f(inputs["q_norm_g"])[0]
    gk = f(inputs["k_norm_g"])[0]
    gqk = np.stack([np.tile(gq, 2), np.tile(gk, 2)], axis=1)
    gqk = np.ascontiguousarray(gqk)
    sinks = np.ascontiguousarray(np.broadcast_to(f(inputs["attn_sinks"])[0][None, :], (128, NQ)))
    rb = f(inputs["rel_bias"])
    k = np.arange(128)[:, None]
    q = np.arange(128)[None, :]
    biasT = np.empty((128, NQ, 2, 128), np.float32)
    for kb in range(2):
        dist = q + 128 - (kb * 128 + k)
        valid = (dist >= 0) & (dist < 128)
        g = rb[t5_bucket(dist)]
        g = np.where(valid[:, :, None], g, np.float32(-1e30))
        biasT[:, :, kb, :] = np.transpose(g, (0, 2, 1))
    biasT = np.ascontiguousarray(biasT.reshape(128, NQ * 256))
    common = {
        "w_in": f(inputs["w_in"])[0], "w_attn_o": f(inputs["w_attn_o"])[0], "w_conv_out": f(inputs["w_conv_out"])[0],
        "w_out": f(inputs["w_out"])[0], "w_ff1": f(inputs["w_ff1"])[0], "w_ff2": f(inputs["w_ff2"])[0],
        "rowp": np.ascontiguousarray(rowp), "gqk": gqk, "sinks": sinks, "biasT": biasT,
        "ident": np.eye(128, dtype=np.float32),
    }
    maps = []
    for c in range(ncores):
        m = dict(common)
        m["x"] = np.ascontiguousarray(x[c, :ntiles * T])
        maps.append(m)
    return maps


def kernel(**inputs):
    ntiles = SEQ // T
    nc = build(ntiles)
    maps = host_layout(inputs, ntiles, 8)
    res = run_bass_kernel_spmd(nc, maps, core_ids=list(range(8)))
    return np.stack([r["out"] for r in res.results], axis=0).astype(np.float32)
```

```python
import numpy as np
from contextlib import ExitStack
import concourse.bass as bass
import concourse.mybir as mybir
from concourse.bass_utils import run_bass_kernel_spmd

F32 = mybir.dt.float32
BF16 = mybir.dt.bfloat16
AF = mybir.ActivationFunctionType
ALU = mybir.AluOpType
AX = mybir.AxisListType

D = 1024
SEQ = 8192
NB = 4
T = NB * 128
HD = 64
NQ = 16
NKV = 4
CW = 31
DFF = 4096
EPS = 1e-6
V_END = 1536
GLU_END = V_END + 2048
IN_W = GLU_END + 2048
NCOLP = 5 + CW
NSLOT = 4
ENGS = ("pe", "act", "dve", "pool", "sp")


class Buf:
    __slots__ = ("name", "w", "r")

    def __init__(self, name):
        self.name = name
        self.w = None
        self.r = []


class Sched:
    def __init__(self):
        self.ops = {e: [] for e in ENGS}
        self.cnt = {}
        self.seen = {e: {} for e in ENGS}
        self.keys = []

    def _bump(self, key, n):
        if key not in self.cnt:
            self.cnt[key] = 0
            self.keys.append(key)
        self.cnt[key] += n
        return (key, self.cnt[key])

    def add(self, eng, fn, reads=(), writes=(), dkey=None, ndma=1, extra=()):
        deps = {}

        def need(t):
            if t is None:
                return
            k, v = t
            if eng == "pe" and k == "pe":
                return
            if deps.get(k, 0) < v:
                deps[k] = v
        for b in reads:
            need(b.w)
        for b in writes:
            need(b.w)
            for t in b.r:
                need(t)
        for t in extra:
            need(t)
        waits = []
        seen = self.seen[eng]
        for k, v in deps.items():
            if seen.get(k, 0) >= v:
                continue
            seen[k] = v
            waits.append((k, v))
        if fn is None:
            tok = None
        elif dkey is not None:
            tok = self._bump(dkey, 16 * ndma)
        else:
            tok = self._bump(eng, 1)
        self.ops[eng].append((waits, fn, tok, dkey is not None))
        if tok is not None:
            for b in reads:
                b.r.append(tok)
            for b in writes:
                b.w = tok
                b.r = []
        return tok

    def emit(self, nc, stack):
        sems = {}
        for i, k in enumerate(self.keys):
            sems[k] = stack.enter_context(nc.semaphore("s%d" % i))
        block = stack.enter_context(nc.Block())
        handles = {"pe": block.tensor, "act": block.scalar, "dve": block.vector,
                   "pool": block.gpsimd, "sp": block.sync}
        for eng in ENGS:
            ops = self.ops[eng]
            if not ops:
                continue

            def body(e, ops=ops):
                for waits, fn, tok, is_dma in ops:
                    for k, v in waits:
                        e.wait_ge(sems[k], v)
                    if fn is None:
                        continue
                    if is_dma:
                        fn(e, sems[tok[0]])
                    else:
                        fn(e).then_inc(sems[tok[0]], 1)
            handles[eng](body)


def build(ntiles):
    seq = ntiles * T
    nc = bass.Bass("TRN2", target_bir_lowering=False)
    x_d = nc.dram_tensor("x", [seq, D], F32, kind="ExternalInput").ap()
    w_in_d = nc.dram_tensor("w_in", [D, IN_W], F32, kind="ExternalInput").ap()
    w_ao_d = nc.dram_tensor("w_attn_o", [D, D], F32, kind="ExternalInput").ap()
    w_co_d = nc.dram_tensor("w_conv_out", [D, D], F32, kind="ExternalInput").ap()
    w_out_d = nc.dram_tensor("w_out", [D, D], F32, kind="ExternalInput").ap()
    w_ff1_d = nc.dram_tensor("w_ff1", [D, DFF], F32, kind="ExternalInput").ap()
    w_ff2_d = nc.dram_tensor("w_ff2", [DFF, D], F32, kind="ExternalInput").ap()
    rowp_d = nc.dram_tensor("rowp", [NCOLP, D], F32, kind="ExternalInput").ap()
    gqk_d = nc.dram_tensor("gqk", [128, 2], F32, kind="ExternalInput").ap()
    sink_d = nc.dram_tensor("sinks", [128, NQ], F32, kind="ExternalInput").ap()
    bias_d = nc.dram_tensor("biasT", [128, NQ * 2 * 128], F32, kind="ExternalInput").ap()
    ident_d = nc.dram_tensor("ident", [128, 128], F32, kind="ExternalInput").ap()
    out_d = nc.dram_tensor("out", [seq, D], F32, kind="ExternalOutput").ap()
    wscr = nc.dram_tensor("wscr", [33, 128, 8 * 512], BF16).ap()

    S = Sched()
    with ExitStack() as st:
        def sb(name, shape, dt):
            return st.enter_context(nc.sbuf_tensor(name, shape, dt))

        x_tb = [sb("x_t%d" % i, [128, NB, D], F32) for i in range(2)]
        uT = sb("uT", [128, 8, T], BF16)
        xnb = [sb("xn%d" % i, [128, D], BF16) for i in range(2)]
        sq = sb("sq", [128, 512], F32)
        sq2 = sb("sq2", [128, 512], F32)
        junk = sq[:].bitcast(BF16)
        gbc = [sb("gbc%d" % i, [128, D], F32) for i in range(2)]
        qn = sb("qn", [128, D], BF16)
        kn = sb("kn", [128, 256], BF16)
        regA = sb("regA", [128, NQ * T], BF16)
        QTz = regA[:].rearrange("p (h t) -> p h t", h=NQ)
        sT = regA[:, 0:8 * T].rearrange("p (c t) -> p c t", c=8)
        mT = regA[:, 8 * T:16 * T].rearrange("p (c t) -> p c t", c=8)
        KT = sb("KT", [128, 2, (NB + 1) * 128], BF16)
        Vaug = sb("Vaug", [128, NB + 1, NKV * 65], BF16)
        Eb = [sb("E%d" % i, [128, 512], BF16) for i in range(3)]
        PTb = [sb("PT%d" % i, [128, 512], BF16) for i in range(3)]
        EB = sb("EB", [128, NQ * 256], BF16)
        attn_n = sb("attn_n", [128, D], BF16)
        attnT = sb("attnT", [128, 8, T], BF16)
        tnb = [sb("tn%d" % i, [128, 512], F32) for i in range(2)]
        regB = sb("regB", [128, 16 * 1024], BF16)
        hT = regB[:, 0:32 * T].rearrange("p (c t) -> p c t", c=32)
        regBf = regB[:].bitcast(F32)
        hbuf = regB[:, 0:8 * (T + 32)].rearrange("p (c t) -> p c t", c=8)
        cv = regBf[:, 4 * (T + 32):4 * (T + 32) + 8 * T].rearrange("p (c t) -> p c t", c=8)
        dg = [sb("dg%d" % i, [128, CW, 128], BF16) for i in range(2)]
        cvb = [sb("cvb%d" % i, [128, T], BF16) for i in range(2)]
        sqb = [sb("sqb%d" % i, [128, T], BF16) for i in range(2)]
        mean_sb = sb("mean_sb", [128, T], F32)
        rstd_sb = sb("rstd_sb", [128, T], F32)
        nmr_sb = sb("nmr_sb", [128, T], F32)
        ybuf = sb("ybuf", [128, T], F32)
        zbuf = sb("zbuf", [128, T], F32)
        rl = [sb("rl%d" % i, [128, T], BF16) for i in range(2)]
        wsl = [sb("wsl%d" % i, [128, 8, 512], BF16) for i in range(NSLOT)]
        ident_f = sb("ident_f", [128, 128], F32)
        ident_b = sb("ident_b", [128, 128], BF16)
        onesm = sb("onesm", [128, 128], BF16)
        rowp = regBf[0:NCOLP, 4096:4096 + D]
        colp = sb("colp", [128, 8, NCOLP], F32)
        colh = sb("colh", [128, 8, NCOLP], F32)
        gqk = sb("gqk_sb", [128, 2], F32)
        gqkp = sb("gqkp", [128, 1], F32)
        es = sb("es", [128, NQ], F32)
        bias_sb = regBf[:, 0:1024]
        ss1 = sb("ss1", [128, NB], F32)
        r1 = sb("r1", [128, NB], F32)
        ssq = sb("ssq", [128, 20], F32)
        rqk = sb("rqk", [128, 20], F32)
        den = sb("den", [128, NQ], F32)
        rden = sb("rden", [128, NQ], F32)
        halo_h = sb("halo_h", [128, 8, 32], BF16)
        ps = [st.enter_context(nc.psum_tensor("ps%d" % i, [128, 512], F32)) for i in range(8)]
        psb = [p[:].bitcast(BF16) for p in ps]

        B = {}

        def bf(name):
            if name not in B:
                B[name] = Buf(name)
            return B[name]
        pB = [bf("ps%d" % i) for i in range(8)]
        rot = [0]

        pool_now = [tuple(range(8))]

        reserved = set()

        def nextbank():
            p = pool_now[0]
            while True:
                i = p[rot[0] % len(p)]
                rot[0] += 1
                if i not in reserved:
                    return i
        tgl = {}

        def alt(name, n=2):
            tgl[name] = tgl.get(name, -1) + 1
            return tgl[name] % n

        S.add("sp", lambda e, s: e.dma_start(out=ident_f[:], in_=ident_d).then_inc(s, 16), writes=[bf("identf")], dkey="c0")
        S.add("sp", lambda e, s: e.dma_start(out=rowp, in_=rowp_d).then_inc(s, 16), writes=[bf("rowp"), bf("hT")], dkey="c1")
        S.add("sp", lambda e, s: e.dma_start(out=gqk[:], in_=gqk_d).then_inc(s, 16), writes=[bf("gqk")], dkey="c2")
        S.add("sp", lambda e, s: e.dma_start(out=es[:], in_=sink_d).then_inc(s, 16), writes=[bf("es")], dkey="c3")
        for gi in range(2):
            S.add("sp", lambda e, s, gi=gi: e.dma_start(out=gbc[gi][:], in_=rowp_d[gi:gi + 1, :].partition_broadcast(128)).then_inc(s, 16),
                  writes=[bf("gbc")], dkey="c5%d" % gi)
        S.add("dve", lambda e: e.tensor_copy(out=ident_b[:], in_=ident_f[:]), reads=[bf("identf")], writes=[bf("identb")])
        S.add("dve", lambda e: e.memset(onesm[:], 1.0 / D), writes=[bf("onesm")])
        S.add("dve", lambda e: e.memset(regA[:], 0.0), writes=[bf("QTz"), bf("sT"), bf("mT")])
        S.add("dve", lambda e: e.memset(KT[:], 0.0), writes=[bf("KT%d" % i) for i in range(NB + 1)])
        S.add("dve", lambda e: e.memset(Vaug[:, 0, :], 0.0), writes=[bf("V0")])
        S.add("dve", lambda e: e.memset(Vaug[:, 1:NB + 1, :], 1.0), writes=[bf("V%d" % i) for i in range(1, NB + 1)])
        S.add("dve", lambda e: e.memset(halo_h[:], 0.0), writes=[bf("halo_h")])
        S.add("dve", lambda e: e.scalar_tensor_tensor(out=gqkp[:], in0=gqk[:, 0:1], scalar=HD ** -0.5, in1=gqk[:, 1:2], op0=ALU.mult, op1=ALU.mult), reads=[bf("gqk")], writes=[bf("gqkp")])
        S.add("act", lambda e: e.activation(out=es[:], in_=es[:], func=AF.Exp), reads=[bf("es")], writes=[bf("es")])
        for hg in range(4):
            S.add("sp", lambda e, s, hg=hg: e.dma_start(out=bias_sb[:], in_=bias_d[:, hg * 1024:(hg + 1) * 1024]).then_inc(s, 16),
                  writes=[bf("bias_sb"), bf("hT")], dkey="c4")
            S.add("act", lambda e, hg=hg: e.activation(out=EB[:, hg * 1024:(hg + 1) * 1024], in_=bias_sb[:], func=AF.Exp),
                  reads=[bf("bias_sb"), bf("hT")], writes=[bf("EB")])
        for kc in range(8):
            bk = nextbank()
            S.add("pe", lambda e, kc=kc, bk=bk: e.transpose(out=ps[bk][:, 0:NCOLP], in_=rowp[:, kc * 128:(kc + 1) * 128],
                                                        identity=ident_f[0:NCOLP, 0:NCOLP]),
                  reads=[bf("rowp"), bf("identf"), bf("hT")], writes=[pB[bk]])
            S.add("dve", lambda e, kc=kc, bk=bk: e.tensor_copy(out=colp[:, kc, :], in_=ps[bk][:, 0:NCOLP]),
                  reads=[pB[bk]], writes=[bf("colp")])
        S.add("dve", lambda e: e.tensor_scalar(out=colh[:], in0=colp[:], scalar1=0.5, scalar2=None, op0=ALU.mult),
              reads=[bf("colp")], writes=[bf("colh")])
        CP = [bf("colp"), bf("colh")]

        def unit_srcs():
            def colpanel(w, c0):
                return w.rearrange("(kc p) n -> p kc n", p=128)[:, :, c0:c0 + 512]
            u = []
            for c0 in (0, 512, 1024):
                u.append(colpanel(w_in_d, c0))
            for j in range(2):
                u.append(colpanel(w_in_d, V_END + j * 512))
                u.append(colpanel(w_in_d, V_END + 1024 + j * 512))
            for j in range(2):
                u.append(colpanel(w_ao_d, j * 512))
                u.append(colpanel(w_in_d, GLU_END + j * 512))
            for j in range(2):
                u.append(colpanel(w_co_d, j * 512))
                u.append(colpanel(w_in_d, GLU_END + 1024 + j * 512))
            for j in range(2):
                u.append(colpanel(w_out_d, j * 512))
            for f in range(8):
                u.append(colpanel(w_ff1_d, f * 512))
            for j2 in range(2):
                for g in range(4):
                    u.append(w_ff2_d[g * 1024:(g + 1) * 1024, j2 * 512:(j2 + 1) * 512].rearrange("(kc p) n -> p kc n", p=128))
            return u
        USRC = unit_srcs()
        NU = len(USRC)
        assert NU == 33
        total_units = ntiles * NU
        issued = [0]
        taken = [0]
        released = [0]

        def issue_fetch():
            n = issued[0]
            if n >= total_units:
                return
            issued[0] += 1
            tile, u = divmod(n, NU)
            s = n % NSLOT
            wb = bf("wsl%d" % s)
            if tile == 0:
                S.add("pool", lambda e, sem, u=u, s=s: e.dma_start(out=wsl[s][:], in_=USRC[u]).then_inc(sem, 16),
                      writes=[wb], dkey="wq%d" % s)
                if ntiles > 1:
                    S.add("sp", lambda e, sem, u=u, s=s: e.dma_start(out=wscr[u], in_=wsl[s][:].rearrange("p k n -> p (k n)")).then_inc(sem, 16),
                          reads=[wb], writes=[bf("scr%d" % u)], dkey="ws%d" % s)
            else:
                S.add("sp", lambda e, sem, u=u, s=s: e.dma_start(out=wsl[s][:].rearrange("p k n -> p (k n)"), in_=wscr[u]).then_inc(sem, 16),
                      reads=[bf("scr%d" % u)], writes=[wb], dkey="wl%d" % s)

        def topup():
            while issued[0] < min(total_units, released[0] + NSLOT):
                issue_fetch()

        def next_unit():
            topup()
            assert taken[0] < issued[0]
            s = taken[0] % NSLOT
            taken[0] += 1
            return wsl[s], bf("wsl%d" % s)

        def release(n):
            released[0] += n
            topup()

        def mm_group(bank, col0, ncol, lhs_fn, rhs_fn, nk, reads, extra_w=()):
            def fn(e):
                i = None
                for k in range(nk):
                    i = e.matmul(ps[bank][:, col0:col0 + ncol], lhsT=lhs_fn(k), rhs=rhs_fn(k), start=(k == 0), stop=(k == nk - 1))
                return i
            return S.add("pe", fn, reads=reads, writes=[pB[bank]] + list(extra_w))

        UT = [bf("uT%d" % j) for j in range(NB)]

        def rms_sq(j, x_t, xbufs):
            S.add("act", lambda e, j=j: e.activation(out=junk, in_=x_t[:, j, :], func=AF.Square, accum_out=ss1[:, j:j + 1]),
                  reads=[xbufs[j]], writes=[bf("sq"), bf("ss1_%d" % j)])

        def rms_head(c0=0, c1=NB):
            rb = [bf("r1_%d" % j) for j in range(c0, c1)]
            S.add("act", lambda e: e.activation(out=r1[:, c0:c1], in_=ss1[:, c0:c1], func=AF.Sqrt, bias=EPS, scale=1.0 / D),
                  reads=[bf("ss1_%d" % j) for j in range(c0, c1)], writes=rb)
            S.add("dve", lambda e: e.reciprocal(out=r1[:, c0:c1], in_=r1[:, c0:c1]), reads=rb, writes=rb)

        def rms_finish(gi, x_t, xbufs, blocks=range(NB), head=True):
            if head:
                rms_head()
            for j in blocks:
                a = alt("xn")
                S.add("dve", lambda e, j=j, a=a: e.scalar_tensor_tensor(out=xnb[a][:], in0=x_t[:, j, :], scalar=r1[:, j:j + 1], in1=gbc[gi][:], op0=ALU.mult, op1=ALU.mult),
                      reads=[xbufs[j], bf("r1_%d" % j), bf("gbc")], writes=[bf("xn%d" % a)])
                bk = nextbank()

                def tr(e, bk=bk, a=a):
                    i = None
                    for kc in range(8):
                        i = e.transpose(out=psb[bk][:, kc * 128:(kc + 1) * 128], in_=xnb[a][:, kc * 128:(kc + 1) * 128], identity=ident_b[:])
                    return i
                S.add("pe", tr, reads=[bf("xn%d" % a), bf("identb")], writes=[pB[bk]])
                S.add("act", lambda e, j=j, bk=bk: e.activation(out=uT[:, :, j * 128:(j + 1) * 128], in_=psb[bk][:].rearrange("p (c t) -> p c t", c=8), func=AF.Copy),
                      reads=[pB[bk]], writes=[UT[j]])

        out_tokens = []
        for tile in range(ntiles):
            t0 = tile * T
            x_t = x_tb[tile % 2]
            xb = [bf("x%d_%d" % (tile % 2, j)) for j in range(NB)]

            def load_x(tl):
                xt2 = x_tb[tl % 2]
                xb2 = [bf("x%d_%d" % (tl % 2, j)) for j in range(NB)]
                tt0 = tl * T
                S.add("sp", lambda e, s: e.dma_start(out=xt2[:], in_=x_d[tt0:tt0 + T, :].rearrange("(j p) d -> p j d", p=128)).then_inc(s, 16),
                      writes=xb2, dkey="xl%d" % (tl % 2))

            def rms1_of(tl, part="all", blocks=range(NB)):
                xt2 = x_tb[tl % 2]
                xb2 = [bf("x%d_%d" % (tl % 2, j)) for j in range(NB)]
                if part in ("all", "head"):
                    for j in range(NB):
                        rms_sq(j, xt2, xb2)
                if part == "all":
                    rms_finish(0, xt2, xb2)
                elif part == "head":
                    rms_head()
                else:
                    rms_finish(0, xt2, xb2, blocks=blocks, head=False)
            if tile == 0:
                load_x(0)
                rms1_of(0)

            def build_dg(c):
                a = c % 2
                S.add("pool", lambda e, a=a, c=c: e.tensor_tensor(out=dg[a][:], in0=ident_b[:].unsqueeze(1).to_broadcast([128, CW, 128]),
                                                                   in1=colh[:, c, 5:5 + CW].unsqueeze(2).to_broadcast([128, CW, 128]), op=ALU.mult),
                      reads=[bf("identb")] + CP, writes=[bf("dg%d" % a)])

            build_dg(0)
            build_dg(1)
            rot[0] = 4
            wq0, wq0b = next_unit()
            wq1, wq1b = next_unit()
            wkv, wkvb = next_unit()
            qkv_banks = {}

            def qkv_mm(j):
                banks = []
                for (w, wb) in ((wq0, wq0b), (wq1, wq1b), (wkv, wkvb)):
                    bk = nextbank()
                    banks.append(bk)
                    mm_group(bk, 0, 512, lambda k, j=j: uT[:, k, j * 128:(j + 1) * 128], lambda k, w=w: w[:, k, :], 8, [UT[j], wb])
                qkv_banks[j] = banks
                reserved.update(banks)

            def qkv_post(j, part="ab"):
                bq0, bq1, bkv = qkv_banks[j]
                if "a" in part:
                    qkv_post_a(j, bq0, bq1, bkv)
                if "b" in part:
                    qkv_post_b(j, bq0, bq1, bkv)

            def qkv_post_a(j, bq0, bq1, bkv):
                for bi, bk in enumerate((bq0, bq1)):
                    sqx, sqB = ((sq, bf("sq")), (sq2, bf("sq2")))[bi]
                    S.add("act", lambda e, bk=bk, sqx=sqx: e.activation(out=sqx[:], in_=ps[bk][:], func=AF.Square), reads=[pB[bk]], writes=[sqB])
                    S.add("dve", lambda e, bi=bi, sqx=sqx: e.tensor_reduce(out=ssq[:, bi * 8:(bi + 1) * 8], in_=sqx[:].rearrange("p (h d) -> p h d", d=HD), axis=AX.X, op=ALU.add),
                          reads=[sqB], writes=[bf("ssq")])
                S.add("act", lambda e, bk=bkv: e.activation(out=sq[:, 0:256], in_=ps[bk][:, 0:256], func=AF.Square), reads=[pB[bkv]], writes=[bf("sq")])
                S.add("dve", lambda e: e.tensor_reduce(out=ssq[:, 16:20], in_=sq[:, 0:256].rearrange("p (h d) -> p h d", d=HD), axis=AX.X, op=ALU.add),
                      reads=[bf("sq")], writes=[bf("ssq")])
                S.add("act", lambda e: e.activation(out=rqk[:], in_=ssq[:], func=AF.Sqrt, bias=EPS, scale=1.0 / HD), reads=[bf("ssq")], writes=[bf("rqk")])
                S.add("dve", lambda e: e.reciprocal(out=rqk[:], in_=rqk[:]), reads=[bf("rqk")], writes=[bf("rqk")])
                for hi, bk in enumerate((bq0, bq1)):
                    S.add("dve", lambda e, hi=hi, bk=bk: e.tensor_tensor(
                        out=qn[:, hi * 512:(hi + 1) * 512].rearrange("p (lo mid d) -> p mid lo d", lo=4, mid=2, d=HD),
                        in0=ps[bk][:].rearrange("p (mid lo d) -> p mid lo d", mid=2, lo=4, d=HD),
                        in1=rqk[:, hi * 8:(hi + 1) * 8].rearrange("p (mid lo) -> p mid lo", mid=2).unsqueeze(3).to_broadcast([128, 2, 4, HD]),
                        op=ALU.mult), reads=[pB[bk], bf("rqk")], writes=[bf("qn")])
                S.add("dve", lambda e, bk=bkv: e.tensor_tensor(
                    out=kn[:].rearrange("p (h d) -> p h d", d=HD), in0=ps[bk][:, 0:256].rearrange("p (h d) -> p h d", d=HD),
                    in1=rqk[:, 16:20].unsqueeze(2).to_broadcast([128, 4, HD]), op=ALU.mult), reads=[pB[bkv], bf("rqk")], writes=[bf("kn")])
                S.add("act", lambda e, j=j, bk=bkv: e.activation(out=Vaug[:, j + 1, :].rearrange("p (h d) -> p h d", d=65)[:, :, 0:64],
                                                                 in_=ps[bk][:, 256:512].rearrange("p (h d) -> p h d", d=HD), func=AF.Copy),
                      reads=[pB[bkv]], writes=[bf("V%d" % (j + 1))])

            def qkv_post_b(j, bq0, bq1, bkv):
                bt = nextbank()

                def trq(e, bt=bt):
                    i = None
                    for c in range(8):
                        i = e.transpose(out=psb[bt][:, c * 128:(c + 1) * 128], in_=qn[:, c * 128:(c + 1) * 128], identity=ident_b[:])
                    return i
                S.add("pe", trq, reads=[bf("qn"), bf("identb")], writes=[pB[bt]])
                for mid in range(2):
                    eng = "act" if mid == 0 else "dve"
                    dst = QTz[mid * 64:(mid + 1) * 64, :, j * 128:(j + 1) * 128].rearrange("p (hi m lo) t -> p hi m lo t", hi=2, m=2, lo=4)[:, :, mid, :, :]
                    src = psb[bt][mid * 64:(mid + 1) * 64, :].rearrange("p (hi lo t) -> p hi lo t", hi=2, lo=4)
                    if eng == "act":
                        S.add("act", lambda e, dst=dst, src=src: e.activation(out=dst, in_=src, func=AF.Copy), reads=[pB[bt]], writes=[bf("QTz")])
                    else:
                        S.add("dve", lambda e, dst=dst, src=src: e.tensor_copy(out=dst, in_=src), reads=[pB[bt]], writes=[bf("QTz")])
                bt2 = nextbank()

                def trk(e, bt2=bt2):
                    i = None
                    for c in range(2):
                        i = e.transpose(out=psb[bt2][:, c * 128:(c + 1) * 128], in_=kn[:, c * 128:(c + 1) * 128], identity=ident_b[:])
                    return i
                S.add("pe", trk, reads=[bf("kn"), bf("identb")], writes=[pB[bt2]])
                S.add("act", lambda e, j=j, bt2=bt2: e.activation(out=KT[:, :, (j + 1) * 128:(j + 2) * 128], in_=psb[bt2][:, 0:256].rearrange("p (g t) -> p g t", g=2),
                                                                   func=AF.Identity, scale=gqkp[:, 0:1]),
                      reads=[pB[bt2], bf("gqkp")], writes=[bf("KT%d" % (j + 1))])
                reserved.difference_update(qkv_banks[j])

            def glu_gen():
                S.add("pool", lambda e: e.tensor_copy(out=hbuf[:, :, 0:32], in_=halo_h[:]), reads=[bf("halo_h")], writes=[bf("hbuf%d" % c_) for c_ in range(8)] + [bf("hT")] + [bf("hT_%d" % f) for f in range(8)])
                for half in range(2):
                    wa, wab = next_unit()
                    wg, wgb = next_unit()
                    for o4 in range(4):
                        c = half * 4 + o4
                        ba = nextbank()
                        mm_group(ba, 0, T, lambda k, wa=wa, o4=o4: wa[:, k, o4 * 128:(o4 + 1) * 128], lambda k: uT[:, k, :], 8, UT + [wab])
                        bg = nextbank()
                        mm_group(bg, 0, T, lambda k, wg=wg, o4=o4: wg[:, k, o4 * 128:(o4 + 1) * 128], lambda k: uT[:, k, :], 8, UT + [wgb])
                        a = alt("tn")
                        S.add("act", lambda e, bg=bg, a=a: e.activation(out=tnb[a][:], in_=ps[bg][:], func=AF.Tanh, scale=0.5), reads=[pB[bg]], writes=[bf("tn%d" % a)])
                        S.add("dve", lambda e, ba=ba, a=a, c=c: e.scalar_tensor_tensor(out=hbuf[:, c, 32:32 + T], in0=tnb[a][:], scalar=1.0, in1=ps[ba][:], op0=ALU.add, op1=ALU.mult),
                              reads=[bf("tn%d" % a), pB[ba]], writes=[bf("hbuf%d" % c)])
                        yield
                    release(2)
                S.add("pool", lambda e: e.tensor_copy(out=halo_h[:], in_=hbuf[:, :, T:T + 32]), reads=[bf("hbuf%d" % c_) for c_ in range(8)], writes=[bf("halo_h")])


            def drain(g, n=100):
                for _ in range(n):
                    try:
                        next(g)
                    except StopIteration:
                        return
            qkv_mm(0)
            qkv_mm(1)
            qkv_post(0)
            qkv_mm(2)
            qkv_post(1)
            qkv_mm(3)
            release(3)
            qkv_post(2)
            qkv_post(3, "a")
            gg = glu_gen()
            drain(gg, 3)
            qkv_post(3, "b")
            drain(gg)
            def gated_proj_gen(actT, actbuf, first, held=None, nhold=0):
                nh = [nhold]
                for half in range(2):
                    wp, wpb = next_unit()
                    wg, wgb = next_unit()
                    for o4 in range(4):
                        oc = half * 4 + o4
                        bp = nextbank()
                        mm_group(bp, 0, T, lambda k, wp=wp, o4=o4: wp[:, k, o4 * 128:(o4 + 1) * 128], lambda k: actT[:, k, :], 8, [actbuf, wpb])
                        bg = nextbank()
                        mm_group(bg, 0, T, lambda k, wg=wg, o4=o4: wg[:, k, o4 * 128:(o4 + 1) * 128], lambda k: uT[:, k, :], 8, UT + [wgb])
                        def evac(bp=bp, bg=bg, oc=oc):
                            a = alt("tn")
                            S.add("act", lambda e, bg=bg, a=a: e.activation(out=tnb[a][:], in_=ps[bg][:], func=AF.Tanh, scale=0.5), reads=[pB[bg]], writes=[bf("tn%d" % a)])
                            if first:
                                S.add("dve", lambda e, bp=bp, a=a, oc=oc: e.scalar_tensor_tensor(out=mT[:, oc, :], in0=tnb[a][:], scalar=1.0, in1=ps[bp][:], op0=ALU.add, op1=ALU.mult),
                                      reads=[bf("tn%d" % a), pB[bp]], writes=[bf("mT"), bf("QTz")])
                            else:
                                S.add("dve", lambda e, bp=bp, a=a: e.scalar_tensor_tensor(out=tnb[a][:], in0=tnb[a][:], scalar=1.0, in1=ps[bp][:], op0=ALU.add, op1=ALU.mult),
                                      reads=[bf("tn%d" % a), pB[bp]], writes=[bf("tn%d" % a)])
                                S.add("dve", lambda e, a=a, oc=oc: e.tensor_tensor(out=mT[:, oc, :], in0=tnb[a][:], in1=mT[:, oc, :], op=ALU.add),
                                      reads=[bf("tn%d" % a), bf("mT")], writes=[bf("mT")])
                        if held is not None and nh[0] > 0:
                            nh[0] -= 1
                            held.append(evac)
                        else:
                            evac()
                        yield
                    release(2)

            def drain(g, n=100):
                for _ in range(n):
                    try:
                        next(g)
                    except StopIteration:
                        return

            CB = 4

            def conv_steps():
                for c in range(8):
                    a = c % 2
                    for j0 in range(0, CW, 4):
                        j1 = min(CW, j0 + 4)

                        def cm(e, c=c, a=a, j0=j0, j1=j1):
                            i = None
                            for jt in range(j0, j1):
                                i = e.matmul(ps[CB][:], lhsT=dg[a][:, jt, :], rhs=hbuf[:, c, 2 + jt:2 + jt + T], start=(jt == 0), stop=(jt == CW - 1))
                            return i
                        S.add("pe", cm, reads=[bf("dg%d" % a), bf("hbuf%d" % c)], writes=[pB[CB]])
                        if j1 == CW:
                            S.add("act", lambda e, c=c: e.activation(out=cv[:, c, :], in_=ps[CB][:], func=AF.Identity, bias=colp[:, c, 2:3]),
                                  reads=[pB[CB]] + CP, writes=[bf("cv%d" % c)])
                            if c + 2 < 8:
                                build_dg(c + 2)
                        yield
            cgen = conv_steps()
            pool_now[0] = (0, 1, 2, 3)

            def conv_step(n=1):
                for _ in range(n):
                    try:
                        next(cgen)
                    except StopIteration:
                        return
            OB = (5, 6, 7)

            def ohead(h):
                return OB[h // 7], (h % 7) * 65
            deferred = []
            for j in range(NB):
                pend = []
                for pr in range(8):
                    if pr == 3 and deferred:
                        deferred.pop(0)()
                    h0 = 2 * pr
                    bk = nextbank()

                    def smm(e, bk=bk, h0=h0, j=j):
                        i = None
                        for hl in range(2):
                            h = h0 + hl
                            gp = (h // 4) // 2
                            for kb in range(2):
                                c = (hl * 2 + kb) * 128
                                i = e.matmul(ps[bk][:, c:c + 128], lhsT=KT[:, gp, (j + kb) * 128:(j + kb + 1) * 128],
                                             rhs=QTz[:, h, j * 128:(j + 1) * 128], start=True, stop=True)
                        return i
                    S.add("pe", smm, reads=[bf("QTz"), bf("KT%d" % j), bf("KT%d" % (j + 1))], writes=[pB[bk]])
                    a = alt("E", 3)
                    S.add("act", lambda e, bk=bk, a=a: e.activation(out=Eb[a][:], in_=ps[bk][:], func=AF.Exp), reads=[pB[bk]], writes=[bf("E%d" % a)])
                    S.add("dve", lambda e, a=a, h0=h0: e.tensor_tensor(out=PTb[a][:], in0=Eb[a][:], in1=EB[:, h0 * 256:(h0 + 2) * 256], op=ALU.mult),
                          reads=[bf("E%d" % a), bf("EB")], writes=[bf("PT%d" % a)])
                    pend.append((h0, a))
                    conv_step(2)
                    if len(pend) == 3 or pr == 7:
                        todo = pend[:1] if pr < 7 else pend
                        pend = pend[1:] if pr < 7 else []
                        for (hh0, aa) in todo:
                            def pv(e, hh0=hh0, aa=aa, j=j):
                                i = None
                                for hl in range(2):
                                    h = hh0 + hl
                                    g = h // 4
                                    ob, off = ohead(h)
                                    for kb in range(2):
                                        c = (hl * 2 + kb) * 128
                                        i = e.matmul(ps[ob][:, off:off + 65], lhsT=PTb[aa][:, c:c + 128], rhs=Vaug[:, j + kb, g * 65:(g + 1) * 65],
                                                     start=(kb == 0), stop=(kb == 1))
                                return i
                            obs = sorted(set(ohead(hh0 + hl)[0] for hl in range(2)))
                            S.add("pe", pv, reads=[bf("PT%d" % aa), bf("V%d" % j), bf("V%d" % (j + 1))], writes=[pB[o] for o in obs])
                for bi, ob in enumerate(OB):
                    hs = bi * 7
                    nh = 7 if bi < 2 else 2
                    ov = ps[ob][:, 0:nh * 65].rearrange("p (h d) -> p h d", d=65)
                    S.add("dve", lambda e, ov=ov, hs=hs, nh=nh: e.tensor_tensor(out=den[:, hs:hs + nh].unsqueeze(2), in0=ov[:, :, 64:65], in1=es[:, hs:hs + nh].unsqueeze(2), op=ALU.add),
                          reads=[pB[ob], bf("es")], writes=[bf("den")])
                S.add("dve", lambda e: e.reciprocal(out=rden[:], in_=den[:]), reads=[bf("den")], writes=[bf("rden")])
                for bi, ob in enumerate(OB):
                    hs = bi * 7
                    nh = 7 if bi < 2 else 2
                    ov = ps[ob][:, 0:nh * 65].rearrange("p (h d) -> p h d", d=65)
                    S.add("dve", lambda e, ov=ov, hs=hs, nh=nh: e.tensor_tensor(out=attn_n[:, hs * 64:(hs + nh) * 64].rearrange("p (h d) -> p h d", d=HD), in0=ov[:, :, 0:64],
                                                                               in1=rden[:, hs:hs + nh].unsqueeze(2).to_broadcast([128, nh, HD]), op=ALU.mult),
                          reads=[pB[ob], bf("rden")], writes=[bf("attn_n")])
                def finish_blk(j=j):
                    bt = nextbank()

                    def tra(e, bt=bt):
                        i = None
                        for c in range(8):
                            i = e.transpose(out=psb[bt][:, c * 128:(c + 1) * 128], in_=attn_n[:, c * 128:(c + 1) * 128], identity=ident_b[:])
                        return i
                    S.add("pe", tra, reads=[bf("attn_n"), bf("identb")], writes=[pB[bt]])
                    S.add("act", lambda e, j=j, bt=bt: e.activation(out=attnT[:, :, j * 128:(j + 1) * 128], in_=psb[bt][:].rearrange("p (c t) -> p c t", c=8), func=AF.Copy),
                          reads=[pB[bt]], writes=[bf("attnT")])
                deferred.append(finish_blk)
            while deferred:
                deferred.pop(0)()
            conv_step(1000)
            S.add("pool", lambda e: e.tensor_copy(out=KT[:, :, 0:128], in_=KT[:, :, NB * 128:(NB + 1) * 128]), reads=[bf("KT%d" % NB)], writes=[bf("KT0")])
            S.add("pool", lambda e: e.tensor_copy(out=Vaug[:, 0, :], in_=Vaug[:, NB, :]), reads=[bf("V%d" % NB)], writes=[bf("V0")])

            pool_now[0] = (0, 1, 2, 3, 6, 7)
            BM, BQ = 4, 5
            for c in range(8):
                a = alt("cvb")
                S.add("dve", lambda e, c=c, a=a: e.tensor_copy(out=cvb[a][:], in_=cv[:, c, :]), reads=[bf("cv%d" % c)], writes=[bf("cvb%d" % a)])
                S.add("act", lambda e, c=c, a=a: e.activation(out=sqb[a][:], in_=cv[:, c, :], func=AF.Square), reads=[bf("cv%d" % c)], writes=[bf("sqb%d" % a)])
                S.add("pe", lambda e, c=c, a=a: e.matmul(ps[BM][:], lhsT=onesm[:], rhs=cvb[a][:], start=(c == 0), stop=(c == 7)),
                      reads=[bf("cvb%d" % a), bf("onesm")], writes=[pB[BM]])
                S.add("pe", lambda e, c=c, a=a: e.matmul(ps[BQ][:], lhsT=onesm[:], rhs=sqb[a][:], start=(c == 0), stop=(c == 7)),
                      reads=[bf("sqb%d" % a), bf("onesm")], writes=[pB[BQ]])
            held = []
            ga = gated_proj_gen(attnT, bf("attnT"), True, held, 3)
            drain(ga, 3)
            S.add("act", lambda e: e.activation(out=mean_sb[:], in_=ps[BM][:], func=AF.Copy), reads=[pB[BM]], writes=[bf("mean")])
            S.add("dve", lambda e: e.tensor_tensor(out=nmr_sb[:], in0=mean_sb[:], in1=mean_sb[:], op=ALU.mult), reads=[bf("mean")], writes=[bf("nmr")])
            S.add("dve", lambda e: e.tensor_tensor(out=rstd_sb[:], in0=ps[BQ][:], in1=nmr_sb[:], op=ALU.subtract), reads=[pB[BQ], bf("nmr")], writes=[bf("rstd")])
            S.add("dve", lambda e: e.tensor_scalar(out=rstd_sb[:], in0=rstd_sb[:], scalar1=0.0, scalar2=None, op0=ALU.max), reads=[bf("rstd")], writes=[bf("rstd")])
            S.add("act", lambda e: e.activation(out=rstd_sb[:], in_=rstd_sb[:], func=AF.Ln, bias=EPS, scale=1.0), reads=[bf("rstd")], writes=[bf("rstd")])
            S.add("act", lambda e: e.activation(out=rstd_sb[:], in_=rstd_sb[:], func=AF.Exp, scale=-0.5), reads=[bf("rstd")], writes=[bf("rstd")])
            S.add("dve", lambda e: e.scalar_tensor_tensor(out=nmr_sb[:], in0=mean_sb[:], scalar=-1.0, in1=rstd_sb[:], op0=ALU.mult, op1=ALU.mult),
                  reads=[bf("mean"), bf("rstd")], writes=[bf("nmr")])
            for ev in held:
                ev()
            ybs = [(ybuf[:], bf("ybuf")), (xnb[0][:].bitcast(F32), bf("xn0"))]
            zbs = [(zbuf[:], bf("zbuf")), (xnb[1][:].bitcast(F32), bf("xn1"))]
            def ln_pre(c):
                yb, ybB = ybs[c % 2]
                S.add("pool", lambda e, c=c, yb=yb: e.tensor_tensor(out=yb, in0=cv[:, c, :], in1=rstd_sb[:], op=ALU.mult), reads=[bf("cv%d" % c), bf("rstd")], writes=[ybB])
                S.add("dve", lambda e, yb=yb: e.tensor_tensor(out=yb, in0=yb, in1=nmr_sb[:], op=ALU.add), reads=[ybB, bf("nmr")], writes=[ybB])
            ln_pre(0)
            for c in range(8):
                yb, ybB = ybs[c % 2]
                zb, zbB = zbs[c % 2]
                if c + 1 < 8:
                    ln_pre(c + 1)
                S.add("act", lambda e, c=c, yb=yb, zb=zb: e.activation(out=zb, in_=yb, func=AF.Identity, scale=colh[:, c, 3:4], bias=colh[:, c, 4:5]),
                      reads=[ybB] + CP, writes=[zbB])
                a = alt("tn")
                S.add("act", lambda e, a=a, c=c, yb=yb: e.activation(out=tnb[a][:], in_=yb, func=AF.Tanh, scale=colh[:, c, 3:4], bias=colh[:, c, 4:5]),
                      reads=[ybB] + CP, writes=[bf("tn%d" % a)])
                S.add("dve", lambda e, a=a, c=c, zb=zb: e.scalar_tensor_tensor(out=sT[:, c, :], in0=tnb[a][:], scalar=1.0, in1=zb, op0=ALU.add, op1=ALU.mult),
                      reads=[bf("tn%d" % a), zbB], writes=[bf("sT"), bf("QTz")])
                drain(ga, 1)
            drain(ga)


            drain(gated_proj_gen(sT, bf("sT"), False))
            pool_now[0] = tuple(range(8))

            wo = [next_unit(), next_unit()]
            for j in range(NB):
                for half in range(2):
                    w, wb = wo[half]
                    bk = nextbank()
                    mm_group(bk, 0, 512, lambda k, j=j: mT[:, k, j * 128:(j + 1) * 128], lambda k, w=w: w[:, k, :], 8, [bf("mT"), wb])
                    S.add("dve", lambda e, j=j, half=half, bk=bk, x_t=x_t: e.scalar_tensor_tensor(out=x_t[:, j, half * 512:(half + 1) * 512], in0=ps[bk][:], scalar=0.5,
                                                                                      in1=x_t[:, j, half * 512:(half + 1) * 512], op0=ALU.mult, op1=ALU.add),
                          reads=[pB[bk], xb[j]], writes=[xb[j]])
                rms_sq(j, x_t, xb)
                if j == 2:
                    rms_head(0, 3)
            release(2)
            if tile + 1 < ntiles:
                load_x(tile + 1)
            if tile + 1 < ntiles:
                S.add("pool", lambda e: e.memset(regA[:], 0.0), reads=[], writes=[bf("QTz"), bf("sT"), bf("mT")])

            rms_finish(1, x_t, xb, blocks=[0, 1, 2], head=False)
            rms_head(3, 4)
            rms_finish(1, x_t, xb, blocks=[3], head=False)

            for f in range(8):
                if f == 4 and tile + 1 < ntiles:
                    rms1_of(tile + 1, "head")
                w, wb = next_unit()
                for o4 in range(4):
                    fc = f * 4 + o4
                    bk = nextbank()
                    mm_group(bk, 0, T, lambda k, w=w, o4=o4: w[:, k, o4 * 128:(o4 + 1) * 128], lambda k: uT[:, k, :], 8, UT + [wb])
                    a = alt("rl")
                    S.add("act", lambda e, bk=bk, a=a: e.activation(out=rl[a][:], in_=ps[bk][:], func=AF.Relu), reads=[pB[bk]], writes=[bf("rl%d" % a)])
                    S.add("pool", lambda e, a=a, fc=fc: e.tensor_tensor(out=hT[:, fc, :], in0=rl[a][:], in1=rl[a][:], op=ALU.mult),
                          reads=[bf("rl%d" % a)], writes=[bf("hT"), bf("hT_%d" % f)] + [bf("hbuf%d" % c_) for c_ in range(8)] + [bf("cv%d" % c) for c in range(8)])
                release(1)

            for j2 in range(2):
                for g in range(4):
                    if j2 == 0 and tile + 1 < ntiles:
                        pool_now[0] = (0, 1, 2, 3)
                        rms1_of(tile + 1, "body", blocks=[g])
                        pool_now[0] = tuple(range(8))
                    w, wb = next_unit()
                    for j in range(NB):
                        bk = (4 if j2 == 0 else 0) + j

                        def f2(e, w=w, g=g, j=j, bk=bk):
                            i = None
                            for k in range(8):
                                i = e.matmul(ps[bk][:], lhsT=hT[:, g * 8 + k, j * 128:(j + 1) * 128], rhs=w[:, k, :], start=(g == 0 and k == 0), stop=(g == 3 and k == 7))
                            return i
                        S.add("pe", f2, reads=[bf("hT_%d" % (2 * g)), bf("hT_%d" % (2 * g + 1)), wb], writes=[pB[bk]])
                    release(1)
                for j in range(NB):
                    bk = (4 if j2 == 0 else 0) + j
                    S.add("dve", lambda e, j=j, j2=j2, bk=bk, x_t=x_t: e.tensor_tensor(out=x_t[:, j, j2 * 512:(j2 + 1) * 512], in0=ps[bk][:], in1=x_t[:, j, j2 * 512:(j2 + 1) * 512], op=ALU.add),
                          reads=[pB[bk], xb[j]], writes=[xb[j]])
            tk = S.add("sp", lambda e, s, t0=t0, x_t=x_t: e.dma_start(out=out_d[t0:t0 + T, :].rearrange("(j p) d -> p j d", p=128), in_=x_t[:]).then_inc(s, 16),
                       reads=xb, dkey="st%d" % (tile % 2))
            out_tokens.append(tk)
        S.add("sp", None, extra=out_tokens[-2:])
        S.emit(nc, st)
    return nc


def t5_bucket(dist):
    n = np.maximum(dist, 0)
    max_exact = 16
    nf = np.maximum(n, 1).astype(np.float32)
    large = max_exact + (np.log(nf / np.float32(max_exact)) / np.float32(np.log(128 / max_exact)) * np.float32(32 - max_exact)).astype(np.int32)
    large = np.minimum(large, 31)
    return np.where(n < max_exact, n, large)


def host_layout(inputs, ntiles=SEQ // T, ncores=8):
    f = lambda a: np.ascontiguousarray(np.asarray(a, dtype=np.float32))
    x = f(inputs["x"])
    rowp = np.concatenate([f(inputs["norm_mix_g"]), f(inputs["norm_mlp_g"]), f(inputs["b_dw"]), f(inputs["conv_ln_g"]),
                           f(inputs["conv_ln_b"]), f(inputs["w_dw"])[0]], axis=0)
    gq = f(inputs["q_norm_g"])[0]
    gk = f(inputs["k_norm_g"])[0]
    gqk = np.stack([np.tile(gq, 2), np.tile(gk, 2)], axis=1)
    gqk = np.ascontiguousarray(gqk)
    sinks = np.ascontiguousarray(np.broadcast_to(f(inputs["attn_sinks"])[0][None, :], (128, NQ)))
    rb = f(inputs["rel_bias"])
    k = np.arange(128)[:, None]
    q = np.arange(128)[None, :]
    biasT = np.empty((128, NQ, 2, 128), np.float32)
    for kb in range(2):
        dist = q + 128 - (kb * 128 + k)
        valid = (dist >= 0) & (dist < 128)
        g = rb[t5_bucket(dist)]
        g = np.where(valid[:, :, None], g, np.float32(-1e30))
        biasT[:, :, kb, :] = np.transpose(g, (0, 2, 1))
    biasT = np.ascontiguousarray(biasT.reshape(128, NQ * 256))
    common = {
        "w_in": f(inputs["w_in"])[0], "w_attn_o": f(inputs["w_attn_o"])[0], "w_conv_out": f(inputs["w_conv_out"])[0],
        "w_out": f(inputs["w_out"])[0], "w_ff1": f(inputs["w_ff1"])[0], "w_ff2": f(inputs["w_ff2"])[0],
        "rowp": np.ascontiguousarray(rowp), "gqk": gqk, "sinks": sinks, "biasT": biasT,
        "ident": np.eye(128, dtype=np.float32),
    }
    maps = []
    for c in range(ncores):
        m = dict(common)
        m["x"] = np.ascontiguousarray(x[c, :ntiles * T])
        maps.append(m)
    return maps


def kernel(**inputs):
    ntiles = SEQ // T
    nc = build(ntiles)
    maps = host_layout(inputs, ntiles, 8)
    res = run_bass_kernel_spmd(nc, maps, core_ids=list(range(8)))
    return np.stack([r["out"] for r in res.results], axis=0).astype(np.float32)
```
